# Optimizing a Trainium2 kernel written in Bass

```python
import math
import jax, jax.numpy as jnp
from jax import lax
import numpy as np

D_MODEL = 1024
BATCH = 2
SEQ = 8192
DEPTH = 4
DEC_BATCH = 4
DEC_SEQ = 4096
PAST_LEN = 128

N_MIXERS = 2
N_ATTN_LAYERS = (DEPTH + N_MIXERS - 1) // N_MIXERS
N_SSM_LAYERS = DEPTH // N_MIXERS
ATTN_HEADS = 16
HEAD_DIM = 64
ATTN_WIDTH = ATTN_HEADS * HEAD_DIM
DILATED_PAIRS = ((128, 1), (512, 4), (2048, 16))
N_DIL = len(DILATED_PAIRS)
ATTN_IN = (3 * N_DIL + 1) * ATTN_WIDTH
ROPE_THETA = 10000.0
SSM_WIDTH = D_MODEL
GROUP_CH = 16
SSM_GROUPS = SSM_WIDTH // GROUP_CH
STATE = 64
DT_MIN = 0.001
DT_MAX = 0.1
LAMBDA_RE_MAX = -1e-4
NORM_EPS = 1e-6
NEG_INF = -1e30

kernel_name = "dilated_attn_s5_interleaved_adaln_encoder"


def rms_norm(x, g):
    x32 = x.astype(jnp.float32)
    y = x32 * lax.rsqrt(jnp.mean(x32 * x32, axis=-1, keepdims=True) + NORM_EPS)
    return (y * g.astype(jnp.float32)).astype(x.dtype)


def rope_tables(s):
    inv_freq = ROPE_THETA ** (-jnp.arange(0, HEAD_DIM, 2, dtype=jnp.float32) / HEAD_DIM)
    ang = jnp.arange(s, dtype=jnp.float32)[:, None] * inv_freq[None, :]
    return jnp.cos(ang)[:, None, :], jnp.sin(ang)[:, None, :]


def apply_rope(t, cos, sin):
    t32 = t.astype(jnp.float32)
    t1, t2 = jnp.split(t32, 2, axis=-1)
    out = jnp.concatenate([t1 * cos - t2 * sin, t2 * cos + t1 * sin], axis=-1)
    return out.astype(t.dtype)


def dilated_window_attention(q, k, v, dil, radius):
    bsz, s, h, e = q.shape
    blk = radius
    m = s // dil
    nb = -(-m // blk)
    mp = nb * blk

    def to_sub(t):
        return t.reshape(bsz, m, dil, h, e).transpose(0, 2, 3, 1, 4)

    qs = jnp.pad(to_sub(q), ((0, 0), (0, 0), (0, 0), (0, mp - m), (0, 0)))
    qs = qs.reshape(bsz, dil, h, nb, blk, e)

    def neighbours(t):
        tp = jnp.pad(to_sub(t), ((0, 0), (0, 0), (0, 0), (blk, mp - m + blk), (0, 0)))
        tp = tp.reshape(bsz, dil, h, nb + 2, blk, e)
        return jnp.concatenate([tp[:, :, :, :-2], tp[:, :, :, 1:-1], tp[:, :, :, 2:]], axis=4)

    kw = neighbours(k)
    vw = neighbours(v)
    qi = jnp.arange(nb)[:, None, None] * blk + jnp.arange(blk)[None, :, None]
    kj = jnp.arange(nb)[:, None, None] * blk - blk + jnp.arange(3 * blk)[None, None, :]
    valid = (jnp.abs(kj - qi) <= radius) & (kj >= 0) & (kj < m)

    scores = jnp.einsum('bdhnqe,bdhnke->bdhnqk', qs, kw).astype(jnp.float32)
    scores = jnp.where(valid, scores, NEG_INF)
    lse = jax.nn.logsumexp(scores, axis=-1)
    probs = jnp.exp(scores - lse[..., None]).astype(v.dtype)
    o = jnp.einsum('bdhnqk,bdhnke->bdhnqe', probs, vw)
    o = o.reshape(bsz, dil, h, mp, e)[:, :, :, :m].transpose(0, 3, 1, 2, 4).reshape(bsz, s, h, e)
    lse = lse.reshape(bsz, dil, h, mp)[:, :, :, :m].transpose(0, 3, 1, 2).reshape(bsz, s, h)
    return o, lse


def dilated_mixer(h, w_in, w_out):
    bsz, s, _ = h.shape
    proj = h @ w_in
    z = proj[..., 3 * N_DIL * ATTN_WIDTH:]
    cos, sin = rope_tables(s)
    scale = HEAD_DIM ** -0.5
    outs, lses = [], []
    for g, (window, dil) in enumerate(DILATED_PAIRS):
        qkv = proj[..., g * 3 * ATTN_WIDTH:(g + 1) * 3 * ATTN_WIDTH].reshape(bsz, s, 3, ATTN_HEADS, HEAD_DIM)
        q = apply_rope(qkv[:, :, 0], cos, sin) * scale
        k = apply_rope(qkv[:, :, 1], cos, sin)
        v = qkv[:, :, 2]
        o, lse = dilated_window_attention(q, k, v, dil, window // (2 * dil))
        outs.append(o)
        lses.append(lse)
    weights = jax.nn.softmax(jnp.stack(lses, axis=0), axis=0)
    o = jnp.einsum('gbsh,gbshe->bshe', weights.astype(h.dtype), jnp.stack(outs, axis=0))
    y = o.reshape(bsz, s, ATTN_WIDTH) * jax.nn.silu(z)
    return y @ w_out


def _linear_recurrence(e1, e2):
    a1, b1 = e1
    a2, b2 = e2
    return a1 * a2, a2 * b1 + b2


def s5_mixer(h, w_in, lam_re, lam_im, log_dt, b_re, b_im, c_re, c_im, d_skip, w_glu, w_out):
    bsz, s, _ = h.shape
    u, z = jnp.split(h @ w_in, 2, axis=-1)
    u32 = u.astype(jnp.float32)
    ug = u32.reshape(bsz, s, SSM_GROUPS, GROUP_CH)
    y = (d_skip.astype(jnp.float32) * u32).reshape(bsz, s, SSM_GROUPS, GROUP_CH)
    ugc = ug.astype(jnp.complex64)
    for direction in range(2):
        lam = lax.complex(jnp.minimum(lam_re[direction].astype(jnp.float32), LAMBDA_RE_MAX),
                          lam_im[direction].astype(jnp.float32))
        dt = jnp.exp(log_dt[direction].astype(jnp.float32))[:, None]
        lam_bar = jnp.exp(lam * dt)
        b_mat = lax.complex(b_re[direction].astype(jnp.float32), b_im[direction].astype(jnp.float32))
        b_bar = ((lam_bar - 1.0) / lam)[..., None] * b_mat
        bu = jnp.einsum('bsgc,gpc->bsgp', ugc, b_bar)
        a = jnp.broadcast_to(lam_bar, bu.shape)
        _, states = lax.associative_scan(_linear_recurrence, (a, bu), axis=1,
                                         reverse=(direction == 1))
        c_mat = lax.complex(c_re[direction].astype(jnp.float32), c_im[direction].astype(jnp.float32))
        y = y + jnp.real(jnp.einsum('bsgp,gcp->bsgc', states, c_mat))
    y = y.reshape(bsz, s, SSM_WIDTH).astype(h.dtype)
    g = jax.nn.gelu(y)
    y = g * jax.nn.sigmoid(g @ w_glu)
    return (y * jax.nn.silu(z)) @ w_out


def trunk(x, c, norm_g, ada_w, ada_b, attn_w_in, attn_w_out, ssm_w_in, ssm_lam_re, ssm_lam_im,
          ssm_log_dt, ssm_b_re, ssm_b_im, ssm_c_re, ssm_c_im, ssm_d, ssm_w_glu, ssm_w_out, final_norm_g):
    for i in range(DEPTH):
        ada = jax.nn.silu(c) @ ada_w[i] + ada_b[i]
        shift, scale, gate = jnp.split(ada[:, None, :], 3, axis=-1)
        hmod = rms_norm(x, norm_g[i]) * (1.0 + scale) + shift
        j = i // N_MIXERS
        if i % N_MIXERS == 0:
            out = dilated_mixer(hmod, attn_w_in[j], attn_w_out[j])
        else:
            out = s5_mixer(hmod, ssm_w_in[j], ssm_lam_re[j], ssm_lam_im[j], ssm_log_dt[j],
                           ssm_b_re[j], ssm_b_im[j], ssm_c_re[j], ssm_c_im[j], ssm_d[j],
                           ssm_w_glu[j], ssm_w_out[j])
        x = x + gate * out
    return rms_norm(x, final_norm_g)


def setup_inputs(seed: int = 0) -> dict:
    key = jax.random.key(seed)
    ks = jax.random.split(key, 24)
    f32 = jnp.float32

    def nrm(k, shape, s):
        return jax.random.normal(k, shape, f32) * s

    nA, nB = N_ATTN_LAYERS, N_SSM_LAYERS
    G, P, GC = SSM_GROUPS, STATE, GROUP_CH
    lam_im_base = jnp.broadcast_to(jnp.pi * jnp.arange(P, dtype=f32), (nB, 2, G, P))
    return {
        "x_prompt": nrm(ks[0], (BATCH, SEQ, D_MODEL), 1.0),
        "x_sample": nrm(ks[1], (DEC_BATCH, DEC_SEQ, D_MODEL), 1.0),
        "c_prompt": nrm(ks[2], (BATCH, D_MODEL), 1.0),
        "c_sample": nrm(ks[3], (DEC_BATCH, D_MODEL), 1.0),
        "norm_g": 1.0 + nrm(ks[4], (DEPTH, D_MODEL), 0.02),
        "ada_w": nrm(ks[5], (DEPTH, D_MODEL, 3 * D_MODEL), 0.5 * D_MODEL ** -0.5),
        "ada_b": nrm(ks[6], (DEPTH, 3 * D_MODEL), 0.02),
        "attn_w_in": nrm(ks[7], (nA, D_MODEL, ATTN_IN), D_MODEL ** -0.5),
        "attn_w_out": nrm(ks[8], (nA, ATTN_WIDTH, D_MODEL), ATTN_WIDTH ** -0.5),
        "ssm_w_in": nrm(ks[9], (nB, D_MODEL, 2 * SSM_WIDTH), D_MODEL ** -0.5),
        "ssm_lam_re": -0.5 + nrm(ks[10], (nB, 2, G, P), 0.01),
        "ssm_lam_im": lam_im_base + nrm(ks[11], (nB, 2, G, P), 0.01),
        "ssm_log_dt": jax.random.uniform(ks[12], (nB, 2, G), f32, math.log(DT_MIN), math.log(DT_MAX)),
        "ssm_b_re": nrm(ks[13], (nB, 2, G, P, GC), (2 * GC) ** -0.5),
        "ssm_b_im": nrm(ks[14], (nB, 2, G, P, GC), (2 * GC) ** -0.5),
        "ssm_c_re": nrm(ks[15], (nB, 2, G, GC, P), P ** -0.5),
        "ssm_c_im": nrm(ks[16], (nB, 2, G, GC, P), P ** -0.5),
        "ssm_d": nrm(ks[17], (nB, SSM_WIDTH), 0.5),
        "ssm_w_glu": nrm(ks[18], (nB, SSM_WIDTH, SSM_WIDTH), SSM_WIDTH ** -0.5),
        "ssm_w_out": nrm(ks[19], (nB, SSM_WIDTH, D_MODEL), SSM_WIDTH ** -0.5),
        "final_norm_g": 1.0 + nrm(ks[20], (D_MODEL,), 0.02),
    }


def reference(x_prompt, x_sample, c_prompt, c_sample, norm_g, ada_w, ada_b, attn_w_in, attn_w_out,
              ssm_w_in, ssm_lam_re, ssm_lam_im, ssm_log_dt, ssm_b_re, ssm_b_im, ssm_c_re, ssm_c_im,
              ssm_d, ssm_w_glu, ssm_w_out, final_norm_g):
    y_prompt = trunk(x_prompt, c_prompt, norm_g, ada_w, ada_b, attn_w_in, attn_w_out, ssm_w_in,
                     ssm_lam_re, ssm_lam_im, ssm_log_dt, ssm_b_re, ssm_b_im, ssm_c_re, ssm_c_im,
                     ssm_d, ssm_w_glu, ssm_w_out, final_norm_g)
    y_sample = trunk(x_sample, c_sample, norm_g, ada_w, ada_b, attn_w_in, attn_w_out, ssm_w_in,
                     ssm_lam_re, ssm_lam_im, ssm_log_dt, ssm_b_re, ssm_b_im, ssm_c_re, ssm_c_im,
                     ssm_d, ssm_w_glu, ssm_w_out, final_norm_g)
    return (y_prompt, y_sample)
```

```python
import math
import numpy as np
from contextlib import ExitStack
import concourse.bass as bass
import concourse.mybir as mybir
from concourse.bass_utils import run_bass_kernel_spmd

F32 = mybir.dt.float32
BF16 = mybir.dt.bfloat16
I32 = mybir.dt.int32
AF = mybir.ActivationFunctionType
ALU = mybir.AluOpType

D = 1024
NSEG = 2
SEG = 4096
N = NSEG * SEG
NT = N // 128
DEPTH = 4
NEGM = -30000.0
TWO_PI = 2.0 * math.pi
C1 = 6.28125
C2 = TWO_PI - C1


class Buf:
    __slots__ = ("w", "r")

    def __init__(self):
        self.w = None
        self.r = {}


class Prog:
    ENG = ("pe", "act", "dve", "pool", "sp")
    NDSEM = 10

    def __init__(self):
        self.nc = bass.Bass("TRN2", target_bir_lowering=False)
        nc = self.nc
        self.sem = {e: nc.alloc_semaphore("sem_" + e) for e in self.ENG}
        self.cnt = {e: 0 for e in self.ENG}
        self.seen = {e: {} for e in self.ENG}
        self.q = {e: [] for e in self.ENG}
        self.dsem, self.dval, self.drr = {}, {}, {}
        for qn in ("sp", "pool", "act"):
            self.dsem[qn] = [nc.alloc_semaphore(f"dsem_{qn}{i}") for i in range(self.NDSEM)]
            self.dval[qn] = [0] * self.NDSEM
            self.drr[qn] = 0
        self.ninst = 0

    def _deps(self, e, reads, writes, extra=()):
        deps = {}

        def need(key, sem, val):
            if key == "pe" and e == "pe":
                return
            cur = deps.get(key)
            if cur is None or cur[1] < val:
                deps[key] = (sem, val)

        for b in reads:
            if b.w is not None:
                need(*b.w)
        for b in writes:
            if b.w is not None:
                need(*b.w)
            for k, (s, v) in b.r.items():
                need(k, s, v)
        for ev in extra:
            need(*ev)
        seen = self.seen[e]
        for key, (sem, val) in deps.items():
            if seen.get(key, 0) < val:
                self.q[e].append(("wait", sem, val))
                seen[key] = val

    def _mark(self, ev, reads, writes):
        key, sem, val = ev
        for b in reads:
            b.r[key] = (sem, val)
        for b in writes:
            b.w = ev
            b.r = {}

    def emit(self, e, fn, reads=(), writes=()):
        self._deps(e, reads, writes)
        self.cnt[e] += 1
        self.q[e].append(("inst", fn, self.sem[e], 1))
        self._mark((e, self.sem[e], self.cnt[e]), reads, writes)
        self.ninst += 1

    def dma(self, qn, out, in_, reads=(), writes=(), slow=False):
        k = self.drr[qn]
        self.drr[qn] = (k + 1) % self.NDSEM
        sem = self.dsem[qn][k]
        key = f"d{qn}{k}"
        prev = self.dval[qn][k]
        extra = ((key, sem, prev),) if prev else ()
        self._deps(qn, reads, writes, extra)
        val = prev + 16
        self.dval[qn][k] = val
        kw = {"allow_slow_non_contiguous": True} if slow else {}
        self.q[qn].append(("inst", lambda eng, o=out, i=in_, kw=kw: eng.dma_start(out=o, in_=i, **kw), sem, 16))
        self._mark((key, sem, val), reads, writes)
        self.ninst += 1

    def flush(self):
        for qn in self.dsem:
            for k in range(self.NDSEM):
                if self.dval[qn][k]:
                    self.q["sp"].append(("wait", self.dsem[qn][k], self.dval[qn][k]))
        for e in self.ENG:
            if e != "sp" and self.cnt[e]:
                self.q["sp"].append(("wait", self.sem[e], self.cnt[e]))
        nc = self.nc
        with nc.Block() as block:
            def replay(e):
                items = self.q[e]

                def body(eng):
                    for it in items:
                        if it[0] == "wait":
                            eng.wait_ge(it[1], it[2])
                        else:
                            it[1](eng).then_inc(it[2], it[3])
                return body
            block.tensor(replay("pe"))
            block.scalar(replay("act"))
            block.vector(replay("dve"))
            block.gpsimd(replay("pool"))
            block.sync(replay("sp"))
        self.q = {e: [] for e in self.ENG}


DBG = {"stop": None}
_UID = [0]


def _uid(name):
    _UID[0] += 1
    return f"{name}_{_UID[0]}"


class Rot:
    def __init__(self, es, nc, name, n, shape, dt, psum=False):
        mk = nc.psum_tensor if psum else nc.sbuf_tensor
        self.t = [es.enter_context(mk(_uid(name), list(shape), dt)) for i in range(n)]
        self.b = [Buf() for _ in range(n)]
        self.i = 0

    def next(self):
        k = self.i
        self.i = (k + 1) % len(self.t)
        return self.t[k], self.b[k]


def one(es, nc, name, shape, dt, psum=False):
    mk = nc.psum_tensor if psum else nc.sbuf_tensor
    return es.enter_context(mk(_uid(name), list(shape), dt)), Buf()


def build_program(layers=None):
    if layers is None:
        layers = list(range(DEPTH))
    P = Prog()
    nc = P.nc

    def din(name, shape, dt=F32):
        return nc.dram_tensor(name, list(shape), dt, kind="ExternalInput").ap()

    def dscr(name, shape, dt):
        return nc.dram_tensor(name, list(shape), dt, kind="Internal").ap()

    x_in = din("x_in", [N, D])
    c_in = din("c_in", [NSEG, D])
    pos_in = din("pos_in", [1, N])
    flag_in = din("flag_in", [128, 2])
    norm_g = din("norm_g", [DEPTH, D])
    ada_w = din("ada_w", [DEPTH, D, 3 * D])
    ada_b = din("ada_b", [DEPTH, 3 * D])
    attn_w_in = din("attn_w_in", [2, D, 10 * D])
    attn_w_out = din("attn_w_out", [2, D, D])
    ssm_w_in = din("ssm_w_in", [2, D, 2 * D])
    lam_re_i = din("ssm_lam_re", [2, 2, 64, 64])
    lam_im_i = din("ssm_lam_im", [2, 2, 64, 64])
    log_dt_i = din("ssm_log_dt", [2, 2, 64])
    b_re_i = din("ssm_b_re", [2, 2, 64, 64, 16])
    b_im_i = din("ssm_b_im", [2, 2, 64, 64, 16])
    c_re_i = din("ssm_c_re", [2, 2, 64, 16, 64])
    c_im_i = din("ssm_c_im", [2, 2, 64, 16, 64])
    ssm_d = din("ssm_d", [2, D])
    ssm_w_glu = din("ssm_w_glu", [2, D, D])
    ssm_w_out = din("ssm_w_out", [2, D, D])
    fin_g = din("final_norm_g", [1, D])
    y_out = nc.dram_tensor("y_out", [N, D], F32, kind="ExternalOutput").ap()

    XT = dscr("XT", [8, 128, N], F32)
    ROPE = dscr("ROPE", [2, 128, N], F32)
    QT = dscr("QT", [3, 8, 128, N], BF16)
    KT = dscr("KT", [3, 8, 128, N], BF16)
    VV = dscr("VV", [3, N, D], BF16)
    ZS = dscr("ZS", [8, 128, N], F32)
    YT = dscr("YT", [8, 128, N], BF16)
    UT = dscr("UT", [8, 128, N], BF16)
    Y0 = dscr("Y0", [8, 128, N], F32)
    GT = nc.dram_tensor("GT", [8, 128, N], F32, kind="ExternalOutput").ap() if DBG.get("dumpGT") else dscr("GT", [8, 128, N], F32)
    GB = dscr("GB", [8, 128, N], BF16)
    TAB = dscr("TAB", [2, 2, 32, 128, 4, 128], BF16)
    bX, bROPE, bQT, bKT, bVV, bZS, bYT, bUT, bY0, bGT, bGB, bTAB, bOUT = [Buf() for _ in range(13)]

    top = ExitStack()
    ident_f, b_idf = one(top, nc, "ident_f", [128, 128], F32)
    ones_f, b_onf = one(top, nc, "ones_f", [128, 128], F32)
    ident_b, b_idb = one(top, nc, "ident_b", [128, 128], BF16)
    perm_b, b_perm = one(top, nc, "perm_b", [128, 128], BF16)
    masks, b_masks = one(top, nc, "masks", [128, 9, 128], BF16)
    Eo, b_Eo = one(top, nc, "Eo", [128, 2, 128], BF16)
    flag, b_flag = one(top, nc, "flag", [128, 2], F32)
    modA, b_mod = one(top, nc, "modA", [128, DEPTH, 3, 8, NSEG], F32)
    fing, b_fing = one(top, nc, "fing", [128, 8], F32)
    dvec, b_dvec = one(top, nc, "dvec", [128, 2, 8], F32)
    PW, b_PW = one(top, nc, "PW", [128, 4, 32, 12, 3], F32)

    with ExitStack() as es:
        P.emit("pool", lambda e: e.memset(ident_f[:], 0.0), writes=[b_idf])
        P.emit("pool", lambda e: e.affine_select(out=ident_f[:], in_=ident_f[:], pattern=[[-1, 128]],
                                                 compare_op=ALU.not_equal, fill=1.0, base=0, channel_multiplier=1),
               reads=[b_idf], writes=[b_idf])
        P.emit("dve", lambda e: e.tensor_copy(out=ident_b[:], in_=ident_f[:]), reads=[b_idf], writes=[b_idb])
        P.emit("dve", lambda e: e.memset(ones_f[:], 1.0), writes=[b_onf])
        pf, b_pf = one(es, nc, "pf", [128, 128], F32)
        P.emit("pool", lambda e: e.memset(pf[:], 0.0), writes=[b_pf])
        for blk in range(2):
            for half in range(2):
                c0 = blk * 64 + half * 32
                base = -(c0 + (32 if half == 0 else -32))
                P.emit("pool", lambda e, c0=c0, base=base: e.affine_select(
                    out=pf[:, c0:c0 + 32], in_=pf[:, c0:c0 + 32], pattern=[[-1, 32]],
                    compare_op=ALU.not_equal, fill=1.0, base=base, channel_multiplier=1),
                    reads=[b_pf], writes=[b_pf])
        P.emit("dve", lambda e: e.tensor_copy(out=perm_b[:], in_=pf[:]), reads=[b_pf], writes=[b_perm])
        P.emit("dve", lambda e: e.memset(Eo[:], 0.0), writes=[b_Eo])
        P.emit("dve", lambda e: e.memset(Eo[:, 0, 0:64], 1.0), writes=[b_Eo], reads=[b_Eo])
        P.emit("dve", lambda e: e.memset(Eo[:, 1, 64:128], 1.0), writes=[b_Eo], reads=[b_Eo])
        P.dma("sp", flag[:], flag_in, writes=[b_flag])

        Dm, b_Dm = one(es, nc, "Dm", [128, 128], I32)
        Df, b_Df = one(es, nc, "Df", [128, 128], F32)
        t_i, b_ti = one(es, nc, "t_i", [128, 128], I32)
        mm, b_mm = one(es, nc, "mm", [128, 6, 128], F32)
        mk, b_mk = one(es, nc, "mk", [128, 9, 128], F32)
        P.emit("pool", lambda e: e.iota(Dm[:], pattern=[[-1, 128]], base=128, channel_multiplier=1), writes=[b_Dm])
        P.emit("dve", lambda e: e.tensor_copy(out=Df[:], in_=Dm[:]), reads=[b_Dm], writes=[b_Df])
        for idx, msk in ((0, 15), (1, 3)):
            P.emit("dve", lambda e, msk=msk: e.tensor_single_scalar(out=t_i[:], in_=Dm[:], scalar=msk, op=ALU.bitwise_and),
                   reads=[b_Dm], writes=[b_ti])
            P.emit("dve", lambda e, idx=idx: e.tensor_single_scalar(out=mm[:, idx, :], in_=t_i[:], scalar=0, op=ALU.is_equal),
                   reads=[b_ti], writes=[b_mm])
        P.emit("dve", lambda e: e.tensor_single_scalar(out=mm[:, 2, :], in_=Df[:], scalar=128.0, op=ALU.is_ge), reads=[b_Df], writes=[b_mm])
        P.emit("dve", lambda e: e.tensor_single_scalar(out=mm[:, 3, :], in_=Df[:], scalar=128.0, op=ALU.is_le), reads=[b_Df], writes=[b_mm])
        P.emit("dve", lambda e: e.tensor_single_scalar(out=mk[:, 0, :], in_=Df[:], scalar=192.0, op=ALU.is_ge), reads=[b_Df], writes=[b_mk])
        P.emit("dve", lambda e: e.tensor_scalar(out=mk[:, 1, :], in0=Df[:], scalar1=64.0, scalar2=None, op0=ALU.is_ge), reads=[b_Df], writes=[b_mk])
        P.emit("dve", lambda e: e.tensor_single_scalar(out=mm[:, 4, :], in_=Df[:], scalar=192.0, op=ALU.is_le), reads=[b_Df], writes=[b_mm])
        P.emit("dve", lambda e: e.tensor_tensor(out=mk[:, 1, :], in0=mk[:, 1, :], in1=mm[:, 4, :], op=ALU.mult), reads=[b_mk, b_mm], writes=[b_mk])
        P.emit("dve", lambda e: e.tensor_single_scalar(out=mk[:, 2, :], in_=Df[:], scalar=64.0, op=ALU.is_le), reads=[b_Df], writes=[b_mk])
        for gi, mi in ((1, 1), (2, 0)):
            o = 3 * gi
            P.emit("dve", lambda e, o=o, mi=mi: e.tensor_tensor(out=mk[:, o, :], in0=mm[:, mi, :], in1=mm[:, 2, :], op=ALU.mult), reads=[b_mm], writes=[b_mk])
            P.emit("dve", lambda e, o=o, mi=mi: e.tensor_copy(out=mk[:, o + 1, :], in_=mm[:, mi, :]), reads=[b_mm], writes=[b_mk])
            P.emit("dve", lambda e, o=o, mi=mi: e.tensor_tensor(out=mk[:, o + 2, :], in0=mm[:, mi, :], in1=mm[:, 3, :], op=ALU.mult), reads=[b_mm], writes=[b_mk])
        P.emit("dve", lambda e: e.tensor_scalar(out=masks[:], in0=mk[:], scalar1=-1.0, scalar2=-NEGM, op0=ALU.add, op1=ALU.mult),
               reads=[b_mk], writes=[b_masks])

        invf, b_invf = one(es, nc, "invf", [128, 1], F32)
        pid, b_pid = one(es, nc, "pid", [128, 1], I32)
        pidf, b_pidf = one(es, nc, "pidf", [128, 1], F32)
        sgn, b_sgn = one(es, nc, "sgn", [128, 1], F32)
        P.emit("pool", lambda e: e.iota(pid[:], pattern=[[0, 1]], base=0, channel_multiplier=1), writes=[b_pid])
        P.emit("dve", lambda e: e.tensor_single_scalar(out=pid[:], in_=pid[:], scalar=31, op=ALU.bitwise_and), reads=[b_pid], writes=[b_pid])
        P.emit("dve", lambda e: e.tensor_copy(out=pidf[:], in_=pid[:]), reads=[b_pid], writes=[b_pidf])
        P.emit("act", lambda e: e.activation(out=invf[:], in_=pidf[:], func=AF.Exp, scale=-math.log(10000.0) / 32.0),
               reads=[b_pidf], writes=[b_invf])
        P.emit("dve", lambda e: e.memset(sgn[:], 1.0), writes=[b_sgn])
        for hb in range(2):
            P.emit("dve", lambda e, hb=hb: e.memset(sgn[hb * 64:hb * 64 + 32, :], -1.0), reads=[b_sgn], writes=[b_sgn])
        CH = 2048
        posb = Rot(es, nc, "posb", 2, [128, CH], F32)
        ang = Rot(es, nc, "ang", 2, [128, CH], F32)
        kf = Rot(es, nc, "kf", 2, [128, CH], F32)
        ki = Rot(es, nc, "ki", 2, [128, CH], I32)
        tro = Rot(es, nc, "tro", 2, [128, 2, CH], F32)
        for cb in range(N // CH):
            sl = slice(cb * CH, (cb + 1) * CH)
            pt, pb_ = posb.next()
            P.dma("sp", pt[:], pos_in[0:1, sl].to_broadcast([128, CH]), writes=[pb_])
            at, ab = ang.next()
            P.emit("dve", lambda e, at=at, pt=pt: e.tensor_scalar(out=at[:], in0=pt[:], scalar1=invf[:, 0:1], scalar2=None, op0=ALU.mult),
                   reads=[pb_, b_invf], writes=[ab])
            kt_, kb_ = kf.next()
            kit, kib = ki.next()
            P.emit("dve", lambda e, at=at, kt_=kt_: e.tensor_scalar(out=kt_[:], in0=at[:], scalar1=1.0 / TWO_PI, scalar2=None, op0=ALU.mult),
                   reads=[ab], writes=[kb_])
            P.emit("dve", lambda e, kit=kit, kt_=kt_: e.tensor_copy(out=kit[:], in_=kt_[:]), reads=[kb_], writes=[kib])
            P.emit("dve", lambda e, kit=kit, kt_=kt_: e.tensor_copy(out=kt_[:], in_=kit[:]), reads=[kib], writes=[kb_])
            P.emit("dve", lambda e, at=at, kt_=kt_: e.scalar_tensor_tensor(out=at[:], in0=kt_[:], scalar=-C1, in1=at[:], op0=ALU.mult, op1=ALU.add),
                   reads=[kb_, ab], writes=[ab])
            P.emit("dve", lambda e, at=at, kt_=kt_: e.scalar_tensor_tensor(out=at[:], in0=kt_[:], scalar=-C2, in1=at[:], op0=ALU.mult, op1=ALU.add),
                   reads=[kb_, ab], writes=[ab])
            tt, tb_ = tro.next()
            P.emit("dve", lambda e, at=at, tt=tt: e.tensor_scalar(out=tt[:, 0, :], in0=at[:], scalar1=math.pi / 2, scalar2=None, op0=ALU.add), reads=[ab], writes=[tb_])
            P.emit("dve", lambda e, at=at, tt=tt: e.tensor_single_scalar(out=tt[:, 1, :], in_=tt[:, 0, :], scalar=math.pi, op=ALU.is_gt), reads=[tb_], writes=[tb_])
            P.emit("dve", lambda e, at=at, tt=tt: e.scalar_tensor_tensor(out=tt[:, 0, :], in0=tt[:, 1, :], scalar=-TWO_PI, in1=tt[:, 0, :], op0=ALU.mult, op1=ALU.add), reads=[tb_], writes=[tb_])
            P.emit("dve", lambda e, at=at, tt=tt: e.tensor_copy(out=tt[:, 1, :], in_=at[:]), reads=[ab, tb_], writes=[tb_])
            P.emit("dve", lambda e, tt=tt: e.tensor_scalar(out=tt[:], in0=tt[:], scalar1=math.pi, scalar2=-math.pi, op0=ALU.min, op1=ALU.max),
                   reads=[tb_], writes=[tb_])
            P.emit("act", lambda e, tt=tt: e.activation(out=tt[:], in_=tt[:], func=AF.Sin), reads=[tb_], writes=[tb_])
            P.emit("dve", lambda e, tt=tt: e.tensor_scalar(out=tt[:, 1, :], in0=tt[:, 1, :], scalar1=sgn[:, 0:1], scalar2=None, op0=ALU.mult),
                   reads=[tb_, b_sgn], writes=[tb_])
            P.dma("pool", ROPE[:, :, sl].rearrange("c p t -> p c t"), tt[:], reads=[tb_], writes=[bROPE])

        cT, b_cT = one(es, nc, "cT", [128, 8, NSEG], F32)
        cs, b_cs = one(es, nc, "cs", [128, 8, NSEG], BF16)
        adab, b_adab = one(es, nc, "adab", [128, DEPTH, 24], F32)
        ng, b_ng = one(es, nc, "ng", [128, DEPTH, 8], F32)
        for s_ in range(NSEG):
            P.dma("sp", cT[:, :, s_], c_in[s_:s_ + 1, :].rearrange("o (k p) -> p (o k)", p=128), writes=[b_cT], reads=[b_cT], slow=True)
        for l_ in range(DEPTH):
            P.dma("sp", adab[:, l_, :], ada_b[l_:l_ + 1, :].rearrange("o (f p) -> p (o f)", p=128), writes=[b_adab], reads=[b_adab], slow=True)
            P.dma("sp", ng[:, l_, :], norm_g[l_:l_ + 1, :].rearrange("o (f p) -> p (o f)", p=128), writes=[b_ng], reads=[b_ng], slow=True)
        P.dma("sp", fing[:], fin_g.rearrange("o (f p) -> p (o f)", p=128), writes=[b_fing], slow=True)
        for l_ in range(2):
            P.dma("sp", dvec[:, l_, :], ssm_d[l_:l_ + 1, :].rearrange("o (f p) -> p (o f)", p=128), writes=[b_dvec], reads=[b_dvec], slow=True)
        P.emit("act", lambda e: e.activation(out=cs[:], in_=cT[:], func=AF.Silu), reads=[b_cT], writes=[b_cs])
        aw32 = Rot(es, nc, "aw32", 2, [128, 8, 128], F32)
        awb = Rot(es, nc, "awb", 2, [128, 8, 128], BF16)
        pada = Rot(es, nc, "pada", 2, [128, 512], F32, psum=True)
        adat, b_adat = one(es, nc, "adat", [128, DEPTH, 24, NSEG], F32)
        for l in range(DEPTH):
            for f in range(24):
                w32, wb32 = aw32.next()
                P.dma("sp", w32[:], ada_w[l:l + 1, :, f * 128:(f + 1) * 128].rearrange("o (k p) j -> p (o k) j", p=128), writes=[wb32])
                wb, wbb = awb.next()
                P.emit("pool", lambda e, wb=wb, w32=w32: e.tensor_copy(out=wb[:], in_=w32[:]), reads=[wb32], writes=[wbb])
                ps, psb = pada.next()
                for k in range(8):
                    P.emit("pe", lambda e, ps=ps, wb=wb, k=k: e.matmul(ps[:, 0:NSEG], lhsT=wb[:, k, :], rhs=cs[:, k, :], start=(k == 0), stop=(k == 7)),
                           reads=[wbb, b_cs], writes=[psb])
                P.emit("dve", lambda e, ps=ps, l=l, f=f: e.tensor_scalar(out=adat[:, l, f, :], in0=ps[:, 0:NSEG], scalar1=adab[:, l, f:f + 1], scalar2=None, op0=ALU.add),
                       reads=[psb, b_adab], writes=[b_adat])
        for l in range(DEPTH):
            for s in range(NSEG):
                P.emit("dve", lambda e, l=l, s=s: e.tensor_scalar(out=modA[:, l, 0, :, s], in0=adat[:, l, 8:16, s], scalar1=1.0, scalar2=None, op0=ALU.add),
                       reads=[b_adat], writes=[b_mod])
                P.emit("dve", lambda e, l=l, s=s: e.tensor_tensor(out=modA[:, l, 0, :, s], in0=modA[:, l, 0, :, s], in1=ng[:, l, :], op=ALU.mult),
                       reads=[b_mod, b_ng], writes=[b_mod])
                P.emit("dve", lambda e, l=l, s=s: e.tensor_copy(out=modA[:, l, 1, :, s], in_=adat[:, l, 0:8, s]), reads=[b_adat, b_mod], writes=[b_mod])
                P.emit("dve", lambda e, l=l, s=s: e.tensor_copy(out=modA[:, l, 2, :, s], in_=adat[:, l, 16:24, s]), reads=[b_adat, b_mod], writes=[b_mod])

        xin = Rot(es, nc, "xin", 2, [128, D], F32)
        pxt = Rot(es, nc, "pxt", 2, [128, 512], F32, psum=True)
        xst = Rot(es, nc, "xst", 2, [128, 8, 128], F32)
        for tt_ in range(NT):
            xt, xb = xin.next()
            P.dma("sp", xt[:], x_in[tt_ * 128:(tt_ + 1) * 128, :], writes=[xb])
            st, sb_ = xst.next()
            for h in range(2):
                ps, psb = pxt.next()
                for j in range(4):
                    f = h * 4 + j
                    P.emit("pe", lambda e, ps=ps, xt=xt, f=f, j=j: e.matmul(ps[:, j * 128:(j + 1) * 128], lhsT=xt[:, f * 128:(f + 1) * 128], rhs=ident_f[:], start=True, stop=True),
                           reads=[xb, b_idf], writes=[psb])
                P.emit("act", lambda e, ps=ps, st=st, h=h: e.activation(out=st[:, h * 4:(h + 1) * 4, :], in_=ps[:].rearrange("p (j t) -> p j t", j=4), func=AF.Copy),
                       reads=[psb], writes=[sb_])
            P.dma("pool", XT[:, :, tt_ * 128:(tt_ + 1) * 128].rearrange("f p t -> p f t"), st[:], reads=[sb_], writes=[bX])
        P.flush()

    def norm_phase(es, l, s, hm, b_hm):
        xb_r = Rot(es, nc, "nx", 2, [128, 8, 512], F32)
        sq_r = Rot(es, nc, "nsq", 2, [128, 512], F32)
        ps_r = Rot(es, nc, "nps", 2, [128, 512], F32, psum=True)
        rs_r = Rot(es, nc, "nrs", 2, [128, 512], F32)
        tm_r = Rot(es, nc, "ntm", 3, [128, 512], F32)
        for tb in range(SEG // 512):
            t0 = s * SEG + tb * 512
            xt, xb = xb_r.next()
            P.dma("sp", xt[:], XT[:, :, t0:t0 + 512].rearrange("f p t -> p f t"), reads=[bX], writes=[xb])
            ps, psb = ps_r.next()
            for f in range(8):
                sq, sqb = sq_r.next()
                P.emit("act", lambda e, sq=sq, xt=xt, f=f: e.activation(out=sq[:], in_=xt[:, f, :], func=AF.Square), reads=[xb], writes=[sqb])
                P.emit("pe", lambda e, ps=ps, sq=sq, f=f: e.matmul(ps[:], lhsT=ones_f[:], rhs=sq[:], start=(f == 0), stop=(f == 7)),
                       reads=[sqb, b_onf], writes=[psb])
            rs, rsb = rs_r.next()
            P.emit("act", lambda e, rs=rs, ps=ps: e.activation(out=rs[:], in_=ps[:], func=AF.Sqrt, bias=1e-6, scale=1.0 / D), reads=[psb], writes=[rsb])
            P.emit("dve", lambda e, rs=rs: e.reciprocal(out=rs[:], in_=rs[:]), reads=[rsb], writes=[rsb])
            for f in range(8):
                tm, tmb = tm_r.next()
                P.emit("dve", lambda e, tm=tm, xt=xt, rs=rs, f=f: e.tensor_tensor(out=tm[:], in0=xt[:, f, :], in1=rs[:], op=ALU.mult), reads=[xb, rsb], writes=[tmb])
                if l < DEPTH:
                    P.emit("act", lambda e, tm=tm, f=f, tb=tb: e.activation(out=hm[:, f, tb * 512:(tb + 1) * 512], in_=tm[:], func=AF.Identity,
                                                                       bias=modA[:, l, 1, f, s:s + 1], scale=modA[:, l, 0, f, s:s + 1]),
                           reads=[tmb, b_mod], writes=[b_hm])

    def load_w_bf16(es, name, w_ap, ncol, wb, b_wb, rot32):
        for k in range(8):
            w32, b32 = rot32.next()
            P.dma("sp", w32[:, 0:ncol], w_ap[0:1, k * 128:(k + 1) * 128, :].rearrange("o p j -> p (o j)"), writes=[b32])
            P.emit("pool", lambda e, w32=w32, k=k: e.tensor_copy(out=wb[:, k, :], in_=w32[:, 0:ncol]), reads=[b32], writes=[b_wb])

    def outproj_phase(l, w_ap):
        with ExitStack() as es:
            wb, b_wb = one(es, nc, "ow", [128, 8, D], BF16)
            r32 = Rot(es, nc, "ow32", 2, [128, D], F32)
            load_w_bf16(es, "ow", w_ap, D, wb, b_wb, r32)
            yb_r = Rot(es, nc, "oy", 2, [128, 8, 512], BF16)
            xb_r = Rot(es, nc, "ox", 2, [128, 8, 512], F32)
            ps_r = Rot(es, nc, "ops", 3, [128, 512], F32, psum=True)
            for tb in range(N // 512):
                s = (tb * 512) // SEG
                t0 = tb * 512
                yt, yb = yb_r.next()
                P.dma("sp", yt[:], YT[:, :, t0:t0 + 512].rearrange("f p t -> p f t"), reads=[bYT], writes=[yb])
                xt, xb = xb_r.next()
                P.dma("sp", xt[:], XT[:, :, t0:t0 + 512].rearrange("f p t -> p f t"), reads=[bX], writes=[xb])
                for f in range(8):
                    ps, psb = ps_r.next()
                    for k in range(8):
                        P.emit("pe", lambda e, ps=ps, k=k, f=f, yt=yt: e.matmul(ps[:], lhsT=wb[:, k, f * 128:(f + 1) * 128], rhs=yt[:, k, :], start=(k == 0), stop=(k == 7)),
                               reads=[b_wb, yb], writes=[psb])
                    P.emit("dve", lambda e, ps=ps, xt=xt, f=f, s=s: e.scalar_tensor_tensor(out=xt[:, f, :], in0=ps[:], scalar=modA[:, l, 2, f, s:s + 1], in1=xt[:, f, :],
                                                                                  op0=ALU.mult, op1=ALU.add),
                           reads=[psb, xb, b_mod], writes=[xb])
                P.dma("pool", XT[:, :, t0:t0 + 512].rearrange("f p t -> p f t"), xt[:], reads=[xb], writes=[bX])
            P.flush()

    def attn_layer(l, la):
        for s in range(NSEG):
            with ExitStack() as es:
                t1_r = Rot(es, nc, "at1", 2, [128, 512], F32)
                t2_r = Rot(es, nc, "at2", 2, [128, 512], F32)
                hm, b_hm = one(es, nc, "hm", [128, 8, SEG], BF16)
                norm_phase(es, l, s, hm, b_hm)
                w32 = Rot(es, nc, "aw32", 2, [128, 8, 128], F32)
                wbr = Rot(es, nc, "awb", 2, [128, 8, 128], BF16)
                ps_r = Rot(es, nc, "aps", 3, [128, 512], F32, psum=True)
                ps2_r = Rot(es, nc, "aps2", 2, [128, 512], F32, psum=True)
                qb_r = Rot(es, nc, "aqb", 2, [128, 512], BF16)
                rp_r = Rot(es, nc, "arp", 2, [128, 2, 512], F32)
                ob_r = Rot(es, nc, "aob", 3, [128, 512], BF16)
                zo_r = Rot(es, nc, "azo", 2, [128, 512], F32)
                for slab in range(80):
                    col0 = slab * 128
                    kind = col0 // D
                    if DBG.get("kinds") is not None and (9 if kind == 9 else kind % 3) not in DBG["kinds"]:
                        continue
                    sl8 = slab % 8
                    wt, wtb = w32.next()
                    P.dma("sp", wt[:], attn_w_in[la:la + 1, :, col0:col0 + 128].rearrange("o (k p) j -> p (o k) j", p=128), writes=[wtb])
                    wb, wbb = wbr.next()
                    P.emit("pool", lambda e, wb=wb, wt=wt: e.tensor_copy(out=wb[:], in_=wt[:]), reads=[wtb], writes=[wbb])
                    if kind < 9 and kind % 3 == 2:
                        g = kind // 3
                        for t4 in range(SEG // 512):
                            ps, psb = ps_r.next()
                            for j in range(4):
                                tk = t4 * 4 + j
                                for k in range(8):
                                    P.emit("pe", lambda e, ps=ps, j=j, k=k, tk=tk, wb=wb: e.matmul(ps[:, j * 128:(j + 1) * 128], lhsT=hm[:, k, tk * 128:(tk + 1) * 128], rhs=wb[:, k, :],
                                                                                          start=(k == 0), stop=(k == 7)),
                                           reads=[b_hm, wbb], writes=[psb])
                            ob, obb = ob_r.next()
                            P.emit("act", lambda e, ob=ob, ps=ps: e.activation(out=ob[:], in_=ps[:], func=AF.Copy), reads=[psb], writes=[obb])
                            r0 = s * SEG + t4 * 512
                            P.dma("pool", VV[g:g + 1, r0:r0 + 512, col0 % D:col0 % D + 128].rearrange("o (j p) c -> p (o j) c", p=128),
                                  ob[:].rearrange("p (j c) -> p j c", j=4), reads=[obb], writes=[bVV])
                        continue
                    for tb in range(SEG // 512):
                        t0 = s * SEG + tb * 512
                        ps, psb = ps_r.next()
                        for k in range(8):
                            P.emit("pe", lambda e, ps=ps, k=k, tb=tb, wb=wb: e.matmul(ps[:], lhsT=wb[:, k, :], rhs=hm[:, k, tb * 512:(tb + 1) * 512], start=(k == 0), stop=(k == 7)),
                                   reads=[b_hm, wbb], writes=[psb])
                        if kind == 9:
                            zo, zob = zo_r.next()
                            P.emit("act", lambda e, zo=zo, ps=ps: e.activation(out=zo[:], in_=ps[:], func=AF.Silu), reads=[psb], writes=[zob])
                            P.dma("pool", ZS[sl8, :, t0:t0 + 512], zo[:], reads=[zob], writes=[bZS])
                            continue
                        g = kind // 3
                        kq = DBG.get("kq", 9)
                        qb, qbb = qb_r.next()
                        P.emit("act", lambda e, qb=qb, ps=ps: e.activation(out=qb[:], in_=ps[:], func=AF.Copy), reads=[psb], writes=[qbb])
                        ob, obb = ob_r.next()
                        if kq >= 3:
                            ps2, ps2b = ps2_r.next()
                            P.emit("pe", lambda e, ps2=ps2, qb=qb: e.matmul(ps2[:], lhsT=perm_b[:], rhs=qb[:], start=True, stop=True), reads=[qbb, b_perm], writes=[ps2b])
                        if kq >= 2:
                            rp, rpb = rp_r.next()
                            P.dma("sp", rp[:], ROPE[:, :, t0:t0 + 512].rearrange("c p t -> p c t"), reads=[bROPE], writes=[rpb])
                        if kq >= 4:
                            t1, t1b = t1_r.next()
                            t2, t2b = t2_r.next()
                            if kq in (4, 5, 9):
                                P.emit("dve", lambda e, t1=t1, ps=ps, rp=rp: e.tensor_tensor(out=t1[:], in0=ps[:], in1=rp[:, 0, :], op=ALU.mult), reads=[psb, rpb, qbb], writes=[t1b])
                            else:
                                P.emit("dve", lambda e, t1=t1, qb=qb, rp=rp: e.tensor_tensor(out=t1[:], in0=qb[:], in1=rp[:, 0, :], op=ALU.mult), reads=[qbb, rpb], writes=[t1b])
                            if kq in (4, 6, 9):
                                P.emit("dve", lambda e, t2=t2, ps2=ps2, rp=rp: e.tensor_tensor(out=t2[:], in0=ps2[:], in1=rp[:, 1, :], op=ALU.mult), reads=[ps2b, rpb], writes=[t2b])
                            else:
                                P.emit("dve", lambda e, t2=t2, qb=qb, rp=rp: e.tensor_tensor(out=t2[:], in0=qb[:], in1=rp[:, 1, :], op=ALU.mult), reads=[qbb, rpb], writes=[t2b])
                            P.emit("dve", lambda e, ob=ob, t1=t1, t2=t2: e.tensor_tensor(out=ob[:], in0=t1[:], in1=t2[:], op=ALU.add), reads=[t1b, t2b], writes=[obb])
                        else:
                            P.emit("dve", lambda e, ob=ob, qb=qb: e.tensor_copy(out=ob[:], in_=qb[:]), reads=[qbb], writes=[obb])
                        dst = (QT if kind % 3 == 0 else KT)
                        dbuf = (bQT if kind % 3 == 0 else bKT)
                        P.dma("pool", dst[g, sl8, :, t0:t0 + 512], ob[:], reads=[obb], writes=[dbuf])
                P.flush()

        if DBG["stop"] == "proj":
            return
        with ExitStack() as es:
            NK = 25
            q_r = Rot(es, nc, "cq", 2, [128, 3, 2, 128], BF16)
            for i in range(2):
                P.emit("pool", lambda e, i=i: e.memset(q_r.t[i][:], 0.0), writes=[q_r.b[i]])
            k_r = Rot(es, nc, "ck", 2, [128, NK * 128], BF16)
            v_r = Rot(es, nc, "cv", 2, [128, NK, 2, 128], BF16)
            for i in range(2):
                P.emit("pool", lambda e, i=i: e.memset(v_r.t[i][:], 0.0), writes=[v_r.b[i]])
            s_r = Rot(es, nc, "cs", 3, [128, 512], F32, psum=True)
            o_r = Rot(es, nc, "co", 2, [128, 512], F32, psum=True)
            p_r = Rot(es, nc, "cp", 4, [128, 256], BF16)
            rc_r = Rot(es, nc, "crc", 2, [128, 128], F32)
            on_r = Rot(es, nc, "con", 2, [128, 128], F32)
            z_r = Rot(es, nc, "cz", 2, [128, 8, 128], F32)
            y_r = Rot(es, nc, "cy", 2, [128, 8, 128], BF16)
            reach = (1, 2, 8)
            for qt in range(NT):
                seg = qt // (SEG // 128)
                q0 = qt * 128
                zt, zb = z_r.next()
                P.dma("sp", zt[:], ZS[:, :, q0:q0 + 128].rearrange("f p t -> p f t"), reads=[bZS], writes=[zb])
                yt, yb = y_r.next()
                for sl in range(8):
                    qtile, qb_ = q_r.next()
                    for h2 in range(2):
                        P.dma("sp", qtile[h2 * 64:(h2 + 1) * 64, :, h2, :], QT[:, sl, h2 * 64:(h2 + 1) * 64, q0:q0 + 128].rearrange("g p t -> p g t"), reads=[bQT], writes=[qb_])
                    ktile, kb_ = k_r.next()
                    vtile, vb_ = v_r.next()
                    tiles = []
                    off = 0
                    for g in range(3):
                        lo = max(0, qt - reach[g])
                        hi = min(NT - 1, qt + reach[g])
                        n = hi - lo + 1
                        P.dma("sp", ktile[:, off * 128:(off + n) * 128], KT[g, sl, :, lo * 128:(hi + 1) * 128], reads=[bKT], writes=[kb_])
                        for h2 in range(2):
                            P.dma("sp", vtile[:, off:off + n, h2, h2 * 64:(h2 + 1) * 64],
                                  VV[g:g + 1, lo * 128:(hi + 1) * 128, sl * 128 + h2 * 64:sl * 128 + (h2 + 1) * 64].rearrange("o (k p) c -> p (o k) c", p=128),
                                  reads=[bVV], writes=[vb_])
                        for kt_abs in range(lo, hi + 1):
                            j = kt_abs - qt
                            mi = 3 * g + (0 if j == -reach[g] else (2 if j == reach[g] else 1))
                            cross = (kt_abs // (SEG // 128)) != seg
                            tiles.append((g, off + (kt_abs - lo), mi, cross))
                        off += n
                    ot, ob_ = o_r.next()

                    def emit_qk(ti, ktile=ktile, qtile=qtile, kb_=kb_, qb_=qb_, tiles=tiles):
                        g, ix, mi, cross = tiles[ti]
                        st, sb_ = s_r.next()
                        P.emit("pe", lambda e, st=st, ix=ix, g=g: e.matmul(st[:, 0:256], lhsT=ktile[:, ix * 128:(ix + 1) * 128], rhs=qtile[:, g, :, :], start=True, stop=False),
                               reads=[kb_, qb_], writes=[sb_])
                        P.emit("pe", lambda e, st=st, mi=mi: e.matmul(st[:, 0:256], lhsT=ident_b[:], rhs=masks[:, mi:mi + 1, :].to_broadcast([128, 2, 128]), start=False, stop=True),
                               reads=[b_idb, b_masks], writes=[sb_])
                        return st, sb_
                    pending = emit_qk(0)
                    for ti, (g, ix, mi, cross) in enumerate(tiles):
                        st, sb_ = pending
                        if ti + 1 < len(tiles):
                            pending = emit_qk(ti + 1)
                        pt, pb_ = p_r.next()
                        if cross:
                            P.emit("act", lambda e, pt=pt, st=st: e.activation(out=pt[:], in_=st[:, 0:256], func=AF.Exp, bias=flag[:, 1:2], scale=0.125),
                                   reads=[sb_, b_flag], writes=[pb_])
                        else:
                            P.emit("act", lambda e, pt=pt, st=st: e.activation(out=pt[:], in_=st[:, 0:256], func=AF.Exp, scale=0.125), reads=[sb_], writes=[pb_])
                        first = ti == 0
                        last = ti == len(tiles) - 1
                        P.emit("pe", lambda e, ot=ot, vtile=vtile, pt=pt, ix=ix, first=first: e.matmul(ot[:, 0:128], lhsT=vtile[:, ix, 0, :], rhs=pt[:, 0:128], start=first, stop=False),
                               reads=[vb_, pb_], writes=[ob_])
                        P.emit("pe", lambda e, ot=ot, vtile=vtile, pt=pt, ix=ix: e.matmul(ot[:, 0:128], lhsT=vtile[:, ix, 1, :], rhs=pt[:, 128:256], start=False, stop=False),
                               reads=[vb_, pb_], writes=[ob_])
                        P.emit("pe", lambda e, ot=ot, pt=pt: e.matmul(ot[:, 128:256], lhsT=Eo[:, 0, :], rhs=pt[:, 0:128], start=False, stop=False),
                               reads=[b_Eo, pb_], writes=[ob_])
                        P.emit("pe", lambda e, ot=ot, pt=pt, last=last: e.matmul(ot[:, 128:256], lhsT=Eo[:, 1, :], rhs=pt[:, 128:256], start=False, stop=last),
                               reads=[b_Eo, pb_], writes=[ob_])
                    rc, rcb = rc_r.next()
                    P.emit("dve", lambda e, rc=rc, ot=ot: e.reciprocal(out=rc[:], in_=ot[:, 128:256]), reads=[ob_], writes=[rcb])
                    on, onb = on_r.next()
                    P.emit("dve", lambda e, on=on, ot=ot, rc=rc: e.tensor_tensor(out=on[:], in0=ot[:, 0:128], in1=rc[:], op=ALU.mult), reads=[ob_, rcb], writes=[onb])
                    P.emit("dve", lambda e, on=on, yt=yt, zt=zt, sl=sl: e.tensor_tensor(out=yt[:, sl, :], in0=on[:], in1=zt[:, sl, :], op=ALU.mult), reads=[onb, zb], writes=[yb])
                P.dma("pool", YT[:, :, q0:q0 + 128].rearrange("f p t -> p f t"), yt[:], reads=[yb], writes=[bYT])
            P.flush()
        if DBG["stop"] == "core":
            return
        outproj_phase(l, attn_w_out[la:la + 1, :, :])

    def ssm_prep(lb):
        with ExitStack() as es:
            mg2, b_mg2 = one(es, nc, "mg2", [128, 2], F32)
            mq, b_mq = one(es, nc, "mq", [128, 8], F32)
            pid, b_pid = one(es, nc, "spid", [128, 1], I32)
            pq, b_pq = one(es, nc, "spq", [128, 1], I32)
            pqf, b_pqf = one(es, nc, "spqf", [128, 1], F32)
            P.emit("dve", lambda e: e.memset(mg2[:], 0.0), writes=[b_mg2])
            P.emit("dve", lambda e: e.memset(mg2[0:64, 0:1], 1.0), reads=[b_mg2], writes=[b_mg2])
            P.emit("dve", lambda e: e.memset(mg2[64:128, 1:2], 1.0), reads=[b_mg2], writes=[b_mg2])
            P.emit("pool", lambda e: e.iota(pid[:], pattern=[[0, 1]], base=0, channel_multiplier=1), writes=[b_pid])
            P.emit("dve", lambda e: e.tensor_single_scalar(out=pq[:], in_=pid[:], scalar=4, op=ALU.arith_shift_right), reads=[b_pid], writes=[b_pq])
            P.emit("dve", lambda e: e.tensor_copy(out=pqf[:], in_=pq[:]), reads=[b_pq], writes=[b_pqf])
            for qq in range(8):
                P.emit("dve", lambda e, qq=qq: e.tensor_single_scalar(out=mq[:, qq:qq + 1], in_=pqf[:], scalar=float(qq), op=ALU.is_equal), reads=[b_pqf], writes=[b_mq])
            psm = Rot(es, nc, "sps", 3, [128, 512], F32, psum=True)
            for d in range(2):
                ld = lb * 2 + d
                nat, b_nat = one(es, nc, f"nat{d}", [32, 3, 128], F32)
                ldt, b_ldt = one(es, nc, f"ldt{d}", [32, 2], F32)
                P.dma("sp", nat[:, 0, :], lam_re_i[lb, d].rearrange("(q a) p -> q (a p)", a=2), writes=[b_nat])
                P.dma("sp", nat[:, 1, :], lam_im_i[lb, d].rearrange("(q a) p -> q (a p)", a=2), writes=[b_nat], reads=[b_nat])
                P.dma("sp", ldt[:], log_dt_i[lb, d:d + 1, :].rearrange("o (q a) -> q (o a)", a=2), writes=[b_ldt])
                for a in range(2):
                    P.emit("dve", lambda e, a=a, nat=nat, ldt=ldt: e.tensor_scalar(out=nat[:, 2, a * 64:(a + 1) * 64], in0=nat[:, 0, a * 64:(a + 1) * 64], scalar1=0.0, scalar2=ldt[:, a:a + 1],
                                                                         op0=ALU.mult, op1=ALU.add), reads=[b_nat, b_ldt], writes=[b_nat])
                L, b_L = one(es, nc, f"L{d}", [128, 16, 32], F32)
                ps, psb = psm.next()
                for i in range(3):
                    P.emit("pe", lambda e, ps=ps, i=i, nat=nat: e.matmul(ps[:, i * 32:(i + 1) * 32], lhsT=nat[:, i, :], rhs=ident_f[0:32, 0:32], start=True, stop=True),
                           reads=[b_nat, b_idf], writes=[psb])
                P.emit("dve", lambda e, ps=ps, L=L: e.tensor_copy(out=L[:, 0:3, :], in_=ps[:, 0:96].rearrange("p (i q) -> p i q", i=3)), reads=[psb], writes=[b_L])
                Ki, b_Ki = one(es, nc, f"Ki{d}", [128, 32], I32)

                def dv(fn):
                    P.emit("dve", fn, reads=[b_L], writes=[b_L])

                def ac(fn):
                    P.emit("act", fn, reads=[b_L], writes=[b_L])
                dv(lambda e, L=L: e.tensor_scalar(out=L[:, 0, :], in0=L[:, 0, :], scalar1=-1e-4, scalar2=None, op0=ALU.min))
                ac(lambda e, L=L: e.activation(out=L[:, 3, :], in_=L[:, 2, :], func=AF.Exp))
                dv(lambda e, L=L: e.tensor_tensor(out=L[:, 4, :], in0=L[:, 0, :], in1=L[:, 3, :], op=ALU.mult))
                dv(lambda e, L=L: e.tensor_tensor(out=L[:, 5, :], in0=L[:, 1, :], in1=L[:, 3, :], op=ALU.mult))
                ac(lambda e, L=L: e.activation(out=L[:, 6, :], in_=L[:, 4, :], func=AF.Exp))
                dv(lambda e, L=L: e.tensor_scalar(out=L[:, 13, :], in0=L[:, 5, :], scalar1=1.0 / TWO_PI, scalar2=None, op0=ALU.mult))
                P.emit("dve", lambda e, L=L, Ki=Ki: e.tensor_copy(out=Ki[:], in_=L[:, 13, :]), reads=[b_L], writes=[b_Ki])
                P.emit("dve", lambda e, L=L, Ki=Ki: e.tensor_copy(out=L[:, 13, :], in_=Ki[:]), reads=[b_Ki, b_L], writes=[b_L])
                dv(lambda e, L=L: e.scalar_tensor_tensor(out=L[:, 14, :], in0=L[:, 13, :], scalar=-C1, in1=L[:, 5, :], op0=ALU.mult, op1=ALU.add))
                dv(lambda e, L=L: e.scalar_tensor_tensor(out=L[:, 14, :], in0=L[:, 13, :], scalar=-C2, in1=L[:, 14, :], op0=ALU.mult, op1=ALU.add))
                dv(lambda e, L=L: e.tensor_scalar(out=L[:, 7, :], in0=L[:, 14, :], scalar1=math.pi / 2, scalar2=None, op0=ALU.add))
                dv(lambda e, L=L: e.tensor_single_scalar(out=L[:, 8, :], in_=L[:, 7, :], scalar=math.pi, op=ALU.is_gt))
                dv(lambda e, L=L: e.scalar_tensor_tensor(out=L[:, 7, :], in0=L[:, 8, :], scalar=-TWO_PI, in1=L[:, 7, :], op0=ALU.mult, op1=ALU.add))
                dv(lambda e, L=L: e.tensor_copy(out=L[:, 8, :], in_=L[:, 14, :]))
                dv(lambda e, L=L: e.tensor_scalar(out=L[:, 7:9, :], in0=L[:, 7:9, :], scalar1=math.pi, scalar2=-math.pi, op0=ALU.min, op1=ALU.max))
                ac(lambda e, L=L: e.activation(out=L[:, 7:9, :], in_=L[:, 7:9, :], func=AF.Sin))
                dv(lambda e, L=L: e.tensor_tensor(out=L[:, 9, :], in0=L[:, 6, :], in1=L[:, 7, :], op=ALU.mult))
                dv(lambda e, L=L: e.tensor_tensor(out=L[:, 10, :], in0=L[:, 6, :], in1=L[:, 8, :], op=ALU.mult))
                dv(lambda e, L=L: e.tensor_scalar(out=L[:, 13, :], in0=L[:, 9, :], scalar1=-1.0, scalar2=None, op0=ALU.add))
                dv(lambda e, L=L: e.tensor_tensor(out=L[:, 14, :], in0=L[:, 0, :], in1=L[:, 0, :], op=ALU.mult))
                dv(lambda e, L=L: e.tensor_tensor(out=L[:, 15, :], in0=L[:, 1, :], in1=L[:, 1, :], op=ALU.mult))
                dv(lambda e, L=L: e.tensor_tensor(out=L[:, 14, :], in0=L[:, 14, :], in1=L[:, 15, :], op=ALU.add))
                dv(lambda e, L=L: e.reciprocal(out=L[:, 14, :], in_=L[:, 14, :]))
                dv(lambda e, L=L: e.tensor_tensor(out=L[:, 11, :], in0=L[:, 13, :], in1=L[:, 0, :], op=ALU.mult))
                dv(lambda e, L=L: e.tensor_tensor(out=L[:, 15, :], in0=L[:, 10, :], in1=L[:, 1, :], op=ALU.mult))
                dv(lambda e, L=L: e.tensor_tensor(out=L[:, 11, :], in0=L[:, 11, :], in1=L[:, 15, :], op=ALU.add))
                dv(lambda e, L=L: e.tensor_tensor(out=L[:, 11, :], in0=L[:, 11, :], in1=L[:, 14, :], op=ALU.mult))
                dv(lambda e, L=L: e.tensor_tensor(out=L[:, 12, :], in0=L[:, 10, :], in1=L[:, 0, :], op=ALU.mult))
                dv(lambda e, L=L: e.tensor_tensor(out=L[:, 15, :], in0=L[:, 13, :], in1=L[:, 1, :], op=ALU.mult))
                dv(lambda e, L=L: e.tensor_tensor(out=L[:, 12, :], in0=L[:, 12, :], in1=L[:, 15, :], op=ALU.subtract))
                dv(lambda e, L=L: e.tensor_tensor(out=L[:, 12, :], in0=L[:, 12, :], in1=L[:, 14, :], op=ALU.mult))
                P.emit("dve", lambda e, L=L, ld=ld: e.tensor_copy(out=PW[:, ld, :, 0, 0], in_=L[:, 9, :]), reads=[b_L, b_PW], writes=[b_PW])
                P.emit("dve", lambda e, L=L, ld=ld: e.tensor_copy(out=PW[:, ld, :, 0, 1], in_=L[:, 10, :]), reads=[b_L, b_PW], writes=[b_PW])
                for j in range(1, 12):
                    def pwop(fn):
                        P.emit("dve", fn, reads=[b_PW, b_L], writes=[b_PW, b_L])
                    pwop(lambda e, L=L, ld=ld, j=j: e.tensor_tensor(out=L[:, 13, :], in0=PW[:, ld, :, j - 1, 0], in1=PW[:, ld, :, j - 1, 0], op=ALU.mult))
                    pwop(lambda e, L=L, ld=ld, j=j: e.tensor_tensor(out=L[:, 15, :], in0=PW[:, ld, :, j - 1, 1], in1=PW[:, ld, :, j - 1, 1], op=ALU.mult))
                    pwop(lambda e, L=L, ld=ld, j=j: e.tensor_tensor(out=PW[:, ld, :, j, 0], in0=L[:, 13, :], in1=L[:, 15, :], op=ALU.subtract))
                    pwop(lambda e, L=L, ld=ld, j=j: e.tensor_tensor(out=L[:, 13, :], in0=PW[:, ld, :, j - 1, 0], in1=PW[:, ld, :, j - 1, 1], op=ALU.mult))
                    pwop(lambda e, L=L, ld=ld, j=j: e.tensor_scalar(out=PW[:, ld, :, j, 1], in0=L[:, 13, :], scalar1=2.0, scalar2=None, op0=ALU.mult))
                P.emit("dve", lambda e, ld=ld: e.tensor_scalar(out=PW[:, ld, :, :, 2], in0=PW[:, ld, :, :, 1], scalar1=-1.0, scalar2=None, op0=ALU.mult),
                       reads=[b_PW], writes=[b_PW])
                Bn, b_Bn = one(es, nc, f"Bn{d}", [128, 2, 32, 16], F32)
                Bb, b_Bb = one(es, nc, f"Bb{d}", [128, 2, 32, 16], F32)
                tmpB, b_tB = one(es, nc, f"tB{d}", [128, 32, 16], F32)
                P.dma("sp", Bn[:, 0, :, :], b_re_i[lb, d].rearrange("(q a) p c -> (a p) q c", a=2), writes=[b_Bn])
                P.dma("sp", Bn[:, 1, :, :], b_im_i[lb, d].rearrange("(q a) p c -> (a p) q c", a=2), writes=[b_Bn], reads=[b_Bn])
                crb = L[:, 11, :].unsqueeze(2).to_broadcast([128, 32, 16])
                cib = L[:, 12, :].unsqueeze(2).to_broadcast([128, 32, 16])

                def bop(fn):
                    P.emit("dve", fn, reads=[b_Bn, b_L, b_Bb, b_tB], writes=[b_Bb, b_tB])
                bop(lambda e, Bn=Bn, Bb=Bb, crb=crb: e.tensor_tensor(out=Bb[:, 0], in0=Bn[:, 0], in1=crb, op=ALU.mult))
                bop(lambda e, Bn=Bn, tmpB=tmpB, cib=cib: e.tensor_tensor(out=tmpB[:], in0=Bn[:, 1], in1=cib, op=ALU.mult))
                bop(lambda e, Bb=Bb, tmpB=tmpB: e.tensor_tensor(out=Bb[:, 0], in0=Bb[:, 0], in1=tmpB[:], op=ALU.subtract))
                bop(lambda e, Bn=Bn, Bb=Bb, crb=crb: e.tensor_tensor(out=Bb[:, 1], in0=Bn[:, 1], in1=crb, op=ALU.mult))
                bop(lambda e, Bn=Bn, tmpB=tmpB, cib=cib: e.tensor_tensor(out=tmpB[:], in0=Bn[:, 0], in1=cib, op=ALU.mult))
                bop(lambda e, Bb=Bb, tmpB=tmpB: e.tensor_tensor(out=Bb[:, 1], in0=Bb[:, 1], in1=tmpB[:], op=ALU.add))
                Cn, b_Cn = one(es, nc, f"Cn{d}", [128, 2, 8, 64], F32)
                P.dma("sp", Cn[:, 0], c_re_i[lb, d].rearrange("(k q) c p -> (q c) k p", k=8), writes=[b_Cn])
                P.dma("sp", Cn[:, 1], c_im_i[lb, d].rearrange("(k q) c p -> (q c) k p", k=8), writes=[b_Cn], reads=[b_Cn])
                inX, b_inX = one(es, nc, f"inX{d}", [128, 32, 128], F32)
                stg = Rot(es, nc, f"stg{d}", 2, [128, 4, 128], BF16)
                for which in range(4):
                    ri = which % 2
                    P.emit("pool", lambda e, inX=inX: e.memset(inX[:], 0.0), writes=[b_inX])
                    if which < 2:
                        for j4 in range(4):
                            for a in range(2):
                                c0 = j4 * 32 + a * 16
                                P.emit("dve", lambda e, inX=inX, Bb=Bb, j4=j4, a=a, c0=c0, ri=ri: e.tensor_scalar(
                                    out=inX[:, j4::4, c0:c0 + 16], in0=Bb[:, ri, j4::4, :], scalar1=mg2[:, a:a + 1], scalar2=None, op0=ALU.mult),
                                    reads=[b_Bb, b_mg2, b_inX], writes=[b_inX])
                    else:
                        for j4 in range(4):
                            for a in range(2):
                                P.emit("dve", lambda e, inX=inX, Cn=Cn, j4=j4, a=a, ri=ri: e.tensor_scalar(
                                    out=inX[:, j4::4, a * 64:(a + 1) * 64], in0=Cn[:, ri, :, :], scalar1=mq[:, j4 * 2 + a:j4 * 2 + a + 1], scalar2=None, op0=ALU.mult),
                                    reads=[b_Cn, b_mq, b_inX], writes=[b_inX])
                    for p4 in range(8):
                        ps, psb = psm.next()
                        for j in range(4):
                            pp = p4 * 4 + j
                            P.emit("pe", lambda e, ps=ps, j=j, pp=pp, inX=inX: e.matmul(ps[:, j * 128:(j + 1) * 128], lhsT=inX[:, pp, :], rhs=ident_f[:], start=True, stop=True),
                                   reads=[b_inX, b_idf], writes=[psb])
                        st, stb = stg.next()
                        sc = -1.0 if which == 3 else 1.0
                        P.emit("act", lambda e, st=st, ps=ps, sc=sc: e.activation(out=st[:], in_=ps[:].rearrange("p (j c) -> p j c", j=4), func=AF.Copy, scale=sc),
                               reads=[psb], writes=[stb])
                        P.dma("pool", TAB[lb, d, p4 * 4:(p4 + 1) * 4, :, which, :].rearrange("j p c -> p j c"), st[:], reads=[stb], writes=[bTAB])
            P.flush()

    def ssm_layer(l, lb):
        ssm_prep(lb)
        for s in range(NSEG):
            with ExitStack() as es:
                hm, b_hm = one(es, nc, "hm", [128, 8, SEG], BF16)
                norm_phase(es, l, s, hm, b_hm)
                w32 = Rot(es, nc, "sw32", 2, [128, 8, 128], F32)
                wbr = Rot(es, nc, "swb", 2, [128, 8, 128], BF16)
                ps_r = Rot(es, nc, "sps", 3, [128, 512], F32, psum=True)
                ub_r = Rot(es, nc, "sub", 2, [128, 512], BF16)
                uf_r = Rot(es, nc, "suf", 3, [128, 512], F32)
                for slab in range(16):
                    col0 = slab * 128
                    f = slab % 8
                    wt, wtb = w32.next()
                    P.dma("sp", wt[:], ssm_w_in[lb:lb + 1, :, col0:col0 + 128].rearrange("o (k p) j -> p (o k) j", p=128), writes=[wtb])
                    wb, wbb = wbr.next()
                    P.emit("pool", lambda e, wb=wb, wt=wt: e.tensor_copy(out=wb[:], in_=wt[:]), reads=[wtb], writes=[wbb])
                    for tb in range(SEG // 512):
                        t0 = s * SEG + tb * 512
                        ps, psb = ps_r.next()
                        for k in range(8):
                            P.emit("pe", lambda e, ps=ps, k=k, tb=tb, wb=wb: e.matmul(ps[:], lhsT=wb[:, k, :], rhs=hm[:, k, tb * 512:(tb + 1) * 512], start=(k == 0), stop=(k == 7)),
                                   reads=[b_hm, wbb], writes=[psb])
                        uf, ufb = uf_r.next()
                        if slab < 8:
                            ub, ubb = ub_r.next()
                            P.emit("act", lambda e, ub=ub, ps=ps: e.activation(out=ub[:], in_=ps[:], func=AF.Copy), reads=[psb], writes=[ubb])
                            P.dma("pool", UT[f, :, t0:t0 + 512], ub[:], reads=[ubb], writes=[bUT])
                            P.emit("dve", lambda e, uf=uf, ps=ps, f=f: e.tensor_scalar(out=uf[:], in0=ps[:], scalar1=dvec[:, lb, f:f + 1], scalar2=None, op0=ALU.mult),
                                   reads=[psb, b_dvec, ubb], writes=[ufb])
                            P.dma("pool", Y0[f, :, t0:t0 + 512], uf[:], reads=[ufb], writes=[bY0])
                        else:
                            P.emit("act", lambda e, uf=uf, ps=ps: e.activation(out=uf[:], in_=ps[:], func=AF.Silu), reads=[psb], writes=[ufb])
                            P.dma("pool", ZS[f, :, t0:t0 + 512], uf[:], reads=[ufb], writes=[bZS])
                P.flush()

        with ExitStack() as es:
            ut_r = Rot(es, nc, "qu", 1, [128, N], BF16)
            ya_r = Rot(es, nc, "qy", 1, [128, N], F32)
            tb_r = Rot(es, nc, "qt", 2, [128, 4, 128], BF16)
            x_r = Rot(es, nc, "qx", 2, [128, 2, SEG], F32)
            xb_r = Rot(es, nc, "qxb", 1, [128, 2, SEG], BF16)
            pb_r = Rot(es, nc, "qpb", 4, [128, 512], F32, psum=True)
            pc_r = Rot(es, nc, "qpc", 3, [128, 512], F32, psum=True)
            fin, b_fin = one(es, nc, "qfin", [128, 2], F32)
            inj, b_inj = one(es, nc, "qinj", [128, 4], F32)
            g1_r = Rot(es, nc, "qg1", 2, [128, 1024], F32)
            g2_r = Rot(es, nc, "qg2", 2, [128, 1024], F32)
            gb_r = Rot(es, nc, "qgb", 2, [128, 1024], BF16)
            LV = 12
            for kt in range(8):
                ut, ub = ut_r.next()
                P.dma("sp", ut[:], UT[kt, :, :], reads=[bUT], writes=[ub])
                ya, yab = ya_r.next()
                P.dma("sp", ya[:], Y0[kt, :, :], reads=[bY0], writes=[yab])
                for j4 in range(4):
                    pp = kt * 4 + j4
                    for d in DBG.get("dirs", (0, 1)):
                        ld = lb * 2 + d
                        tbt, tbb = tb_r.next()
                        P.dma("sp", tbt[:], TAB[lb, d, pp, :, :, :], reads=[bTAB], writes=[tbb])
                        order = (0, 1) if d == 0 else (1, 0)
                        for oi, s in enumerate(order):
                            X, Xb_ = x_r.next()
                            for blk in range(SEG // 512):
                                c0 = s * SEG + blk * 512
                                for ri in range(2):
                                    ps, psb = pb_r.next()
                                    P.emit("pe", lambda e, ps=ps, tbt=tbt, ut=ut, ri=ri, c0=c0: e.matmul(ps[:], lhsT=tbt[:, ri, :], rhs=ut[:, c0:c0 + 512], start=True, stop=True),
                                           reads=[tbb, ub], writes=[psb])
                                    P.emit("act", lambda e, ps=ps, X=X, ri=ri, blk=blk: e.activation(out=X[:, ri, blk * 512:(blk + 1) * 512], in_=ps[:], func=AF.Copy),
                                           reads=[psb], writes=[Xb_])
                            pw = lambda j, c, ld=ld, pp=pp: PW[:, ld, pp, j, c:c + 1]
                            if oi == 1:
                                tcol = 0 if d == 0 else SEG - 1

                                def io(fn):
                                    P.emit("dve", fn, reads=[b_fin, b_inj, b_PW, b_flag, Xb_], writes=[b_inj, Xb_])
                                io(lambda e, pw=pw: e.tensor_scalar(out=inj[:, 0:1], in0=fin[:, 0:1], scalar1=pw(0, 0), scalar2=None, op0=ALU.mult))
                                io(lambda e, pw=pw: e.scalar_tensor_tensor(out=inj[:, 0:1], in0=fin[:, 1:2], scalar=pw(0, 2), in1=inj[:, 0:1], op0=ALU.mult, op1=ALU.add))
                                io(lambda e, pw=pw: e.tensor_scalar(out=inj[:, 1:2], in0=fin[:, 1:2], scalar1=pw(0, 0), scalar2=None, op0=ALU.mult))
                                io(lambda e, pw=pw: e.scalar_tensor_tensor(out=inj[:, 1:2], in0=fin[:, 0:1], scalar=pw(0, 1), in1=inj[:, 1:2], op0=ALU.mult, op1=ALU.add))
                                for ri in range(2):
                                    io(lambda e, X=X, ri=ri, tcol=tcol: e.scalar_tensor_tensor(out=X[:, ri, tcol:tcol + 1], in0=inj[:, ri:ri + 1], scalar=flag[:, 0:1],
                                                                                      in1=X[:, ri, tcol:tcol + 1], op0=ALU.mult, op1=ALU.add))

                            def cstep(dsl, ssl, j, X=X, pw=pw, Xb_=Xb_):
                                def so(fn):
                                    P.emit("dve", fn, reads=[Xb_, b_PW], writes=[Xb_])
                                so(lambda e: e.scalar_tensor_tensor(out=X[:, 0, dsl], in0=X[:, 0, ssl], scalar=pw(j, 0), in1=X[:, 0, dsl], op0=ALU.mult, op1=ALU.add))
                                so(lambda e: e.scalar_tensor_tensor(out=X[:, 0, dsl], in0=X[:, 1, ssl], scalar=pw(j, 2), in1=X[:, 0, dsl], op0=ALU.mult, op1=ALU.add))
                                so(lambda e: e.scalar_tensor_tensor(out=X[:, 1, dsl], in0=X[:, 1, ssl], scalar=pw(j, 0), in1=X[:, 1, dsl], op0=ALU.mult, op1=ALU.add))
                                so(lambda e: e.scalar_tensor_tensor(out=X[:, 1, dsl], in0=X[:, 0, ssl], scalar=pw(j, 1), in1=X[:, 1, dsl], op0=ALU.mult, op1=ALU.add))
                            for j in range(0 if not DBG.get("noscan") else LV, LV):
                                S_, h = 2 ** (j + 1), 2 ** j
                                if d == 0:
                                    cstep(slice(S_ - 1, SEG, S_), slice(h - 1, SEG, S_), j)
                                else:
                                    cstep(slice(0, SEG, S_), slice(h, SEG, S_), j)
                            for j in range(LV - 2 if not DBG.get("noscan") else -1, -1, -1):
                                S_, h = 2 ** (j + 1), 2 ** j
                                cnt = SEG // S_ - 1
                                if d == 0:
                                    cstep(slice(S_ + h - 1, S_ + h - 1 + (cnt - 1) * S_ + 1, S_), slice(S_ - 1, S_ - 1 + (cnt - 1) * S_ + 1, S_), j)
                                else:
                                    cstep(slice(h, h + (cnt - 1) * S_ + 1, S_), slice(S_, S_ + (cnt - 1) * S_ + 1, S_), j)
                            if oi == 0:
                                fcol = SEG - 1 if d == 0 else 0
                                P.emit("dve", lambda e, X=X, fcol=fcol: e.tensor_copy(out=fin[:], in_=X[:, :, fcol]), reads=[Xb_, b_fin], writes=[b_fin])
                            Xh, Xhb = xb_r.next()
                            P.emit("act", lambda e, Xh=Xh, X=X: e.activation(out=Xh[:], in_=X[:], func=AF.Copy), reads=[Xb_], writes=[Xhb])
                            for blk in range(SEG // 512):
                                c0 = s * SEG + blk * 512
                                ps, psb = pc_r.next()
                                for ri in range(2):
                                    P.emit("pe", lambda e, ps=ps, tbt=tbt, Xh=Xh, ri=ri, blk=blk: e.matmul(ps[:], lhsT=tbt[:, 2 + ri, :], rhs=Xh[:, ri, blk * 512:(blk + 1) * 512], start=(ri == 0), stop=(ri == 1)),
                                           reads=[tbb, Xhb], writes=[psb])
                                P.emit("dve", lambda e, ps=ps, ya=ya, c0=c0: e.tensor_tensor(out=ya[:, c0:c0 + 512], in0=ps[:], in1=ya[:, c0:c0 + 512], op=ALU.add),
                                       reads=[psb, yab], writes=[yab])
                for cb in range(N // 1024):
                    sl = slice(cb * 1024, (cb + 1) * 1024)
                    if DBG.get("rawY"):
                        P.dma("pool", GT[kt, :, sl], ya[:, sl], reads=[yab], writes=[bGT])
                        continue
                    g1, g1b = g1_r.next()
                    g2, g2b = g2_r.next()
                    P.emit("act", lambda e, g1=g1, ya=ya, sl=sl: e.activation(out=g1[:], in_=ya[:, sl], func=AF.Square), reads=[yab], writes=[g1b])
                    P.emit("dve", lambda e, g1=g1: e.tensor_scalar(out=g1[:], in0=g1[:], scalar1=0.044715, scalar2=1.0, op0=ALU.mult, op1=ALU.add), reads=[g1b], writes=[g1b])
                    P.emit("dve", lambda e, g1=g1, ya=ya, sl=sl: e.tensor_tensor(out=g1[:], in0=g1[:], in1=ya[:, sl], op=ALU.mult), reads=[g1b, yab], writes=[g1b])
                    P.emit("act", lambda e, g1=g1: e.activation(out=g1[:], in_=g1[:], func=AF.Sigmoid, scale=1.5957691216057308), reads=[g1b], writes=[g1b])
                    P.emit("dve", lambda e, g1=g1, g2=g2, ya=ya, sl=sl: e.tensor_tensor(out=g2[:], in0=g1[:], in1=ya[:, sl], op=ALU.mult), reads=[g1b, yab], writes=[g2b])
                    gb, gbb = gb_r.next()
                    P.emit("act", lambda e, gb=gb, g2=g2: e.activation(out=gb[:], in_=g2[:], func=AF.Copy), reads=[g2b], writes=[gbb])
                    P.dma("pool", GT[kt, :, sl], g2[:], reads=[g2b], writes=[bGT])
                    P.dma("pool", GB[kt, :, sl], gb[:], reads=[gbb], writes=[bGB])
            P.flush()

        if DBG["stop"] == "scan":
            return
        with ExitStack() as es:
            wb, b_wb = one(es, nc, "gw", [128, 8, D], BF16)
            r32 = Rot(es, nc, "gw32", 2, [128, D], F32)
            load_w_bf16(es, "gw", ssm_w_glu[lb:lb + 1, :, :], D, wb, b_wb, r32)
            gb_r = Rot(es, nc, "gg", 2, [128, 8, 512], BF16)
            g32_r = Rot(es, nc, "gg32", 2, [128, 8, 512], F32)
            z_r = Rot(es, nc, "gz", 2, [128, 8, 512], F32)
            ps_r = Rot(es, nc, "gps", 3, [128, 512], F32, psum=True)
            sg_r = Rot(es, nc, "gsg", 3, [128, 512], F32)
            yo_r = Rot(es, nc, "gyo", 2, [128, 8, 512], BF16)
            for tb in range(N // 512):
                t0 = tb * 512
                gt, gtb = gb_r.next()
                P.dma("sp", gt[:], GB[:, :, t0:t0 + 512].rearrange("f p t -> p f t"), reads=[bGB], writes=[gtb])
                g32, g32b = g32_r.next()
                P.dma("sp", g32[:], GT[:, :, t0:t0 + 512].rearrange("f p t -> p f t"), reads=[bGT], writes=[g32b])
                zt, ztb = z_r.next()
                P.dma("sp", zt[:], ZS[:, :, t0:t0 + 512].rearrange("f p t -> p f t"), reads=[bZS], writes=[ztb])
                yo, yob = yo_r.next()
                for f in range(8):
                    ps, psb = ps_r.next()
                    for k in range(8):
                        P.emit("pe", lambda e, ps=ps, k=k, f=f, gt=gt: e.matmul(ps[:], lhsT=wb[:, k, f * 128:(f + 1) * 128], rhs=gt[:, k, :], start=(k == 0), stop=(k == 7)),
                               reads=[b_wb, gtb], writes=[psb])
                    sg, sgb = sg_r.next()
                    P.emit("act", lambda e, sg=sg, ps=ps: e.activation(out=sg[:], in_=ps[:], func=AF.Sigmoid), reads=[psb], writes=[sgb])
                    P.emit("dve", lambda e, sg=sg, g32=g32, f=f: e.tensor_tensor(out=sg[:], in0=sg[:], in1=g32[:, f, :], op=ALU.mult), reads=[sgb, g32b], writes=[sgb])
                    P.emit("dve", lambda e, sg=sg, zt=zt, yo=yo, f=f: e.tensor_tensor(out=yo[:, f, :], in0=sg[:], in1=zt[:, f, :], op=ALU.mult), reads=[sgb, ztb], writes=[yob])
                P.dma("pool", YT[:, :, t0:t0 + 512].rearrange("f p t -> p f t"), yo[:], reads=[yob], writes=[bYT])
            P.flush()
        outproj_phase(l, ssm_w_out[lb:lb + 1, :, :])

    for l in layers:
        if l % 2 == 0:
            attn_layer(l, l // 2)
        else:
            ssm_layer(l, l // 2)

    with ExitStack() as es:
        xb_r = Rot(es, nc, "fx", 2, [128, 8, 512], F32)
        sq_r = Rot(es, nc, "fsq", 2, [128, 512], F32)
        ps_r = Rot(es, nc, "fps", 2, [128, 512], F32, psum=True)
        rs_r = Rot(es, nc, "frs", 2, [128, 512], F32)
        pt_r = Rot(es, nc, "fpt", 4, [128, 512], F32, psum=True)
        os_r = Rot(es, nc, "fos", 2, [128, D], F32)
        for tb in range(N // 512):
            t0 = tb * 512
            xt, xb = xb_r.next()
            P.dma("sp", xt[:], XT[:, :, t0:t0 + 512].rearrange("f p t -> p f t"), reads=[bX], writes=[xb])
            ps, psb = ps_r.next()
            for f in range(8):
                sq, sqb = sq_r.next()
                P.emit("act", lambda e, sq=sq, xt=xt, f=f: e.activation(out=sq[:], in_=xt[:, f, :], func=AF.Square), reads=[xb], writes=[sqb])
                P.emit("pe", lambda e, ps=ps, sq=sq, f=f: e.matmul(ps[:], lhsT=ones_f[:], rhs=sq[:], start=(f == 0), stop=(f == 7)), reads=[sqb, b_onf], writes=[psb])
            rs, rsb = rs_r.next()
            P.emit("act", lambda e, rs=rs, ps=ps: e.activation(out=rs[:], in_=ps[:], func=AF.Sqrt, bias=1e-6, scale=1.0 / D), reads=[psb], writes=[rsb])
            P.emit("dve", lambda e, rs=rs: e.reciprocal(out=rs[:], in_=rs[:]), reads=[rsb], writes=[rsb])
            for f in range(8):
                P.emit("dve", lambda e, xt=xt, rs=rs, f=f: e.tensor_tensor(out=xt[:, f, :], in0=xt[:, f, :], in1=rs[:], op=ALU.mult), reads=[xb, rsb], writes=[xb])
                P.emit("dve", lambda e, xt=xt, f=f: e.tensor_scalar(out=xt[:, f, :], in0=xt[:, f, :], scalar1=fing[:, f:f + 1], scalar2=None, op0=ALU.mult), reads=[xb, b_fing], writes=[xb])
            for sub in range(4):
                ot, otb = os_r.next()
                for h in range(2):
                    pt, ptb = pt_r.next()
                    for j in range(4):
                        f = h * 4 + j
                        P.emit("pe", lambda e, pt=pt, xt=xt, f=f, j=j, sub=sub: e.matmul(pt[:, j * 128:(j + 1) * 128], lhsT=xt[:, f, sub * 128:(sub + 1) * 128], rhs=ident_f[:], start=True, stop=True),
                               reads=[xb, b_idf], writes=[ptb])
                    P.emit("act", lambda e, pt=pt, ot=ot, h=h: e.activation(out=ot[:, h * 512:(h + 1) * 512], in_=pt[:], func=AF.Copy), reads=[ptb], writes=[otb])
                r0 = t0 + sub * 128
                P.dma("pool", y_out[r0:r0 + 128, :], ot[:], reads=[otb], writes=[bOUT])
        P.flush()
    top.close()
    return nc, P


_CACHE = {}


def kernel(**inputs):
    x_prompt = np.asarray(inputs["x_prompt"], np.float32)
    x_sample = np.asarray(inputs["x_sample"], np.float32)
    c_prompt = np.asarray(inputs["c_prompt"], np.float32)
    c_sample = np.asarray(inputs["c_sample"], np.float32)
    if "nc" not in _CACHE:
        _CACHE["nc"] = build_program()[0]
    nc = _CACHE["nc"]
    shared = {k: np.ascontiguousarray(np.asarray(inputs[k], np.float32)) for k in (
        "norm_g", "ada_w", "ada_b", "attn_w_in", "attn_w_out", "ssm_w_in", "ssm_lam_re", "ssm_lam_im",
        "ssm_log_dt", "ssm_b_re", "ssm_b_im", "ssm_c_re", "ssm_c_im", "ssm_d", "ssm_w_glu", "ssm_w_out")}
    shared["final_norm_g"] = np.ascontiguousarray(np.asarray(inputs["final_norm_g"], np.float32).reshape(1, D))
    in_maps = []
    for core in range(8):
        c = core % 4
        m = dict(shared)
        if c < 2:
            m["x_in"] = np.ascontiguousarray(x_prompt[c])
            m["c_in"] = np.ascontiguousarray(np.stack([c_prompt[c], c_prompt[c]]))
            m["pos_in"] = np.arange(N, dtype=np.float32).reshape(1, N)
            fl = np.zeros((128, 2), np.float32)
            fl[:, 0] = 1.0
        else:
            a = 2 * (c - 2)
            m["x_in"] = np.ascontiguousarray(np.concatenate([x_sample[a], x_sample[a + 1]], axis=0))
            m["c_in"] = np.ascontiguousarray(np.stack([c_sample[a], c_sample[a + 1]]))
            m["pos_in"] = np.concatenate([np.arange(SEG), np.arange(SEG)]).astype(np.float32).reshape(1, N)
            fl = np.zeros((128, 2), np.float32)
            fl[:, 1] = NEGM
        m["flag_in"] = fl
        in_maps.append(m)
    res = run_bass_kernel_spmd(nc, in_maps, core_ids=list(range(8)))
    outs = [np.asarray(r["y_out"], np.float32) for r in res.results]
    y_prompt = np.stack([outs[0], outs[1]]).reshape(2, N, D)
    y_sample = np.stack([outs[2][:SEG], outs[2][SEG:], outs[3][:SEG], outs[3][SEG:]])
    return (y_prompt, y_sample)
```

```python
import math
import numpy as np
from contextlib import ExitStack
import concourse.bass as bass
import concourse.mybir as mybir
from concourse.bass_utils import run_bass_kernel_spmd

F32 = mybir.dt.float32
BF16 = mybir.dt.bfloat16
I32 = mybir.dt.int32
AF = mybir.ActivationFunctionType
ALU = mybir.AluOpType

D = 1024
NSEG = 2
SEG = 4096
N = NSEG * SEG
NT = N // 128
DEPTH = 4
NEGM = -30000.0
TWO_PI = 2.0 * math.pi
C1 = 6.28125
C2 = TWO_PI - C1


class Buf:
    __slots__ = ("w", "r")

    def __init__(self):
        self.w = None
        self.r = {}


class Prog:
    ENG = ("pe", "act", "dve", "pool", "sp")
    NDSEM = 10

    def __init__(self):
        self.nc = bass.Bass("TRN2", target_bir_lowering=False)
        nc = self.nc
        self.sem = {e: nc.alloc_semaphore("sem_" + e) for e in self.ENG}
        self.cnt = {e: 0 for e in self.ENG}
        self.seen = {e: {} for e in self.ENG}
        self.q = {e: [] for e in self.ENG}
        self.dsem, self.dval, self.drr = {}, {}, {}
        for qn in ("sp", "pool", "act"):
            self.dsem[qn] = [nc.alloc_semaphore(f"dsem_{qn}{i}") for i in range(self.NDSEM)]
            self.dval[qn] = [0] * self.NDSEM
            self.drr[qn] = 0
        self.ninst = 0

    def _deps(self, e, reads, writes, extra=()):
        deps = {}

        def need(key, sem, val):
            if key == "pe" and e == "pe":
                return
            cur = deps.get(key)
            if cur is None or cur[1] < val:
                deps[key] = (sem, val)

        for b in reads:
            if b.w is not None:
                need(*b.w)
        for b in writes:
            if b.w is not None:
                need(*b.w)
            for k, (s, v) in b.r.items():
                need(k, s, v)
        for ev in extra:
            need(*ev)
        seen = self.seen[e]
        for key, (sem, val) in deps.items():
            if seen.get(key, 0) < val:
                self.q[e].append(("wait", sem, val))
                seen[key] = val

    def _mark(self, ev, reads, writes):
        key, sem, val = ev
        for b in reads:
            b.r[key] = (sem, val)
        for b in writes:
            b.w = ev
            b.r = {}

    def emit(self, e, fn, reads=(), writes=()):
        self._deps(e, reads, writes)
        self.cnt[e] += 1
        self.q[e].append(("inst", fn, self.sem[e], 1))
        self._mark((e, self.sem[e], self.cnt[e]), reads, writes)
        self.ninst += 1

    def dma(self, qn, out, in_, reads=(), writes=(), slow=False):
        k = self.drr[qn]
        self.drr[qn] = (k + 1) % self.NDSEM
        sem = self.dsem[qn][k]
        key = f"d{qn}{k}"
        prev = self.dval[qn][k]
        extra = ((key, sem, prev),) if prev else ()
        self._deps(qn, reads, writes, extra)
        val = prev + 16
        self.dval[qn][k] = val
        kw = {"allow_slow_non_contiguous": True} if slow else {}
        self.q[qn].append(("inst", lambda eng, o=out, i=in_, kw=kw: eng.dma_start(out=o, in_=i, **kw), sem, 16))
        self._mark((key, sem, val), reads, writes)
        self.ninst += 1

    def flush(self):
        for qn in self.dsem:
            for k in range(self.NDSEM):
                if self.dval[qn][k]:
                    self.q["sp"].append(("wait", self.dsem[qn][k], self.dval[qn][k]))
        for e in self.ENG:
            if e != "sp" and self.cnt[e]:
                self.q["sp"].append(("wait", self.sem[e], self.cnt[e]))
        nc = self.nc
        with nc.Block() as block:
            def replay(e):
                items = self.q[e]

                def body(eng):
                    for it in items:
                        if it[0] == "wait":
                            eng.wait_ge(it[1], it[2])
                        else:
                            it[1](eng).then_inc(it[2], it[3])
                return body
            block.tensor(replay("pe"))
            block.scalar(replay("act"))
            block.vector(replay("dve"))
            block.gpsimd(replay("pool"))
            block.sync(replay("sp"))
        self.q = {e: [] for e in self.ENG}


DBG = {"stop": None}
_UID = [0]


def _uid(name):
    _UID[0] += 1
    return f"{name}_{_UID[0]}"


class Rot:
    def __init__(self, es, nc, name, n, shape, dt, psum=False):
        mk = nc.psum_tensor if psum else nc.sbuf_tensor
        self.t = [es.enter_context(mk(_uid(name), list(shape), dt)) for i in range(n)]
        self.b = [Buf() for _ in range(n)]
        self.i = 0

    def next(self):
        k = self.i
        self.i = (k + 1) % len(self.t)
        return self.t[k], self.b[k]


def one(es, nc, name, shape, dt, psum=False):
    mk = nc.psum_tensor if psum else nc.sbuf_tensor
    return es.enter_context(mk(_uid(name), list(shape), dt)), Buf()


def build_program(layers=None):
    if layers is None:
        layers = list(range(DEPTH))
    P = Prog()
    nc = P.nc

    def din(name, shape, dt=F32):
        return nc.dram_tensor(name, list(shape), dt, kind="ExternalInput").ap()

    def dscr(name, shape, dt):
        return nc.dram_tensor(name, list(shape), dt, kind="Internal").ap()

    x_in = din("x_in", [N, D])
    c_in = din("c_in", [NSEG, D])
    pos_in = din("pos_in", [1, N])
    flag_in = din("flag_in", [128, 2])
    norm_g = din("norm_g", [DEPTH, D])
    ada_w = din("ada_w", [DEPTH, D, 3 * D])
    ada_b = din("ada_b", [DEPTH, 3 * D])
    attn_w_in = din("attn_w_in", [2, D, 10 * D])
    attn_w_out = din("attn_w_out", [2, D, D])
    ssm_w_in = din("ssm_w_in", [2, D, 2 * D])
    lam_re_i = din("ssm_lam_re", [2, 2, 64, 64])
    lam_im_i = din("ssm_lam_im", [2, 2, 64, 64])
    log_dt_i = din("ssm_log_dt", [2, 2, 64])
    b_re_i = din("ssm_b_re", [2, 2, 64, 64, 16])
    b_im_i = din("ssm_b_im", [2, 2, 64, 64, 16])
    c_re_i = din("ssm_c_re", [2, 2, 64, 16, 64])
    c_im_i = din("ssm_c_im", [2, 2, 64, 16, 64])
    ssm_d = din("ssm_d", [2, D])
    ssm_w_glu = din("ssm_w_glu", [2, D, D])
    ssm_w_out = din("ssm_w_out", [2, D, D])
    fin_g = din("final_norm_g", [1, D])
    y_out = nc.dram_tensor("y_out", [N, D], F32, kind="ExternalOutput").ap()

    XT = dscr("XT", [8, 128, N], F32)
    ROPE = dscr("ROPE", [2, 128, N], F32)
    QT = dscr("QT", [3, 8, 128, N], BF16)
    KT = dscr("KT", [3, 8, 128, N], BF16)
    VV = dscr("VV", [3, N, D], BF16)
    ZS = dscr("ZS", [8, 128, N], F32)
    YT = dscr("YT", [8, 128, N], BF16)
    UT = dscr("UT", [8, 128, N], BF16)
    Y0 = dscr("Y0", [8, 128, N], F32)
    GT = nc.dram_tensor("GT", [8, 128, N], F32, kind="ExternalOutput").ap() if DBG.get("dumpGT") else dscr("GT", [8, 128, N], F32)
    GB = dscr("GB", [8, 128, N], BF16)
    TAB = dscr("TAB", [2, 2, 32, 128, 4, 128], BF16)
    bX, bROPE, bQT, bKT, bVV, bZS, bYT, bUT, bY0, bGT, bGB, bTAB, bOUT = [Buf() for _ in range(13)]

    top = ExitStack()
    ident_f, b_idf = one(top, nc, "ident_f", [128, 128], F32)
    ones_f, b_onf = one(top, nc, "ones_f", [128, 128], F32)
    ident_b, b_idb = one(top, nc, "ident_b", [128, 128], BF16)
    perm_b, b_perm = one(top, nc, "perm_b", [128, 128], BF16)
    masks, b_masks = one(top, nc, "masks", [128, 9, 128], BF16)
    Eo, b_Eo = one(top, nc, "Eo", [128, 2, 128], BF16)
    flag, b_flag = one(top, nc, "flag", [128, 2], F32)
    modA, b_mod = one(top, nc, "modA", [128, DEPTH, 3, 8, NSEG], F32)
    fing, b_fing = one(top, nc, "fing", [128, 8], F32)
    dvec, b_dvec = one(top, nc, "dvec", [128, 2, 8], F32)
    PW, b_PW = one(top, nc, "PW", [128, 4, 32, 12, 3], F32)

    with ExitStack() as es:
        P.emit("pool", lambda e: e.memset(ident_f[:], 0.0), writes=[b_idf])
        P.emit("pool", lambda e: e.affine_select(out=ident_f[:], in_=ident_f[:], pattern=[[-1, 128]],
                                                 compare_op=ALU.not_equal, fill=1.0, base=0, channel_multiplier=1),
               reads=[b_idf], writes=[b_idf])
        P.emit("dve", lambda e: e.tensor_copy(out=ident_b[:], in_=ident_f[:]), reads=[b_idf], writes=[b_idb])
        P.emit("dve", lambda e: e.memset(ones_f[:], 1.0), writes=[b_onf])
        pf, b_pf = one(es, nc, "pf", [128, 128], F32)
        P.emit("pool", lambda e: e.memset(pf[:], 0.0), writes=[b_pf])
        for blk in range(2):
            for half in range(2):
                c0 = blk * 64 + half * 32
                base = -(c0 + (32 if half == 0 else -32))
                P.emit("pool", lambda e, c0=c0, base=base: e.affine_select(
                    out=pf[:, c0:c0 + 32], in_=pf[:, c0:c0 + 32], pattern=[[-1, 32]],
                    compare_op=ALU.not_equal, fill=1.0, base=base, channel_multiplier=1),
                    reads=[b_pf], writes=[b_pf])
        P.emit("dve", lambda e: e.tensor_copy(out=perm_b[:], in_=pf[:]), reads=[b_pf], writes=[b_perm])
        P.emit("dve", lambda e: e.memset(Eo[:], 0.0), writes=[b_Eo])
        P.emit("dve", lambda e: e.memset(Eo[:, 0, 0:64], 1.0), writes=[b_Eo], reads=[b_Eo])
        P.emit("dve", lambda e: e.memset(Eo[:, 1, 64:128], 1.0), writes=[b_Eo], reads=[b_Eo])
        P.dma("sp", flag[:], flag_in, writes=[b_flag])

        Dm, b_Dm = one(es, nc, "Dm", [128, 128], I32)
        Df, b_Df = one(es, nc, "Df", [128, 128], F32)
        t_i, b_ti = one(es, nc, "t_i", [128, 128], I32)
        mm, b_mm = one(es, nc, "mm", [128, 6, 128], F32)
        mk, b_mk = one(es, nc, "mk", [128, 9, 128], F32)
        P.emit("pool", lambda e: e.iota(Dm[:], pattern=[[-1, 128]], base=128, channel_multiplier=1), writes=[b_Dm])
        P.emit("dve", lambda e: e.tensor_copy(out=Df[:], in_=Dm[:]), reads=[b_Dm], writes=[b_Df])
        for idx, msk in ((0, 15), (1, 3)):
            P.emit("dve", lambda e, msk=msk: e.tensor_single_scalar(out=t_i[:], in_=Dm[:], scalar=msk, op=ALU.bitwise_and),
                   reads=[b_Dm], writes=[b_ti])
            P.emit("dve", lambda e, idx=idx: e.tensor_single_scalar(out=mm[:, idx, :], in_=t_i[:], scalar=0, op=ALU.is_equal),
                   reads=[b_ti], writes=[b_mm])
        P.emit("dve", lambda e: e.tensor_single_scalar(out=mm[:, 2, :], in_=Df[:], scalar=128.0, op=ALU.is_ge), reads=[b_Df], writes=[b_mm])
        P.emit("dve", lambda e: e.tensor_single_scalar(out=mm[:, 3, :], in_=Df[:], scalar=128.0, op=ALU.is_le), reads=[b_Df], writes=[b_mm])
        P.emit("dve", lambda e: e.tensor_single_scalar(out=mk[:, 0, :], in_=Df[:], scalar=192.0, op=ALU.is_ge), reads=[b_Df], writes=[b_mk])
        P.emit("dve", lambda e: e.tensor_scalar(out=mk[:, 1, :], in0=Df[:], scalar1=64.0, scalar2=None, op0=ALU.is_ge), reads=[b_Df], writes=[b_mk])
        P.emit("dve", lambda e: e.tensor_single_scalar(out=mm[:, 4, :], in_=Df[:], scalar=192.0, op=ALU.is_le), reads=[b_Df], writes=[b_mm])
        P.emit("dve", lambda e: e.tensor_tensor(out=mk[:, 1, :], in0=mk[:, 1, :], in1=mm[:, 4, :], op=ALU.mult), reads=[b_mk, b_mm], writes=[b_mk])
        P.emit("dve", lambda e: e.tensor_single_scalar(out=mk[:, 2, :], in_=Df[:], scalar=64.0, op=ALU.is_le), reads=[b_Df], writes=[b_mk])
        for gi, mi in ((1, 1), (2, 0)):
            o = 3 * gi
            P.emit("dve", lambda e, o=o, mi=mi: e.tensor_tensor(out=mk[:, o, :], in0=mm[:, mi, :], in1=mm[:, 2, :], op=ALU.mult), reads=[b_mm], writes=[b_mk])
            P.emit("dve", lambda e, o=o, mi=mi: e.tensor_copy(out=mk[:, o + 1, :], in_=mm[:, mi, :]), reads=[b_mm], writes=[b_mk])
            P.emit("dve", lambda e, o=o, mi=mi: e.tensor_tensor(out=mk[:, o + 2, :], in0=mm[:, mi, :], in1=mm[:, 3, :], op=ALU.mult), reads=[b_mm], writes=[b_mk])
        P.emit("dve", lambda e: e.tensor_scalar(out=masks[:], in0=mk[:], scalar1=-1.0, scalar2=-NEGM, op0=ALU.add, op1=ALU.mult),
               reads=[b_mk], writes=[b_masks])

        invf, b_invf = one(es, nc, "invf", [128, 1], F32)
        pid, b_pid = one(es, nc, "pid", [128, 1], I32)
        pidf, b_pidf = one(es, nc, "pidf", [128, 1], F32)
        sgn, b_sgn = one(es, nc, "sgn", [128, 1], F32)
        P.emit("pool", lambda e: e.iota(pid[:], pattern=[[0, 1]], base=0, channel_multiplier=1), writes=[b_pid])
        P.emit("dve", lambda e: e.tensor_single_scalar(out=pid[:], in_=pid[:], scalar=31, op=ALU.bitwise_and), reads=[b_pid], writes=[b_pid])
        P.emit("dve", lambda e: e.tensor_copy(out=pidf[:], in_=pid[:]), reads=[b_pid], writes=[b_pidf])
        P.emit("act", lambda e: e.activation(out=invf[:], in_=pidf[:], func=AF.Exp, scale=-math.log(10000.0) / 32.0),
               reads=[b_pidf], writes=[b_invf])
        P.emit("dve", lambda e: e.memset(sgn[:], 1.0), writes=[b_sgn])
        for hb in range(2):
            P.emit("dve", lambda e, hb=hb: e.memset(sgn[hb * 64:hb * 64 + 32, :], -1.0), reads=[b_sgn], writes=[b_sgn])
        CH = 2048
        posb = Rot(es, nc, "posb", 2, [128, CH], F32)
        ang = Rot(es, nc, "ang", 2, [128, CH], F32)
        kf = Rot(es, nc, "kf", 2, [128, CH], F32)
        ki = Rot(es, nc, "ki", 2, [128, CH], I32)
        tro = Rot(es, nc, "tro", 2, [128, 2, CH], F32)
        for cb in range(N // CH):
            sl = slice(cb * CH, (cb + 1) * CH)
            pt, pb_ = posb.next()
            P.dma("sp", pt[:], pos_in[0:1, sl].to_broadcast([128, CH]), writes=[pb_])
            at, ab = ang.next()
            P.emit("dve", lambda e, at=at, pt=pt: e.tensor_scalar(out=at[:], in0=pt[:], scalar1=invf[:, 0:1], scalar2=None, op0=ALU.mult),
                   reads=[pb_, b_invf], writes=[ab])
            kt_, kb_ = kf.next()
            kit, kib = ki.next()
            P.emit("dve", lambda e, at=at, kt_=kt_: e.tensor_scalar(out=kt_[:], in0=at[:], scalar1=1.0 / TWO_PI, scalar2=None, op0=ALU.mult),
                   reads=[ab], writes=[kb_])
            P.emit("dve", lambda e, kit=kit, kt_=kt_: e.tensor_copy(out=kit[:], in_=kt_[:]), reads=[kb_], writes=[kib])
            P.emit("dve", lambda e, kit=kit, kt_=kt_: e.tensor_copy(out=kt_[:], in_=kit[:]), reads=[kib], writes=[kb_])
            P.emit("dve", lambda e, at=at, kt_=kt_: e.scalar_tensor_tensor(out=at[:], in0=kt_[:], scalar=-C1, in1=at[:], op0=ALU.mult, op1=ALU.add),
                   reads=[kb_, ab], writes=[ab])
            P.emit("dve", lambda e, at=at, kt_=kt_: e.scalar_tensor_tensor(out=at[:], in0=kt_[:], scalar=-C2, in1=at[:], op0=ALU.mult, op1=ALU.add),
                   reads=[kb_, ab], writes=[ab])
            tt, tb_ = tro.next()
            P.emit("dve", lambda e, at=at, tt=tt: e.tensor_scalar(out=tt[:, 0, :], in0=at[:], scalar1=math.pi / 2, scalar2=None, op0=ALU.add), reads=[ab], writes=[tb_])
            P.emit("dve", lambda e, at=at, tt=tt: e.tensor_single_scalar(out=tt[:, 1, :], in_=tt[:, 0, :], scalar=math.pi, op=ALU.is_gt), reads=[tb_], writes=[tb_])
            P.emit("dve", lambda e, at=at, tt=tt: e.scalar_tensor_tensor(out=tt[:, 0, :], in0=tt[:, 1, :], scalar=-TWO_PI, in1=tt[:, 0, :], op0=ALU.mult, op1=ALU.add), reads=[tb_], writes=[tb_])
            P.emit("dve", lambda e, at=at, tt=tt: e.tensor_copy(out=tt[:, 1, :], in_=at[:]), reads=[ab, tb_], writes=[tb_])
            P.emit("dve", lambda e, tt=tt: e.tensor_scalar(out=tt[:], in0=tt[:], scalar1=math.pi, scalar2=-math.pi, op0=ALU.min, op1=ALU.max),
                   reads=[tb_], writes=[tb_])
            P.emit("act", lambda e, tt=tt: e.activation(out=tt[:], in_=tt[:], func=AF.Sin), reads=[tb_], writes=[tb_])
            P.emit("dve", lambda e, tt=tt: e.tensor_scalar(out=tt[:, 1, :], in0=tt[:, 1, :], scalar1=sgn[:, 0:1], scalar2=None, op0=ALU.mult),
                   reads=[tb_, b_sgn], writes=[tb_])
            P.dma("pool", ROPE[:, :, sl].rearrange("c p t -> p c t"), tt[:], reads=[tb_], writes=[bROPE])

        cT, b_cT = one(es, nc, "cT", [128, 8, NSEG], F32)
        cs, b_cs = one(es, nc, "cs", [128, 8, NSEG], BF16)
        adab, b_adab = one(es, nc, "adab", [128, DEPTH, 24], F32)
        ng, b_ng = one(es, nc, "ng", [128, DEPTH, 8], F32)
        for s_ in range(NSEG):
            P.dma("sp", cT[:, :, s_], c_in[s_:s_ + 1, :].rearrange("o (k p) -> p (o k)", p=128), writes=[b_cT], reads=[b_cT], slow=True)
        for l_ in range(DEPTH):
            P.dma("sp", adab[:, l_, :], ada_b[l_:l_ + 1, :].rearrange("o (f p) -> p (o f)", p=128), writes=[b_adab], reads=[b_adab], slow=True)
            P.dma("sp", ng[:, l_, :], norm_g[l_:l_ + 1, :].rearrange("o (f p) -> p (o f)", p=128), writes=[b_ng], reads=[b_ng], slow=True)
        P.dma("sp", fing[:], fin_g.rearrange("o (f p) -> p (o f)", p=128), writes=[b_fing], slow=True)
        for l_ in range(2):
            P.dma("sp", dvec[:, l_, :], ssm_d[l_:l_ + 1, :].rearrange("o (f p) -> p (o f)", p=128), writes=[b_dvec], reads=[b_dvec], slow=True)
        P.emit("act", lambda e: e.activation(out=cs[:], in_=cT[:], func=AF.Silu), reads=[b_cT], writes=[b_cs])
        aw32 = Rot(es, nc, "aw32", 2, [128, 8, 128], F32)
        awb = Rot(es, nc, "awb", 2, [128, 8, 128], BF16)
        pada = Rot(es, nc, "pada", 2, [128, 512], F32, psum=True)
        adat, b_adat = one(es, nc, "adat", [128, DEPTH, 24, NSEG], F32)
        for l in range(DEPTH):
            for f in range(24):
                w32, wb32 = aw32.next()
                P.dma("sp", w32[:], ada_w[l:l + 1, :, f * 128:(f + 1) * 128].rearrange("o (k p) j -> p (o k) j", p=128), writes=[wb32])
                wb, wbb = awb.next()
                P.emit("pool", lambda e, wb=wb, w32=w32: e.tensor_copy(out=wb[:], in_=w32[:]), reads=[wb32], writes=[wbb])
                ps, psb = pada.next()
                for k in range(8):
                    P.emit("pe", lambda e, ps=ps, wb=wb, k=k: e.matmul(ps[:, 0:NSEG], lhsT=wb[:, k, :], rhs=cs[:, k, :], start=(k == 0), stop=(k == 7)),
                           reads=[wbb, b_cs], writes=[psb])
                P.emit("dve", lambda e, ps=ps, l=l, f=f: e.tensor_scalar(out=adat[:, l, f, :], in0=ps[:, 0:NSEG], scalar1=adab[:, l, f:f + 1], scalar2=None, op0=ALU.add),
                       reads=[psb, b_adab], writes=[b_adat])
        for l in range(DEPTH):
            for s in range(NSEG):
                P.emit("dve", lambda e, l=l, s=s: e.tensor_scalar(out=modA[:, l, 0, :, s], in0=adat[:, l, 8:16, s], scalar1=1.0, scalar2=None, op0=ALU.add),
                       reads=[b_adat], writes=[b_mod])
                P.emit("dve", lambda e, l=l, s=s: e.tensor_tensor(out=modA[:, l, 0, :, s], in0=modA[:, l, 0, :, s], in1=ng[:, l, :], op=ALU.mult),
                       reads=[b_mod, b_ng], writes=[b_mod])
                P.emit("dve", lambda e, l=l, s=s: e.tensor_copy(out=modA[:, l, 1, :, s], in_=adat[:, l, 0:8, s]), reads=[b_adat, b_mod], writes=[b_mod])
                P.emit("dve", lambda e, l=l, s=s: e.tensor_copy(out=modA[:, l, 2, :, s], in_=adat[:, l, 16:24, s]), reads=[b_adat, b_mod], writes=[b_mod])

        xin = Rot(es, nc, "xin", 2, [128, D], F32)
        pxt = Rot(es, nc, "pxt", 2, [128, 512], F32, psum=True)
        xst = Rot(es, nc, "xst", 2, [128, 8, 128], F32)
        for tt_ in range(NT):
            xt, xb = xin.next()
            P.dma("sp", xt[:], x_in[tt_ * 128:(tt_ + 1) * 128, :], writes=[xb])
            st, sb_ = xst.next()
            for h in range(2):
                ps, psb = pxt.next()
                for j in range(4):
                    f = h * 4 + j
                    P.emit("pe", lambda e, ps=ps, xt=xt, f=f, j=j: e.matmul(ps[:, j * 128:(j + 1) * 128], lhsT=xt[:, f * 128:(f + 1) * 128], rhs=ident_f[:], start=True, stop=True),
                           reads=[xb, b_idf], writes=[psb])
                P.emit("act", lambda e, ps=ps, st=st, h=h: e.activation(out=st[:, h * 4:(h + 1) * 4, :], in_=ps[:].rearrange("p (j t) -> p j t", j=4), func=AF.Copy),
                       reads=[psb], writes=[sb_])
            P.dma("pool", XT[:, :, tt_ * 128:(tt_ + 1) * 128].rearrange("f p t -> p f t"), st[:], reads=[sb_], writes=[bX])
        P.flush()

    def norm_phase(es_outer, l, s, hm, b_hm):
        with ExitStack() as es:
            _norm_phase(es, l, s, hm, b_hm)
            P.flush()

    def _norm_phase(es, l, s, hm, b_hm):
        xb_r = Rot(es, nc, "nx", 2, [128, 8, 512], F32)
        sq_r = Rot(es, nc, "nsq", 2, [128, 512], F32)
        ps_r = Rot(es, nc, "nps", 2, [128, 512], F32, psum=True)
        rs_r = Rot(es, nc, "nrs", 2, [128, 512], F32)
        tm_r = Rot(es, nc, "ntm", 3, [128, 512], F32)
        for tb in range(SEG // 512):
            t0 = s * SEG + tb * 512
            xt, xb = xb_r.next()
            P.dma("sp", xt[:], XT[:, :, t0:t0 + 512].rearrange("f p t -> p f t"), reads=[bX], writes=[xb])
            ps, psb = ps_r.next()
            for f in range(8):
                sq, sqb = sq_r.next()
                P.emit("act", lambda e, sq=sq, xt=xt, f=f: e.activation(out=sq[:], in_=xt[:, f, :], func=AF.Square), reads=[xb], writes=[sqb])
                P.emit("pe", lambda e, ps=ps, sq=sq, f=f: e.matmul(ps[:], lhsT=ones_f[:], rhs=sq[:], start=(f == 0), stop=(f == 7)),
                       reads=[sqb, b_onf], writes=[psb])
            rs, rsb = rs_r.next()
            P.emit("act", lambda e, rs=rs, ps=ps: e.activation(out=rs[:], in_=ps[:], func=AF.Sqrt, bias=1e-6, scale=1.0 / D), reads=[psb], writes=[rsb])
            P.emit("dve", lambda e, rs=rs: e.reciprocal(out=rs[:], in_=rs[:]), reads=[rsb], writes=[rsb])
            for f in range(8):
                tm, tmb = tm_r.next()
                P.emit("dve", lambda e, tm=tm, xt=xt, rs=rs, f=f: e.tensor_tensor(out=tm[:], in0=xt[:, f, :], in1=rs[:], op=ALU.mult), reads=[xb, rsb], writes=[tmb])
                if l < DEPTH:
                    P.emit("act", lambda e, tm=tm, f=f, tb=tb: e.activation(out=hm[:, f, tb * 512:(tb + 1) * 512], in_=tm[:], func=AF.Identity,
                                                                       bias=modA[:, l, 1, f, s:s + 1], scale=modA[:, l, 0, f, s:s + 1]),
                           reads=[tmb, b_mod], writes=[b_hm])

    def load_w_bf16(es, name, w_ap, ncol, wb, b_wb, rot32):
        for k in range(8):
            w32, b32 = rot32.next()
            P.dma("sp", w32[:, 0:ncol], w_ap[0:1, k * 128:(k + 1) * 128, :].rearrange("o p j -> p (o j)"), writes=[b32])
            P.emit("pool", lambda e, w32=w32, k=k: e.tensor_copy(out=wb[:, k, :], in_=w32[:, 0:ncol]), reads=[b32], writes=[b_wb])

    def outproj_phase(l, w_ap):
        with ExitStack() as es:
            wb, b_wb = one(es, nc, "ow", [128, 8, D], BF16)
            r32 = Rot(es, nc, "ow32", 2, [128, D], F32)
            load_w_bf16(es, "ow", w_ap, D, wb, b_wb, r32)
            yb_r = Rot(es, nc, "oy", 2, [128, 8, 512], BF16)
            xb_r = Rot(es, nc, "ox", 2, [128, 8, 512], F32)
            ps_r = Rot(es, nc, "ops", 3, [128, 512], F32, psum=True)
            for tb in range(N // 512):
                s = (tb * 512) // SEG
                t0 = tb * 512
                yt, yb = yb_r.next()
                P.dma("sp", yt[:], YT[:, :, t0:t0 + 512].rearrange("f p t -> p f t"), reads=[bYT], writes=[yb])
                xt, xb = xb_r.next()
                P.dma("sp", xt[:], XT[:, :, t0:t0 + 512].rearrange("f p t -> p f t"), reads=[bX], writes=[xb])
                for f in range(8):
                    ps, psb = ps_r.next()
                    for k in range(8):
                        P.emit("pe", lambda e, ps=ps, k=k, f=f, yt=yt: e.matmul(ps[:], lhsT=wb[:, k, f * 128:(f + 1) * 128], rhs=yt[:, k, :], start=(k == 0), stop=(k == 7)),
                               reads=[b_wb, yb], writes=[psb])
                    P.emit("dve", lambda e, ps=ps, xt=xt, f=f, s=s: e.scalar_tensor_tensor(out=xt[:, f, :], in0=ps[:], scalar=modA[:, l, 2, f, s:s + 1], in1=xt[:, f, :],
                                                                                  op0=ALU.mult, op1=ALU.add),
                           reads=[psb, xb, b_mod], writes=[xb])
                P.dma("pool", XT[:, :, t0:t0 + 512].rearrange("f p t -> p f t"), xt[:], reads=[xb], writes=[bX])
            P.flush()

    def attn_layer(l, la):
        for s in range(NSEG):
            with ExitStack() as es:
                t1_r = Rot(es, nc, "at1", 2, [128, 512], F32)
                t2_r = Rot(es, nc, "at2", 2, [128, 512], F32)
                hm, b_hm = one(es, nc, "hm", [128, 8, SEG], BF16)
                norm_phase(es, l, s, hm, b_hm)
                w32 = Rot(es, nc, "aw32", 2, [128, 8, 128], F32)
                wbr = Rot(es, nc, "awb", 3, [128, 8, 128], BF16)
                ps_r = Rot(es, nc, "aps", 4, [128, 512], F32, psum=True)
                ps2_r = Rot(es, nc, "aps2", 2, [128, 512], F32, psum=True)
                qb_r = Rot(es, nc, "aqb", 3, [128, 512], BF16)
                ropeS, b_ropeS = one(es, nc, "ropeS", [128, 2, SEG], F32)
                P.dma("sp", ropeS[:], ROPE[:, :, s * SEG:(s + 1) * SEG].rearrange("c p t -> p c t"), reads=[bROPE], writes=[b_ropeS])
                ob_r = Rot(es, nc, "aob", 4, [128, 512], BF16)
                zo_r = Rot(es, nc, "azo", 3, [128, 512], F32)
                slabs = [sl_ for sl_ in range(80) if DBG.get("kinds") is None or (9 if sl_ * 128 // D == 9 else (sl_ * 128 // D) % 3) in DBG["kinds"]]

                def load_w(slab):
                    c0_ = slab * 128
                    wt, wtb = w32.next()
                    P.dma("sp", wt[:], attn_w_in[la:la + 1, :, c0_:c0_ + 128].rearrange("o (k p) j -> p (o k) j", p=128), writes=[wtb])
                    wb, wbb = wbr.next()
                    P.emit("pool", lambda e, wb=wb, wt=wt: e.tensor_copy(out=wb[:], in_=wt[:]), reads=[wtb], writes=[wbb])
                    return wb, wbb
                nxt = load_w(slabs[0]) if slabs else None
                for si, slab in enumerate(slabs):
                    col0 = slab * 128
                    kind = col0 // D
                    sl8 = slab % 8
                    wb, wbb = nxt
                    if si + 1 < len(slabs):
                        nxt = load_w(slabs[si + 1])
                    if kind < 9 and kind % 3 == 2:
                        g = kind // 3
                        for t4 in range(SEG // 512):
                            ps, psb = ps_r.next()
                            for j in range(4):
                                tk = t4 * 4 + j
                                for k in range(8):
                                    P.emit("pe", lambda e, ps=ps, j=j, k=k, tk=tk, wb=wb: e.matmul(ps[:, j * 128:(j + 1) * 128], lhsT=hm[:, k, tk * 128:(tk + 1) * 128], rhs=wb[:, k, :],
                                                                                          start=(k == 0), stop=(k == 7)),
                                           reads=[b_hm, wbb], writes=[psb])
                            ob, obb = ob_r.next()
                            P.emit("act", lambda e, ob=ob, ps=ps: e.activation(out=ob[:], in_=ps[:], func=AF.Copy), reads=[psb], writes=[obb])
                            r0 = s * SEG + t4 * 512
                            P.dma("sp", VV[g:g + 1, r0:r0 + 512, col0 % D:col0 % D + 128].rearrange("o (j p) c -> p (o j) c", p=128),
                                  ob[:].rearrange("p (j c) -> p j c", j=4), reads=[obb], writes=[bVV])
                        continue
                    for tb in range(SEG // 512):
                        t0 = s * SEG + tb * 512
                        ps, psb = ps_r.next()
                        for k in range(8):
                            P.emit("pe", lambda e, ps=ps, k=k, tb=tb, wb=wb: e.matmul(ps[:], lhsT=wb[:, k, :], rhs=hm[:, k, tb * 512:(tb + 1) * 512], start=(k == 0), stop=(k == 7)),
                                   reads=[b_hm, wbb], writes=[psb])
                        if kind == 9:
                            zo, zob = zo_r.next()
                            P.emit("act", lambda e, zo=zo, ps=ps: e.activation(out=zo[:], in_=ps[:], func=AF.Silu), reads=[psb], writes=[zob])
                            P.dma("sp", ZS[sl8, :, t0:t0 + 512], zo[:], reads=[zob], writes=[bZS])
                            continue
                        g = kind // 3
                        kq = DBG.get("kq", 9)
                        qb, qbb = qb_r.next()
                        P.emit("act", lambda e, qb=qb, ps=ps: e.activation(out=qb[:], in_=ps[:], func=AF.Copy), reads=[psb], writes=[qbb])
                        ob, obb = ob_r.next()
                        if kq >= 3:
                            ps2, ps2b = ps2_r.next()
                            P.emit("pe", lambda e, ps2=ps2, qb=qb: e.matmul(ps2[:], lhsT=perm_b[:], rhs=qb[:], start=True, stop=True), reads=[qbb, b_perm], writes=[ps2b])
                        rp, rpb = ropeS[:, :, tb * 512:(tb + 1) * 512], b_ropeS
                        if kq >= 4:
                            t1, t1b = t1_r.next()
                            t2, t2b = t2_r.next()
                            if kq in (4, 5, 9):
                                P.emit("dve", lambda e, t1=t1, ps=ps, rp=rp: e.tensor_tensor(out=t1[:], in0=ps[:], in1=rp[:, 0, :], op=ALU.mult), reads=[psb, rpb, qbb], writes=[t1b])
                            else:
                                P.emit("dve", lambda e, t1=t1, qb=qb, rp=rp: e.tensor_tensor(out=t1[:], in0=qb[:], in1=rp[:, 0, :], op=ALU.mult), reads=[qbb, rpb], writes=[t1b])
                            if kq in (4, 6, 9):
                                P.emit("dve", lambda e, t2=t2, ps2=ps2, rp=rp: e.tensor_tensor(out=t2[:], in0=ps2[:], in1=rp[:, 1, :], op=ALU.mult), reads=[ps2b, rpb], writes=[t2b])
                            else:
                                P.emit("dve", lambda e, t2=t2, qb=qb, rp=rp: e.tensor_tensor(out=t2[:], in0=qb[:], in1=rp[:, 1, :], op=ALU.mult), reads=[qbb, rpb], writes=[t2b])
                            P.emit("dve", lambda e, ob=ob, t1=t1, t2=t2: e.tensor_tensor(out=ob[:], in0=t1[:], in1=t2[:], op=ALU.add), reads=[t1b, t2b], writes=[obb])
                        else:
                            P.emit("dve", lambda e, ob=ob, qb=qb: e.tensor_copy(out=ob[:], in_=qb[:]), reads=[qbb], writes=[obb])
                        dst = (QT if kind % 3 == 0 else KT)
                        dbuf = (bQT if kind % 3 == 0 else bKT)
                        P.dma("sp", dst[g, sl8, :, t0:t0 + 512], ob[:], reads=[obb], writes=[dbuf])
                P.flush()

        if DBG["stop"] == "proj":
            return
        with ExitStack() as es:
            NK = 25
            q_r = Rot(es, nc, "cq", 2, [128, 3, 2, 128], BF16)
            for i in range(2):
                P.emit("pool", lambda e, i=i: e.memset(q_r.t[i][:], 0.0), writes=[q_r.b[i]])
            k_r = Rot(es, nc, "ck", 2, [128, NK * 128], BF16)
            v_r = Rot(es, nc, "cv", 2, [128, NK, 2, 128], BF16)
            for i in range(2):
                P.emit("pool", lambda e, i=i: e.memset(v_r.t[i][:], 0.0), writes=[v_r.b[i]])
            s_r = Rot(es, nc, "cs", 3, [128, 512], F32, psum=True)
            o_r = Rot(es, nc, "co", 2, [128, 512], F32, psum=True)
            p_r = Rot(es, nc, "cp", 4, [128, 256], BF16)
            rc_r = Rot(es, nc, "crc", 2, [128, 128], F32)
            on_r = Rot(es, nc, "con", 2, [128, 128], F32)
            z_r = Rot(es, nc, "cz", 2, [128, 8, 128], F32)
            y_r = Rot(es, nc, "cy", 2, [128, 8, 128], BF16)
            reach = (1, 2, 8)
            for qt in range(NT):
                seg = qt // (SEG // 128)
                q0 = qt * 128
                zt, zb = z_r.next()
                P.dma("sp", zt[:], ZS[:, :, q0:q0 + 128].rearrange("f p t -> p f t"), reads=[bZS], writes=[zb])
                yt, yb = y_r.next()
                for sl in range(8):
                    qtile, qb_ = q_r.next()
                    for h2 in range(2):
                        P.dma("sp", qtile[h2 * 64:(h2 + 1) * 64, :, h2, :], QT[:, sl, h2 * 64:(h2 + 1) * 64, q0:q0 + 128].rearrange("g p t -> p g t"), reads=[bQT], writes=[qb_])
                    ktile, kb_ = k_r.next()
                    vtile, vb_ = v_r.next()
                    tiles = []
                    off = 0
                    for g in range(3):
                        lo = max(0, qt - reach[g])
                        hi = min(NT - 1, qt + reach[g])
                        n = hi - lo + 1
                        P.dma("sp", ktile[:, off * 128:(off + n) * 128], KT[g, sl, :, lo * 128:(hi + 1) * 128], reads=[bKT], writes=[kb_])
                        for h2 in range(2):
                            P.dma("sp", vtile[:, off:off + n, h2, h2 * 64:(h2 + 1) * 64],
                                  VV[g:g + 1, lo * 128:(hi + 1) * 128, sl * 128 + h2 * 64:sl * 128 + (h2 + 1) * 64].rearrange("o (k p) c -> p (o k) c", p=128),
                                  reads=[bVV], writes=[vb_])
                        for kt_abs in range(lo, hi + 1):
                            j = kt_abs - qt
                            mi = 3 * g + (0 if j == -reach[g] else (2 if j == reach[g] else 1))
                            cross = (kt_abs // (SEG // 128)) != seg
                            tiles.append((g, off + (kt_abs - lo), mi, cross))
                        off += n
                    ot, ob_ = o_r.next()

                    def emit_qk(ti, ktile=ktile, qtile=qtile, kb_=kb_, qb_=qb_, tiles=tiles):
                        g, ix, mi, cross = tiles[ti]
                        st, sb_ = s_r.next()
                        P.emit("pe", lambda e, st=st, ix=ix, g=g: e.matmul(st[:, 0:256], lhsT=ktile[:, ix * 128:(ix + 1) * 128], rhs=qtile[:, g, :, :], start=True, stop=False),
                               reads=[kb_, qb_], writes=[sb_])
                        P.emit("pe", lambda e, st=st, mi=mi: e.matmul(st[:, 0:256], lhsT=ident_b[:], rhs=masks[:, mi:mi + 1, :].to_broadcast([128, 2, 128]), start=False, stop=True),
                               reads=[b_idb, b_masks], writes=[sb_])
                        return st, sb_
                    pending = emit_qk(0)
                    for ti, (g, ix, mi, cross) in enumerate(tiles):
                        st, sb_ = pending
                        if ti + 1 < len(tiles):
                            pending = emit_qk(ti + 1)
                        pt, pb_ = p_r.next()
                        if cross:
                            P.emit("act", lambda e, pt=pt, st=st: e.activation(out=pt[:], in_=st[:, 0:256], func=AF.Exp, bias=flag[:, 1:2], scale=0.125),
                                   reads=[sb_, b_flag], writes=[pb_])
                        else:
                            P.emit("act", lambda e, pt=pt, st=st: e.activation(out=pt[:], in_=st[:, 0:256], func=AF.Exp, scale=0.125), reads=[sb_], writes=[pb_])
                        first = ti == 0
                        last = ti == len(tiles) - 1
                        P.emit("pe", lambda e, ot=ot, vtile=vtile, pt=pt, ix=ix, first=first: e.matmul(ot[:, 0:128], lhsT=vtile[:, ix, 0, :], rhs=pt[:, 0:128], start=first, stop=False),
                               reads=[vb_, pb_], writes=[ob_])
                        P.emit("pe", lambda e, ot=ot, vtile=vtile, pt=pt, ix=ix: e.matmul(ot[:, 0:128], lhsT=vtile[:, ix, 1, :], rhs=pt[:, 128:256], start=False, stop=False),
                               reads=[vb_, pb_], writes=[ob_])
                        P.emit("pe", lambda e, ot=ot, pt=pt: e.matmul(ot[:, 128:256], lhsT=Eo[:, 0, :], rhs=pt[:, 0:128], start=False, stop=False),
                               reads=[b_Eo, pb_], writes=[ob_])
                        P.emit("pe", lambda e, ot=ot, pt=pt, last=last: e.matmul(ot[:, 128:256], lhsT=Eo[:, 1, :], rhs=pt[:, 128:256], start=False, stop=last),
                               reads=[b_Eo, pb_], writes=[ob_])
                    rc, rcb = rc_r.next()
                    P.emit("dve", lambda e, rc=rc, ot=ot: e.reciprocal(out=rc[:], in_=ot[:, 128:256]), reads=[ob_], writes=[rcb])
                    on, onb = on_r.next()
                    P.emit("dve", lambda e, on=on, ot=ot, rc=rc: e.tensor_tensor(out=on[:], in0=ot[:, 0:128], in1=rc[:], op=ALU.mult), reads=[ob_, rcb], writes=[onb])
                    P.emit("dve", lambda e, on=on, yt=yt, zt=zt, sl=sl: e.tensor_tensor(out=yt[:, sl, :], in0=on[:], in1=zt[:, sl, :], op=ALU.mult), reads=[onb, zb], writes=[yb])
                P.dma("pool", YT[:, :, q0:q0 + 128].rearrange("f p t -> p f t"), yt[:], reads=[yb], writes=[bYT])
            P.flush()
        if DBG["stop"] == "core":
            return
        outproj_phase(l, attn_w_out[la:la + 1, :, :])

    def ssm_prep(lb):
        with ExitStack() as es:
            mg2, b_mg2 = one(es, nc, "mg2", [128, 2], F32)
            mq, b_mq = one(es, nc, "mq", [128, 8], F32)
            pid, b_pid = one(es, nc, "spid", [128, 1], I32)
            pq, b_pq = one(es, nc, "spq", [128, 1], I32)
            pqf, b_pqf = one(es, nc, "spqf", [128, 1], F32)
            P.emit("dve", lambda e: e.memset(mg2[:], 0.0), writes=[b_mg2])
            P.emit("dve", lambda e: e.memset(mg2[0:64, 0:1], 1.0), reads=[b_mg2], writes=[b_mg2])
            P.emit("dve", lambda e: e.memset(mg2[64:128, 1:2], 1.0), reads=[b_mg2], writes=[b_mg2])
            P.emit("pool", lambda e: e.iota(pid[:], pattern=[[0, 1]], base=0, channel_multiplier=1), writes=[b_pid])
            P.emit("dve", lambda e: e.tensor_single_scalar(out=pq[:], in_=pid[:], scalar=4, op=ALU.arith_shift_right), reads=[b_pid], writes=[b_pq])
            P.emit("dve", lambda e: e.tensor_copy(out=pqf[:], in_=pq[:]), reads=[b_pq], writes=[b_pqf])
            for qq in range(8):
                P.emit("dve", lambda e, qq=qq: e.tensor_single_scalar(out=mq[:, qq:qq + 1], in_=pqf[:], scalar=float(qq), op=ALU.is_equal), reads=[b_pqf], writes=[b_mq])
            psm = Rot(es, nc, "sps", 3, [128, 512], F32, psum=True)
            for d in range(2):
                ld = lb * 2 + d
                nat, b_nat = one(es, nc, f"nat{d}", [32, 3, 128], F32)
                ldt, b_ldt = one(es, nc, f"ldt{d}", [32, 2], F32)
                P.dma("sp", nat[:, 0, :], lam_re_i[lb, d].rearrange("(q a) p -> q (a p)", a=2), writes=[b_nat])
                P.dma("sp", nat[:, 1, :], lam_im_i[lb, d].rearrange("(q a) p -> q (a p)", a=2), writes=[b_nat], reads=[b_nat])
                P.dma("sp", ldt[:], log_dt_i[lb, d:d + 1, :].rearrange("o (q a) -> q (o a)", a=2), writes=[b_ldt])
                for a in range(2):
                    P.emit("dve", lambda e, a=a, nat=nat, ldt=ldt: e.tensor_scalar(out=nat[:, 2, a * 64:(a + 1) * 64], in0=nat[:, 0, a * 64:(a + 1) * 64], scalar1=0.0, scalar2=ldt[:, a:a + 1],
                                                                         op0=ALU.mult, op1=ALU.add), reads=[b_nat, b_ldt], writes=[b_nat])
                L, b_L = one(es, nc, f"L{d}", [128, 16, 32], F32)
                ps, psb = psm.next()
                for i in range(3):
                    P.emit("pe", lambda e, ps=ps, i=i, nat=nat: e.matmul(ps[:, i * 32:(i + 1) * 32], lhsT=nat[:, i, :], rhs=ident_f[0:32, 0:32], start=True, stop=True),
                           reads=[b_nat, b_idf], writes=[psb])
                P.emit("dve", lambda e, ps=ps, L=L: e.tensor_copy(out=L[:, 0:3, :], in_=ps[:, 0:96].rearrange("p (i q) -> p i q", i=3)), reads=[psb], writes=[b_L])
                Ki, b_Ki = one(es, nc, f"Ki{d}", [128, 32], I32)

                def dv(fn):
                    P.emit("dve", fn, reads=[b_L], writes=[b_L])

                def ac(fn):
                    P.emit("act", fn, reads=[b_L], writes=[b_L])
                dv(lambda e, L=L: e.tensor_scalar(out=L[:, 0, :], in0=L[:, 0, :], scalar1=-1e-4, scalar2=None, op0=ALU.min))
                ac(lambda e, L=L: e.activation(out=L[:, 3, :], in_=L[:, 2, :], func=AF.Exp))
                dv(lambda e, L=L: e.tensor_tensor(out=L[:, 4, :], in0=L[:, 0, :], in1=L[:, 3, :], op=ALU.mult))
                dv(lambda e, L=L: e.tensor_tensor(out=L[:, 5, :], in0=L[:, 1, :], in1=L[:, 3, :], op=ALU.mult))
                ac(lambda e, L=L: e.activation(out=L[:, 6, :], in_=L[:, 4, :], func=AF.Exp))
                dv(lambda e, L=L: e.tensor_scalar(out=L[:, 13, :], in0=L[:, 5, :], scalar1=1.0 / TWO_PI, scalar2=None, op0=ALU.mult))
                P.emit("dve", lambda e, L=L, Ki=Ki: e.tensor_copy(out=Ki[:], in_=L[:, 13, :]), reads=[b_L], writes=[b_Ki])
                P.emit("dve", lambda e, L=L, Ki=Ki: e.tensor_copy(out=L[:, 13, :], in_=Ki[:]), reads=[b_Ki, b_L], writes=[b_L])
                dv(lambda e, L=L: e.scalar_tensor_tensor(out=L[:, 14, :], in0=L[:, 13, :], scalar=-C1, in1=L[:, 5, :], op0=ALU.mult, op1=ALU.add))
                dv(lambda e, L=L: e.scalar_tensor_tensor(out=L[:, 14, :], in0=L[:, 13, :], scalar=-C2, in1=L[:, 14, :], op0=ALU.mult, op1=ALU.add))
                dv(lambda e, L=L: e.tensor_scalar(out=L[:, 7, :], in0=L[:, 14, :], scalar1=math.pi / 2, scalar2=None, op0=ALU.add))
                dv(lambda e, L=L: e.tensor_single_scalar(out=L[:, 8, :], in_=L[:, 7, :], scalar=math.pi, op=ALU.is_gt))
                dv(lambda e, L=L: e.scalar_tensor_tensor(out=L[:, 7, :], in0=L[:, 8, :], scalar=-TWO_PI, in1=L[:, 7, :], op0=ALU.mult, op1=ALU.add))
                dv(lambda e, L=L: e.tensor_copy(out=L[:, 8, :], in_=L[:, 14, :]))
                dv(lambda e, L=L: e.tensor_scalar(out=L[:, 7:9, :], in0=L[:, 7:9, :], scalar1=math.pi, scalar2=-math.pi, op0=ALU.min, op1=ALU.max))
                ac(lambda e, L=L: e.activation(out=L[:, 7:9, :], in_=L[:, 7:9, :], func=AF.Sin))
                dv(lambda e, L=L: e.tensor_tensor(out=L[:, 9, :], in0=L[:, 6, :], in1=L[:, 7, :], op=ALU.mult))
                dv(lambda e, L=L: e.tensor_tensor(out=L[:, 10, :], in0=L[:, 6, :], in1=L[:, 8, :], op=ALU.mult))
                dv(lambda e, L=L: e.tensor_scalar(out=L[:, 13, :], in0=L[:, 9, :], scalar1=-1.0, scalar2=None, op0=ALU.add))
                dv(lambda e, L=L: e.tensor_tensor(out=L[:, 14, :], in0=L[:, 0, :], in1=L[:, 0, :], op=ALU.mult))
                dv(lambda e, L=L: e.tensor_tensor(out=L[:, 15, :], in0=L[:, 1, :], in1=L[:, 1, :], op=ALU.mult))
                dv(lambda e, L=L: e.tensor_tensor(out=L[:, 14, :], in0=L[:, 14, :], in1=L[:, 15, :], op=ALU.add))
                dv(lambda e, L=L: e.reciprocal(out=L[:, 14, :], in_=L[:, 14, :]))
                dv(lambda e, L=L: e.tensor_tensor(out=L[:, 11, :], in0=L[:, 13, :], in1=L[:, 0, :], op=ALU.mult))
                dv(lambda e, L=L: e.tensor_tensor(out=L[:, 15, :], in0=L[:, 10, :], in1=L[:, 1, :], op=ALU.mult))
                dv(lambda e, L=L: e.tensor_tensor(out=L[:, 11, :], in0=L[:, 11, :], in1=L[:, 15, :], op=ALU.add))
                dv(lambda e, L=L: e.tensor_tensor(out=L[:, 11, :], in0=L[:, 11, :], in1=L[:, 14, :], op=ALU.mult))
                dv(lambda e, L=L: e.tensor_tensor(out=L[:, 12, :], in0=L[:, 10, :], in1=L[:, 0, :], op=ALU.mult))
                dv(lambda e, L=L: e.tensor_tensor(out=L[:, 15, :], in0=L[:, 13, :], in1=L[:, 1, :], op=ALU.mult))
                dv(lambda e, L=L: e.tensor_tensor(out=L[:, 12, :], in0=L[:, 12, :], in1=L[:, 15, :], op=ALU.subtract))
                dv(lambda e, L=L: e.tensor_tensor(out=L[:, 12, :], in0=L[:, 12, :], in1=L[:, 14, :], op=ALU.mult))
                P.emit("dve", lambda e, L=L, ld=ld: e.tensor_copy(out=PW[:, ld, :, 0, 0], in_=L[:, 9, :]), reads=[b_L, b_PW], writes=[b_PW])
                P.emit("dve", lambda e, L=L, ld=ld: e.tensor_copy(out=PW[:, ld, :, 0, 1], in_=L[:, 10, :]), reads=[b_L, b_PW], writes=[b_PW])
                for j in range(1, 12):
                    def pwop(fn):
                        P.emit("dve", fn, reads=[b_PW, b_L], writes=[b_PW, b_L])
                    pwop(lambda e, L=L, ld=ld, j=j: e.tensor_tensor(out=L[:, 13, :], in0=PW[:, ld, :, j - 1, 0], in1=PW[:, ld, :, j - 1, 0], op=ALU.mult))
                    pwop(lambda e, L=L, ld=ld, j=j: e.tensor_tensor(out=L[:, 15, :], in0=PW[:, ld, :, j - 1, 1], in1=PW[:, ld, :, j - 1, 1], op=ALU.mult))
                    pwop(lambda e, L=L, ld=ld, j=j: e.tensor_tensor(out=PW[:, ld, :, j, 0], in0=L[:, 13, :], in1=L[:, 15, :], op=ALU.subtract))
                    pwop(lambda e, L=L, ld=ld, j=j: e.tensor_tensor(out=L[:, 13, :], in0=PW[:, ld, :, j - 1, 0], in1=PW[:, ld, :, j - 1, 1], op=ALU.mult))
                    pwop(lambda e, L=L, ld=ld, j=j: e.tensor_scalar(out=PW[:, ld, :, j, 1], in0=L[:, 13, :], scalar1=2.0, scalar2=None, op0=ALU.mult))
                P.emit("dve", lambda e, ld=ld: e.tensor_scalar(out=PW[:, ld, :, :, 2], in0=PW[:, ld, :, :, 1], scalar1=-1.0, scalar2=None, op0=ALU.mult),
                       reads=[b_PW], writes=[b_PW])
                Bn, b_Bn = one(es, nc, f"Bn{d}", [128, 2, 32, 16], F32)
                Bb, b_Bb = one(es, nc, f"Bb{d}", [128, 2, 32, 16], F32)
                tmpB, b_tB = one(es, nc, f"tB{d}", [128, 32, 16], F32)
                P.dma("sp", Bn[:, 0, :, :], b_re_i[lb, d].rearrange("(q a) p c -> (a p) q c", a=2), writes=[b_Bn])
                P.dma("sp", Bn[:, 1, :, :], b_im_i[lb, d].rearrange("(q a) p c -> (a p) q c", a=2), writes=[b_Bn], reads=[b_Bn])
                crb = L[:, 11, :].unsqueeze(2).to_broadcast([128, 32, 16])
                cib = L[:, 12, :].unsqueeze(2).to_broadcast([128, 32, 16])

                def bop(fn):
                    P.emit("dve", fn, reads=[b_Bn, b_L, b_Bb, b_tB], writes=[b_Bb, b_tB])
                bop(lambda e, Bn=Bn, Bb=Bb, crb=crb: e.tensor_tensor(out=Bb[:, 0], in0=Bn[:, 0], in1=crb, op=ALU.mult))
                bop(lambda e, Bn=Bn, tmpB=tmpB, cib=cib: e.tensor_tensor(out=tmpB[:], in0=Bn[:, 1], in1=cib, op=ALU.mult))
                bop(lambda e, Bb=Bb, tmpB=tmpB: e.tensor_tensor(out=Bb[:, 0], in0=Bb[:, 0], in1=tmpB[:], op=ALU.subtract))
                bop(lambda e, Bn=Bn, Bb=Bb, crb=crb: e.tensor_tensor(out=Bb[:, 1], in0=Bn[:, 1], in1=crb, op=ALU.mult))
                bop(lambda e, Bn=Bn, tmpB=tmpB, cib=cib: e.tensor_tensor(out=tmpB[:], in0=Bn[:, 0], in1=cib, op=ALU.mult))
                bop(lambda e, Bb=Bb, tmpB=tmpB: e.tensor_tensor(out=Bb[:, 1], in0=Bb[:, 1], in1=tmpB[:], op=ALU.add))
                Cn, b_Cn = one(es, nc, f"Cn{d}", [128, 2, 8, 64], F32)
                P.dma("sp", Cn[:, 0], c_re_i[lb, d].rearrange("(k q) c p -> (q c) k p", k=8), writes=[b_Cn])
                P.dma("sp", Cn[:, 1], c_im_i[lb, d].rearrange("(k q) c p -> (q c) k p", k=8), writes=[b_Cn], reads=[b_Cn])
                inX, b_inX = one(es, nc, f"inX{d}", [128, 32, 128], F32)
                stg = Rot(es, nc, f"stg{d}", 2, [128, 4, 128], BF16)
                for which in range(4):
                    ri = which % 2
                    P.emit("pool", lambda e, inX=inX: e.memset(inX[:], 0.0), writes=[b_inX])
                    if which < 2:
                        for j4 in range(4):
                            for a in range(2):
                                c0 = j4 * 32 + a * 16
                                P.emit("dve", lambda e, inX=inX, Bb=Bb, j4=j4, a=a, c0=c0, ri=ri: e.tensor_scalar(
                                    out=inX[:, j4::4, c0:c0 + 16], in0=Bb[:, ri, j4::4, :], scalar1=mg2[:, a:a + 1], scalar2=None, op0=ALU.mult),
                                    reads=[b_Bb, b_mg2, b_inX], writes=[b_inX])
                    else:
                        for j4 in range(4):
                            for a in range(2):
                                P.emit("dve", lambda e, inX=inX, Cn=Cn, j4=j4, a=a, ri=ri: e.tensor_scalar(
                                    out=inX[:, j4::4, a * 64:(a + 1) * 64], in0=Cn[:, ri, :, :], scalar1=mq[:, j4 * 2 + a:j4 * 2 + a + 1], scalar2=None, op0=ALU.mult),
                                    reads=[b_Cn, b_mq, b_inX], writes=[b_inX])
                    for p4 in range(8):
                        ps, psb = psm.next()
                        for j in range(4):
                            pp = p4 * 4 + j
                            P.emit("pe", lambda e, ps=ps, j=j, pp=pp, inX=inX: e.matmul(ps[:, j * 128:(j + 1) * 128], lhsT=inX[:, pp, :], rhs=ident_f[:], start=True, stop=True),
                                   reads=[b_inX, b_idf], writes=[psb])
                        st, stb = stg.next()
                        sc = -1.0 if which == 3 else 1.0
                        P.emit("act", lambda e, st=st, ps=ps, sc=sc: e.activation(out=st[:], in_=ps[:].rearrange("p (j c) -> p j c", j=4), func=AF.Copy, scale=sc),
                               reads=[psb], writes=[stb])
                        P.dma("pool", TAB[lb, d, p4 * 4:(p4 + 1) * 4, :, which, :].rearrange("j p c -> p j c"), st[:], reads=[stb], writes=[bTAB])
            P.flush()

    def ssm_layer(l, lb):
        ssm_prep(lb)
        if DBG["stop"] == "prep":
            return
        for s in range(NSEG):
            with ExitStack() as es:
                hm, b_hm = one(es, nc, "hm", [128, 8, SEG], BF16)
                norm_phase(es, l, s, hm, b_hm)
                w32 = Rot(es, nc, "sw32", 2, [128, 8, 128], F32)
                wbr = Rot(es, nc, "swb", 2, [128, 8, 128], BF16)
                ps_r = Rot(es, nc, "sps", 3, [128, 512], F32, psum=True)
                ub_r = Rot(es, nc, "sub", 2, [128, 512], BF16)
                uf_r = Rot(es, nc, "suf", 3, [128, 512], F32)
                for slab in range(16):
                    col0 = slab * 128
                    f = slab % 8
                    wt, wtb = w32.next()
                    P.dma("sp", wt[:], ssm_w_in[lb:lb + 1, :, col0:col0 + 128].rearrange("o (k p) j -> p (o k) j", p=128), writes=[wtb])
                    wb, wbb = wbr.next()
                    P.emit("pool", lambda e, wb=wb, wt=wt: e.tensor_copy(out=wb[:], in_=wt[:]), reads=[wtb], writes=[wbb])
                    for tb in range(SEG // 512):
                        t0 = s * SEG + tb * 512
                        ps, psb = ps_r.next()
                        for k in range(8):
                            P.emit("pe", lambda e, ps=ps, k=k, tb=tb, wb=wb: e.matmul(ps[:], lhsT=wb[:, k, :], rhs=hm[:, k, tb * 512:(tb + 1) * 512], start=(k == 0), stop=(k == 7)),
                                   reads=[b_hm, wbb], writes=[psb])
                        uf, ufb = uf_r.next()
                        if slab < 8:
                            ub, ubb = ub_r.next()
                            P.emit("act", lambda e, ub=ub, ps=ps: e.activation(out=ub[:], in_=ps[:], func=AF.Copy), reads=[psb], writes=[ubb])
                            P.dma("sp", UT[f, :, t0:t0 + 512], ub[:], reads=[ubb], writes=[bUT])
                            P.emit("dve", lambda e, uf=uf, ps=ps, f=f: e.tensor_scalar(out=uf[:], in0=ps[:], scalar1=dvec[:, lb, f:f + 1], scalar2=None, op0=ALU.mult),
                                   reads=[psb, b_dvec, ubb], writes=[ufb])
                            P.dma("sp", Y0[f, :, t0:t0 + 512], uf[:], reads=[ufb], writes=[bY0])
                        else:
                            P.emit("act", lambda e, uf=uf, ps=ps: e.activation(out=uf[:], in_=ps[:], func=AF.Silu), reads=[psb], writes=[ufb])
                            P.dma("sp", ZS[f, :, t0:t0 + 512], uf[:], reads=[ufb], writes=[bZS])
                P.flush()

        if DBG["stop"] == "inproj":
            return
        with ExitStack() as es:
            ut_r = Rot(es, nc, "qu", 1, [128, N], BF16)
            ya_r = Rot(es, nc, "qy", 1, [128, N], F32)
            tb_r = Rot(es, nc, "qt", 4, [128, 4, 128], BF16)
            x_r = Rot(es, nc, "qx", 3, [128, 2, SEG], F32)
            xb_r = Rot(es, nc, "qxb", 1, [128, 2, SEG], BF16)
            pb_r = Rot(es, nc, "qpb", 4, [128, 512], F32, psum=True)
            pc_r = Rot(es, nc, "qpc", 3, [128, 512], F32, psum=True)
            fins = [one(es, nc, "qfin", [128, 2], F32) for _ in range(2)]
            injs = [one(es, nc, "qinj", [128, 4], F32) for _ in range(2)]
            g1_r = Rot(es, nc, "qg1", 2, [128, 512], F32)
            g2_r = Rot(es, nc, "qg2", 2, [128, 512], F32)
            gb_r = Rot(es, nc, "qgb", 2, [128, 512], BF16)
            LV = 12
            dirs = DBG.get("dirs", (0, 1))

            def bu_fill(tbt, tbb, ut, ub, s):
                X, Xb_ = x_r.next()
                for blk in range(SEG // 512):
                    c0 = s * SEG + blk * 512
                    for ri in range(2):
                        ps, psb = pb_r.next()
                        P.emit("pe", lambda e, ps=ps, tbt=tbt, ut=ut, ri=ri, c0=c0: e.matmul(ps[:], lhsT=tbt[:, ri, :], rhs=ut[:, c0:c0 + 512], start=True, stop=True),
                               reads=[tbb, ub], writes=[psb])
                        P.emit("act", lambda e, ps=ps, X=X, ri=ri, blk=blk: e.activation(out=X[:, ri, blk * 512:(blk + 1) * 512], in_=ps[:], func=AF.Copy),
                               reads=[psb], writes=[Xb_])
                return X, Xb_

            def inject(X, Xb_, d, pw, fin, b_fin, inj, b_inj):
                tcol = 0 if d == 0 else SEG - 1

                def io(fn):
                    P.emit("dve", fn, reads=[b_fin, b_inj, b_PW, b_flag, Xb_], writes=[b_inj, Xb_])
                io(lambda e: e.tensor_scalar(out=inj[:, 0:1], in0=fin[:, 0:1], scalar1=pw(0, 0), scalar2=None, op0=ALU.mult))
                io(lambda e: e.scalar_tensor_tensor(out=inj[:, 0:1], in0=fin[:, 1:2], scalar=pw(0, 2), in1=inj[:, 0:1], op0=ALU.mult, op1=ALU.add))
                io(lambda e: e.tensor_scalar(out=inj[:, 1:2], in0=fin[:, 1:2], scalar1=pw(0, 0), scalar2=None, op0=ALU.mult))
                io(lambda e: e.scalar_tensor_tensor(out=inj[:, 1:2], in0=fin[:, 0:1], scalar=pw(0, 1), in1=inj[:, 1:2], op0=ALU.mult, op1=ALU.add))
                for ri in range(2):
                    io(lambda e, ri=ri: e.scalar_tensor_tensor(out=X[:, ri, tcol:tcol + 1], in0=inj[:, ri:ri + 1], scalar=flag[:, 0:1],
                                                         in1=X[:, ri, tcol:tcol + 1], op0=ALU.mult, op1=ALU.add))

            def scan_ops(X, Xb_, d, pw):
                ops = []

                def cstep(dsl, ssl, j):
                    def so(fn):
                        ops.append(lambda fn=fn: P.emit("dve", fn, reads=[Xb_, b_PW], writes=[Xb_]))
                    so(lambda e: e.scalar_tensor_tensor(out=X[:, :, dsl], in0=X[:, :, ssl], scalar=pw(j, 0), in1=X[:, :, dsl], op0=ALU.mult, op1=ALU.add))
                    so(lambda e: e.scalar_tensor_tensor(out=X[:, 0, dsl], in0=X[:, 1, ssl], scalar=pw(j, 2), in1=X[:, 0, dsl], op0=ALU.mult, op1=ALU.add))
                    so(lambda e: e.scalar_tensor_tensor(out=X[:, 1, dsl], in0=X[:, 0, ssl], scalar=pw(j, 1), in1=X[:, 1, dsl], op0=ALU.mult, op1=ALU.add))
                if DBG.get("noscan"):
                    return ops
                for j in range(LV):
                    S_, h = 2 ** (j + 1), 2 ** j
                    if d == 0:
                        cstep(slice(S_ - 1, SEG, S_), slice(h - 1, SEG, S_), j)
                    else:
                        cstep(slice(0, SEG, S_), slice(h, SEG, S_), j)
                for j in range(LV - 2, -1, -1):
                    S_, h = 2 ** (j + 1), 2 ** j
                    cnt = SEG // S_ - 1
                    if d == 0:
                        cstep(slice(S_ + h - 1, S_ + h - 1 + (cnt - 1) * S_ + 1, S_), slice(S_ - 1, S_ - 1 + (cnt - 1) * S_ + 1, S_), j)
                    else:
                        cstep(slice(h, h + (cnt - 1) * S_ + 1, S_), slice(S_, S_ + (cnt - 1) * S_ + 1, S_), j)
                return ops

            def interleave(lists):
                n = max(len(l_) for l_ in lists)
                for i in range(n):
                    for l_ in lists:
                        if i < len(l_):
                            l_[i]()

            def cmat(X, Xb_, tbt, tbb, ya, yab, s):
                Xh, Xhb = xb_r.next()
                P.emit("act", lambda e, Xh=Xh, X=X: e.activation(out=Xh[:], in_=X[:], func=AF.Copy), reads=[Xb_], writes=[Xhb])
                for blk in range(SEG // 512):
                    c0 = s * SEG + blk * 512
                    ps, psb = pc_r.next()
                    for ri in range(2):
                        P.emit("pe", lambda e, ps=ps, tbt=tbt, Xh=Xh, ri=ri, blk=blk: e.matmul(ps[:], lhsT=tbt[:, 2 + ri, :], rhs=Xh[:, ri, blk * 512:(blk + 1) * 512], start=(ri == 0), stop=(ri == 1)),
                               reads=[tbb, Xhb], writes=[psb])
                    P.emit("dve", lambda e, ps=ps, ya=ya, c0=c0: e.tensor_tensor(out=ya[:, c0:c0 + 512], in0=ps[:], in1=ya[:, c0:c0 + 512], op=ALU.add),
                           reads=[psb, yab], writes=[yab])

            for kt in range(8):
                ut, ub = ut_r.next()
                P.dma("sp", ut[:], UT[kt, :, :], reads=[bUT], writes=[ub])
                ya, yab = ya_r.next()
                P.dma("sp", ya[:], Y0[kt, :, :], reads=[bY0], writes=[yab])
                for j4 in range(4):
                    pp = kt * 4 + j4
                    tbs, pws = {}, {}
                    for d in dirs:
                        ld = lb * 2 + d
                        tbt, tbb = tb_r.next()
                        P.dma("sp", tbt[:], TAB[lb, d, pp, :, :, :], reads=[bTAB], writes=[tbb])
                        tbs[d] = (tbt, tbb)
                        pws[d] = (lambda j, c, ld=ld, pp=pp: PW[:, ld, pp, j, c:c + 1])
                    first = {}
                    for d in dirs:
                        s = 0 if d == 0 else 1
                        X, Xb_ = bu_fill(tbs[d][0], tbs[d][1], ut, ub, s)
                        first[d] = (X, Xb_, s)
                    interleave([scan_ops(first[d][0], first[d][1], d, pws[d]) for d in dirs])
                    for d in dirs:
                        X, Xb_, s = first[d]
                        fcol = SEG - 1 if d == 0 else 0
                        fin, b_fin = fins[d]
                        P.emit("dve", lambda e, X=X, fcol=fcol, fin=fin: e.tensor_copy(out=fin[:], in_=X[:, :, fcol]), reads=[Xb_, b_fin], writes=[b_fin])
                    second = {}
                    for d in dirs:
                        X, Xb_, s = first[d]
                        cmat(X, Xb_, tbs[d][0], tbs[d][1], ya, yab, s)
                        s2 = 1 - s
                        X2, X2b = bu_fill(tbs[d][0], tbs[d][1], ut, ub, s2)
                        inject(X2, X2b, d, pws[d], fins[d][0], fins[d][1], injs[d][0], injs[d][1])
                        second[d] = (X2, X2b, s2)
                    interleave([scan_ops(second[d][0], second[d][1], d, pws[d]) for d in dirs])
                    for d in dirs:
                        X2, X2b, s2 = second[d]
                        cmat(X2, X2b, tbs[d][0], tbs[d][1], ya, yab, s2)
                for cb in range(N // 512):
                    sl = slice(cb * 512, (cb + 1) * 512)
                    if DBG.get("rawY"):
                        P.dma("pool", GT[kt, :, sl], ya[:, sl], reads=[yab], writes=[bGT])
                        continue
                    g1, g1b = g1_r.next()
                    g2, g2b = g2_r.next()
                    P.emit("act", lambda e, g1=g1, ya=ya, sl=sl: e.activation(out=g1[:], in_=ya[:, sl], func=AF.Square), reads=[yab], writes=[g1b])
                    P.emit("dve", lambda e, g1=g1: e.tensor_scalar(out=g1[:], in0=g1[:], scalar1=0.044715, scalar2=1.0, op0=ALU.mult, op1=ALU.add), reads=[g1b], writes=[g1b])
                    P.emit("dve", lambda e, g1=g1, ya=ya, sl=sl: e.tensor_tensor(out=g1[:], in0=g1[:], in1=ya[:, sl], op=ALU.mult), reads=[g1b, yab], writes=[g1b])
                    P.emit("act", lambda e, g1=g1: e.activation(out=g1[:], in_=g1[:], func=AF.Sigmoid, scale=1.5957691216057308), reads=[g1b], writes=[g1b])
                    P.emit("dve", lambda e, g1=g1, g2=g2, ya=ya, sl=sl: e.tensor_tensor(out=g2[:], in0=g1[:], in1=ya[:, sl], op=ALU.mult), reads=[g1b, yab], writes=[g2b])
                    gb, gbb = gb_r.next()
                    P.emit("act", lambda e, gb=gb, g2=g2: e.activation(out=gb[:], in_=g2[:], func=AF.Copy), reads=[g2b], writes=[gbb])
                    P.dma("pool", GT[kt, :, sl], g2[:], reads=[g2b], writes=[bGT])
                    P.dma("pool", GB[kt, :, sl], gb[:], reads=[gbb], writes=[bGB])
            P.flush()

        if DBG["stop"] == "scan":
            return
        with ExitStack() as es:
            wb, b_wb = one(es, nc, "gw", [128, 8, D], BF16)
            r32 = Rot(es, nc, "gw32", 2, [128, D], F32)
            load_w_bf16(es, "gw", ssm_w_glu[lb:lb + 1, :, :], D, wb, b_wb, r32)
            gb_r = Rot(es, nc, "gg", 2, [128, 8, 512], BF16)
            g32_r = Rot(es, nc, "gg32", 2, [128, 8, 512], F32)
            z_r = Rot(es, nc, "gz", 2, [128, 8, 512], F32)
            ps_r = Rot(es, nc, "gps", 3, [128, 512], F32, psum=True)
            sg_r = Rot(es, nc, "gsg", 3, [128, 512], F32)
            yo_r = Rot(es, nc, "gyo", 2, [128, 8, 512], BF16)
            for tb in range(N // 512):
                t0 = tb * 512
                gt, gtb = gb_r.next()
                P.dma("sp", gt[:], GB[:, :, t0:t0 + 512].rearrange("f p t -> p f t"), reads=[bGB], writes=[gtb])
                g32, g32b = g32_r.next()
                P.dma("sp", g32[:], GT[:, :, t0:t0 + 512].rearrange("f p t -> p f t"), reads=[bGT], writes=[g32b])
                zt, ztb = z_r.next()
                P.dma("sp", zt[:], ZS[:, :, t0:t0 + 512].rearrange("f p t -> p f t"), reads=[bZS], writes=[ztb])
                yo, yob = yo_r.next()
                for f in range(8):
                    ps, psb = ps_r.next()
                    for k in range(8):
                        P.emit("pe", lambda e, ps=ps, k=k, f=f, gt=gt: e.matmul(ps[:], lhsT=wb[:, k, f * 128:(f + 1) * 128], rhs=gt[:, k, :], start=(k == 0), stop=(k == 7)),
                               reads=[b_wb, gtb], writes=[psb])
                    sg, sgb = sg_r.next()
                    P.emit("act", lambda e, sg=sg, ps=ps: e.activation(out=sg[:], in_=ps[:], func=AF.Sigmoid), reads=[psb], writes=[sgb])
                    P.emit("dve", lambda e, sg=sg, g32=g32, f=f: e.tensor_tensor(out=sg[:], in0=sg[:], in1=g32[:, f, :], op=ALU.mult), reads=[sgb, g32b], writes=[sgb])
                    P.emit("dve", lambda e, sg=sg, zt=zt, yo=yo, f=f: e.tensor_tensor(out=yo[:, f, :], in0=sg[:], in1=zt[:, f, :], op=ALU.mult), reads=[sgb, ztb], writes=[yob])
                P.dma("pool", YT[:, :, t0:t0 + 512].rearrange("f p t -> p f t"), yo[:], reads=[yob], writes=[bYT])
            P.flush()
        outproj_phase(l, ssm_w_out[lb:lb + 1, :, :])

    for l in layers:
        if l % 2 == 0:
            attn_layer(l, l // 2)
        else:
            ssm_layer(l, l // 2)

    with ExitStack() as es:
        xb_r = Rot(es, nc, "fx", 2, [128, 8, 512], F32)
        sq_r = Rot(es, nc, "fsq", 2, [128, 512], F32)
        ps_r = Rot(es, nc, "fps", 2, [128, 512], F32, psum=True)
        rs_r = Rot(es, nc, "frs", 2, [128, 512], F32)
        pt_r = Rot(es, nc, "fpt", 4, [128, 512], F32, psum=True)
        os_r = Rot(es, nc, "fos", 2, [128, D], F32)
        for tb in range(N // 512):
            t0 = tb * 512
            xt, xb = xb_r.next()
            P.dma("sp", xt[:], XT[:, :, t0:t0 + 512].rearrange("f p t -> p f t"), reads=[bX], writes=[xb])
            ps, psb = ps_r.next()
            for f in range(8):
                sq, sqb = sq_r.next()
                P.emit("act", lambda e, sq=sq, xt=xt, f=f: e.activation(out=sq[:], in_=xt[:, f, :], func=AF.Square), reads=[xb], writes=[sqb])
                P.emit("pe", lambda e, ps=ps, sq=sq, f=f: e.matmul(ps[:], lhsT=ones_f[:], rhs=sq[:], start=(f == 0), stop=(f == 7)), reads=[sqb, b_onf], writes=[psb])
            rs, rsb = rs_r.next()
            P.emit("act", lambda e, rs=rs, ps=ps: e.activation(out=rs[:], in_=ps[:], func=AF.Sqrt, bias=1e-6, scale=1.0 / D), reads=[psb], writes=[rsb])
            P.emit("dve", lambda e, rs=rs: e.reciprocal(out=rs[:], in_=rs[:]), reads=[rsb], writes=[rsb])
            for f in range(8):
                P.emit("dve", lambda e, xt=xt, rs=rs, f=f: e.tensor_tensor(out=xt[:, f, :], in0=xt[:, f, :], in1=rs[:], op=ALU.mult), reads=[xb, rsb], writes=[xb])
                P.emit("dve", lambda e, xt=xt, f=f: e.tensor_scalar(out=xt[:, f, :], in0=xt[:, f, :], scalar1=fing[:, f:f + 1], scalar2=None, op0=ALU.mult), reads=[xb, b_fing], writes=[xb])
            for sub in range(4):
                ot, otb = os_r.next()
                for h in range(2):
                    pt, ptb = pt_r.next()
                    for j in range(4):
                        f = h * 4 + j
                        P.emit("pe", lambda e, pt=pt, xt=xt, f=f, j=j, sub=sub: e.matmul(pt[:, j * 128:(j + 1) * 128], lhsT=xt[:, f, sub * 128:(sub + 1) * 128], rhs=ident_f[:], start=True, stop=True),
                               reads=[xb, b_idf], writes=[ptb])
                    P.emit("act", lambda e, pt=pt, ot=ot, h=h: e.activation(out=ot[:, h * 512:(h + 1) * 512], in_=pt[:], func=AF.Copy), reads=[ptb], writes=[otb])
                r0 = t0 + sub * 128
                P.dma("pool", y_out[r0:r0 + 128, :], ot[:], reads=[otb], writes=[bOUT])
        P.flush()
    top.close()
    return nc, P


_CACHE = {}


def kernel(**inputs):
    x_prompt = np.asarray(inputs["x_prompt"], np.float32)
    x_sample = np.asarray(inputs["x_sample"], np.float32)
    c_prompt = np.asarray(inputs["c_prompt"], np.float32)
    c_sample = np.asarray(inputs["c_sample"], np.float32)
    if "nc" not in _CACHE:
        _CACHE["nc"] = build_program()[0]
    nc = _CACHE["nc"]
    shared = {k: np.ascontiguousarray(np.asarray(inputs[k], np.float32)) for k in (
        "norm_g", "ada_w", "ada_b", "attn_w_in", "attn_w_out", "ssm_w_in", "ssm_lam_re", "ssm_lam_im",
        "ssm_log_dt", "ssm_b_re", "ssm_b_im", "ssm_c_re", "ssm_c_im", "ssm_d", "ssm_w_glu", "ssm_w_out")}
    shared["final_norm_g"] = np.ascontiguousarray(np.asarray(inputs["final_norm_g"], np.float32).reshape(1, D))
    in_maps = []
    for core in range(8):
        c = core % 4
        m = dict(shared)
        if c < 2:
            m["x_in"] = np.ascontiguousarray(x_prompt[c])
            m["c_in"] = np.ascontiguousarray(np.stack([c_prompt[c], c_prompt[c]]))
            m["pos_in"] = np.arange(N, dtype=np.float32).reshape(1, N)
            fl = np.zeros((128, 2), np.float32)
            fl[:, 0] = 1.0
        else:
            a = 2 * (c - 2)
            m["x_in"] = np.ascontiguousarray(np.concatenate([x_sample[a], x_sample[a + 1]], axis=0))
            m["c_in"] = np.ascontiguousarray(np.stack([c_sample[a], c_sample[a + 1]]))
            m["pos_in"] = np.concatenate([np.arange(SEG), np.arange(SEG)]).astype(np.float32).reshape(1, N)
            fl = np.zeros((128, 2), np.float32)
            fl[:, 1] = NEGM
        m["flag_in"] = fl
        in_maps.append(m)
    res = run_bass_kernel_spmd(nc, in_maps, core_ids=list(range(8)))
    outs = [np.asarray(r["y_out"], np.float32) for r in res.results]
    y_prompt = np.stack([outs[0], outs[1]]).reshape(2, N, D)
    y_sample = np.stack([outs[2][:SEG], outs[2][SEG:], outs[3][:SEG], outs[3][SEG:]])
    return (y_prompt, y_sample)
```

```python
import math
import numpy as np
from contextlib import ExitStack
import concourse.bass as bass
import concourse.mybir as mybir
from concourse.bass_utils import run_bass_kernel_spmd

F32 = mybir.dt.float32
BF16 = mybir.dt.bfloat16
I32 = mybir.dt.int32
AF = mybir.ActivationFunctionType
ALU = mybir.AluOpType

D = 1024
NSEG = 2
SEG = 4096
N = NSEG * SEG
NT = N // 128
DEPTH = 4
NEGM = -30000.0
TWO_PI = 2.0 * math.pi
C1 = 6.28125
C2 = TWO_PI - C1


class Buf:
    __slots__ = ("w", "r")

    def __init__(self):
        self.w = None
        self.r = {}


class Prog:
    ENG = ("pe", "act", "dve", "pool", "sp")
    NDSEM = 10

    def __init__(self):
        self.nc = bass.Bass("TRN2", target_bir_lowering=False)
        nc = self.nc
        self.sem = {e: nc.alloc_semaphore("sem_" + e) for e in self.ENG}
        self.cnt = {e: 0 for e in self.ENG}
        self.seen = {e: {} for e in self.ENG}
        self.q = {e: [] for e in self.ENG}
        self.dsem, self.dval, self.drr = {}, {}, {}
        for qn in ("sp", "pool", "act"):
            self.dsem[qn] = [nc.alloc_semaphore(f"dsem_{qn}{i}") for i in range(self.NDSEM)]
            self.dval[qn] = [0] * self.NDSEM
            self.drr[qn] = 0
        self.ninst = 0

    def _deps(self, e, reads, writes, extra=()):
        deps = {}

        def need(key, sem, val):
            if key == "pe" and e == "pe":
                return
            cur = deps.get(key)
            if cur is None or cur[1] < val:
                deps[key] = (sem, val)

        for b in reads:
            if b.w is not None:
                need(*b.w)
        for b in writes:
            if b.w is not None:
                need(*b.w)
            for k, (s, v) in b.r.items():
                need(k, s, v)
        for ev in extra:
            need(*ev)
        seen = self.seen[e]
        for key, (sem, val) in deps.items():
            if seen.get(key, 0) < val:
                self.q[e].append(("wait", sem, val))
                seen[key] = val

    def _mark(self, ev, reads, writes):
        key, sem, val = ev
        for b in reads:
            b.r[key] = (sem, val)
        for b in writes:
            b.w = ev
            b.r = {}

    def emit(self, e, fn, reads=(), writes=()):
        self._deps(e, reads, writes)
        self.cnt[e] += 1
        self.q[e].append(("inst", fn, self.sem[e], 1))
        self._mark((e, self.sem[e], self.cnt[e]), reads, writes)
        self.ninst += 1

    def dma(self, qn, out, in_, reads=(), writes=(), slow=False):
        k = self.drr[qn]
        self.drr[qn] = (k + 1) % self.NDSEM
        sem = self.dsem[qn][k]
        key = f"d{qn}{k}"
        prev = self.dval[qn][k]
        extra = ((key, sem, prev),) if prev else ()
        self._deps(qn, reads, writes, extra)
        val = prev + 16
        self.dval[qn][k] = val
        kw = {"allow_slow_non_contiguous": True} if slow else {}
        self.q[qn].append(("inst", lambda eng, o=out, i=in_, kw=kw: eng.dma_start(out=o, in_=i, **kw), sem, 16))
        self._mark((key, sem, val), reads, writes)
        self.ninst += 1

    def flush(self):
        for qn in self.dsem:
            for k in range(self.NDSEM):
                if self.dval[qn][k]:
                    self.q["sp"].append(("wait", self.dsem[qn][k], self.dval[qn][k]))
        for e in self.ENG:
            if e != "sp" and self.cnt[e]:
                self.q["sp"].append(("wait", self.sem[e], self.cnt[e]))
        nc = self.nc
        with nc.Block() as block:
            def replay(e):
                items = self.q[e]

                def body(eng):
                    for it in items:
                        if it[0] == "wait":
                            eng.wait_ge(it[1], it[2])
                        else:
                            it[1](eng).then_inc(it[2], it[3])
                return body
            block.tensor(replay("pe"))
            block.scalar(replay("act"))
            block.vector(replay("dve"))
            block.gpsimd(replay("pool"))
            block.sync(replay("sp"))
        self.q = {e: [] for e in self.ENG}


DBG = {"stop": None}
_UID = [0]


def _uid(name):
    _UID[0] += 1
    return f"{name}_{_UID[0]}"


class Rot:
    def __init__(self, es, nc, name, n, shape, dt, psum=False):
        mk = nc.psum_tensor if psum else nc.sbuf_tensor
        self.t = [es.enter_context(mk(_uid(name), list(shape), dt)) for i in range(n)]
        self.b = [Buf() for _ in range(n)]
        self.i = 0

    def next(self):
        k = self.i
        self.i = (k + 1) % len(self.t)
        return self.t[k], self.b[k]


def one(es, nc, name, shape, dt, psum=False):
    mk = nc.psum_tensor if psum else nc.sbuf_tensor
    return es.enter_context(mk(_uid(name), list(shape), dt)), Buf()


def build_program(layers=None):
    if layers is None:
        layers = list(range(DEPTH))
    P = Prog()
    nc = P.nc

    def din(name, shape, dt=F32):
        return nc.dram_tensor(name, list(shape), dt, kind="ExternalInput").ap()

    def dscr(name, shape, dt):
        return nc.dram_tensor(name, list(shape), dt, kind="Internal").ap()

    x_in = din("x_in", [N, D])
    c_in = din("c_in", [NSEG, D])
    pos_in = din("pos_in", [1, N])
    flag_in = din("flag_in", [128, 2])
    norm_g = din("norm_g", [DEPTH, D])
    ada_w = din("ada_w", [DEPTH, D, 3 * D])
    ada_b = din("ada_b", [DEPTH, 3 * D])
    attn_w_in = din("attn_w_in", [2, D, 10 * D])
    attn_w_out = din("attn_w_out", [2, D, D])
    ssm_w_in = din("ssm_w_in", [2, D, 2 * D])
    lam_re_i = din("ssm_lam_re", [2, 2, 64, 64])
    lam_im_i = din("ssm_lam_im", [2, 2, 64, 64])
    log_dt_i = din("ssm_log_dt", [2, 2, 64])
    b_re_i = din("ssm_b_re", [2, 2, 64, 64, 16])
    b_im_i = din("ssm_b_im", [2, 2, 64, 64, 16])
    c_re_i = din("ssm_c_re", [2, 2, 64, 16, 64])
    c_im_i = din("ssm_c_im", [2, 2, 64, 16, 64])
    ssm_d = din("ssm_d", [2, D])
    ssm_w_glu = din("ssm_w_glu", [2, D, D])
    ssm_w_out = din("ssm_w_out", [2, D, D])
    fin_g = din("final_norm_g", [1, D])
    y_out = nc.dram_tensor("y_out", [N, D], F32, kind="ExternalOutput").ap()

    XT = dscr("XT", [8, 128, N], F32)
    ROPE = dscr("ROPE", [2, 128, N], F32)
    QT = dscr("QT", [3, 8, 128, N], BF16)
    KT = dscr("KT", [3, 8, 128, N], BF16)
    VV = dscr("VV", [3, N, 8, 130], BF16)
    ZS = dscr("ZS", [8, 128, N], F32)
    YT = dscr("YT", [8, 128, N], BF16)
    UT = dscr("UT", [8, 128, N], BF16)
    Y0 = dscr("Y0", [8, 128, N], F32)
    GT = nc.dram_tensor("GT", [8, 128, N], F32, kind="ExternalOutput").ap() if DBG.get("dumpGT") else dscr("GT", [8, 128, N], F32)
    GB = dscr("GB", [8, 128, N], BF16)
    TAB = dscr("TAB", [2, 2, 32, 128, 4, 128], BF16)
    bX, bROPE, bQT, bKT, bVV, bZS, bYT, bUT, bY0, bGT, bGB, bTAB, bOUT = [Buf() for _ in range(13)]

    top = ExitStack()
    ident_f, b_idf = one(top, nc, "ident_f", [128, 128], F32)
    ones_f, b_onf = one(top, nc, "ones_f", [128, 128], F32)
    ident_b, b_idb = one(top, nc, "ident_b", [128, 128], BF16)
    perm_b, b_perm = one(top, nc, "perm_b", [128, 128], BF16)
    masks, b_masks = one(top, nc, "masks", [128, 9, 128], BF16)
    Eo, b_Eo = one(top, nc, "Eo", [128, 2, 128], BF16)
    flag, b_flag = one(top, nc, "flag", [128, 2], F32)
    modA, b_mod = one(top, nc, "modA", [128, DEPTH, 3, 8, NSEG], F32)
    fing, b_fing = one(top, nc, "fing", [128, 8], F32)
    dvec, b_dvec = one(top, nc, "dvec", [128, 2, 8], F32)
    PW, b_PW = one(top, nc, "PW", [128, 4, 32, 12, 3], F32)

    with ExitStack() as es:
        P.emit("pool", lambda e: e.memset(ident_f[:], 0.0), writes=[b_idf])
        P.emit("pool", lambda e: e.affine_select(out=ident_f[:], in_=ident_f[:], pattern=[[-1, 128]],
                                                 compare_op=ALU.not_equal, fill=1.0, base=0, channel_multiplier=1),
               reads=[b_idf], writes=[b_idf])
        P.emit("dve", lambda e: e.tensor_copy(out=ident_b[:], in_=ident_f[:]), reads=[b_idf], writes=[b_idb])
        P.emit("dve", lambda e: e.memset(ones_f[:], 1.0), writes=[b_onf])
        pf, b_pf = one(es, nc, "pf", [128, 128], F32)
        P.emit("pool", lambda e: e.memset(pf[:], 0.0), writes=[b_pf])
        for blk in range(2):
            for half in range(2):
                c0 = blk * 64 + half * 32
                base = -(c0 + (32 if half == 0 else -32))
                P.emit("pool", lambda e, c0=c0, base=base: e.affine_select(
                    out=pf[:, c0:c0 + 32], in_=pf[:, c0:c0 + 32], pattern=[[-1, 32]],
                    compare_op=ALU.not_equal, fill=1.0, base=base, channel_multiplier=1),
                    reads=[b_pf], writes=[b_pf])
        P.emit("dve", lambda e: e.tensor_copy(out=perm_b[:], in_=pf[:]), reads=[b_pf], writes=[b_perm])
        P.emit("dve", lambda e: e.memset(Eo[:], 0.0), writes=[b_Eo])
        P.emit("dve", lambda e: e.memset(Eo[:, 0, 0:64], 1.0), writes=[b_Eo], reads=[b_Eo])
        P.emit("dve", lambda e: e.memset(Eo[:, 1, 64:128], 1.0), writes=[b_Eo], reads=[b_Eo])
        P.dma("sp", flag[:], flag_in, writes=[b_flag])

        Dm, b_Dm = one(es, nc, "Dm", [128, 128], I32)
        Df, b_Df = one(es, nc, "Df", [128, 128], F32)
        t_i, b_ti = one(es, nc, "t_i", [128, 128], I32)
        mm, b_mm = one(es, nc, "mm", [128, 6, 128], F32)
        mk, b_mk = one(es, nc, "mk", [128, 9, 128], F32)
        P.emit("pool", lambda e: e.iota(Dm[:], pattern=[[-1, 128]], base=128, channel_multiplier=1), writes=[b_Dm])
        P.emit("dve", lambda e: e.tensor_copy(out=Df[:], in_=Dm[:]), reads=[b_Dm], writes=[b_Df])
        for idx, msk in ((0, 15), (1, 3)):
            P.emit("dve", lambda e, msk=msk: e.tensor_single_scalar(out=t_i[:], in_=Dm[:], scalar=msk, op=ALU.bitwise_and),
                   reads=[b_Dm], writes=[b_ti])
            P.emit("dve", lambda e, idx=idx: e.tensor_single_scalar(out=mm[:, idx, :], in_=t_i[:], scalar=0, op=ALU.is_equal),
                   reads=[b_ti], writes=[b_mm])
        P.emit("dve", lambda e: e.tensor_single_scalar(out=mm[:, 2, :], in_=Df[:], scalar=128.0, op=ALU.is_ge), reads=[b_Df], writes=[b_mm])
        P.emit("dve", lambda e: e.tensor_single_scalar(out=mm[:, 3, :], in_=Df[:], scalar=128.0, op=ALU.is_le), reads=[b_Df], writes=[b_mm])
        P.emit("dve", lambda e: e.tensor_single_scalar(out=mk[:, 0, :], in_=Df[:], scalar=192.0, op=ALU.is_ge), reads=[b_Df], writes=[b_mk])
        P.emit("dve", lambda e: e.tensor_scalar(out=mk[:, 1, :], in0=Df[:], scalar1=64.0, scalar2=None, op0=ALU.is_ge), reads=[b_Df], writes=[b_mk])
        P.emit("dve", lambda e: e.tensor_single_scalar(out=mm[:, 4, :], in_=Df[:], scalar=192.0, op=ALU.is_le), reads=[b_Df], writes=[b_mm])
        P.emit("dve", lambda e: e.tensor_tensor(out=mk[:, 1, :], in0=mk[:, 1, :], in1=mm[:, 4, :], op=ALU.mult), reads=[b_mk, b_mm], writes=[b_mk])
        P.emit("dve", lambda e: e.tensor_single_scalar(out=mk[:, 2, :], in_=Df[:], scalar=64.0, op=ALU.is_le), reads=[b_Df], writes=[b_mk])
        for gi, mi in ((1, 1), (2, 0)):
            o = 3 * gi
            P.emit("dve", lambda e, o=o, mi=mi: e.tensor_tensor(out=mk[:, o, :], in0=mm[:, mi, :], in1=mm[:, 2, :], op=ALU.mult), reads=[b_mm], writes=[b_mk])
            P.emit("dve", lambda e, o=o, mi=mi: e.tensor_copy(out=mk[:, o + 1, :], in_=mm[:, mi, :]), reads=[b_mm], writes=[b_mk])
            P.emit("dve", lambda e, o=o, mi=mi: e.tensor_tensor(out=mk[:, o + 2, :], in0=mm[:, mi, :], in1=mm[:, 3, :], op=ALU.mult), reads=[b_mm], writes=[b_mk])
        P.emit("dve", lambda e: e.tensor_scalar(out=masks[:], in0=mk[:], scalar1=-1.0, scalar2=-NEGM, op0=ALU.add, op1=ALU.mult),
               reads=[b_mk], writes=[b_masks])

        invf, b_invf = one(es, nc, "invf", [128, 1], F32)
        pid, b_pid = one(es, nc, "pid", [128, 1], I32)
        pidf, b_pidf = one(es, nc, "pidf", [128, 1], F32)
        sgn, b_sgn = one(es, nc, "sgn", [128, 1], F32)
        P.emit("pool", lambda e: e.iota(pid[:], pattern=[[0, 1]], base=0, channel_multiplier=1), writes=[b_pid])
        P.emit("dve", lambda e: e.tensor_single_scalar(out=pid[:], in_=pid[:], scalar=31, op=ALU.bitwise_and), reads=[b_pid], writes=[b_pid])
        P.emit("dve", lambda e: e.tensor_copy(out=pidf[:], in_=pid[:]), reads=[b_pid], writes=[b_pidf])
        P.emit("act", lambda e: e.activation(out=invf[:], in_=pidf[:], func=AF.Exp, scale=-math.log(10000.0) / 32.0),
               reads=[b_pidf], writes=[b_invf])
        P.emit("dve", lambda e: e.memset(sgn[:], 1.0), writes=[b_sgn])
        for hb in range(2):
            P.emit("dve", lambda e, hb=hb: e.memset(sgn[hb * 64:hb * 64 + 32, :], -1.0), reads=[b_sgn], writes=[b_sgn])
        CH = 2048
        posb = Rot(es, nc, "posb", 2, [128, CH], F32)
        ang = Rot(es, nc, "ang", 2, [128, CH], F32)
        kf = Rot(es, nc, "kf", 2, [128, CH], F32)
        ki = Rot(es, nc, "ki", 2, [128, CH], I32)
        tro = Rot(es, nc, "tro", 2, [128, 2, CH], F32)
        for cb in range(N // CH):
            sl = slice(cb * CH, (cb + 1) * CH)
            pt, pb_ = posb.next()
            P.dma("sp", pt[:], pos_in[0:1, sl].to_broadcast([128, CH]), writes=[pb_])
            at, ab = ang.next()
            P.emit("dve", lambda e, at=at, pt=pt: e.tensor_scalar(out=at[:], in0=pt[:], scalar1=invf[:, 0:1], scalar2=None, op0=ALU.mult),
                   reads=[pb_, b_invf], writes=[ab])
            kt_, kb_ = kf.next()
            kit, kib = ki.next()
            P.emit("dve", lambda e, at=at, kt_=kt_: e.tensor_scalar(out=kt_[:], in0=at[:], scalar1=1.0 / TWO_PI, scalar2=None, op0=ALU.mult),
                   reads=[ab], writes=[kb_])
            P.emit("dve", lambda e, kit=kit, kt_=kt_: e.tensor_copy(out=kit[:], in_=kt_[:]), reads=[kb_], writes=[kib])
            P.emit("dve", lambda e, kit=kit, kt_=kt_: e.tensor_copy(out=kt_[:], in_=kit[:]), reads=[kib], writes=[kb_])
            P.emit("dve", lambda e, at=at, kt_=kt_: e.scalar_tensor_tensor(out=at[:], in0=kt_[:], scalar=-C1, in1=at[:], op0=ALU.mult, op1=ALU.add),
                   reads=[kb_, ab], writes=[ab])
            P.emit("dve", lambda e, at=at, kt_=kt_: e.scalar_tensor_tensor(out=at[:], in0=kt_[:], scalar=-C2, in1=at[:], op0=ALU.mult, op1=ALU.add),
                   reads=[kb_, ab], writes=[ab])
            tt, tb_ = tro.next()
            P.emit("dve", lambda e, at=at, tt=tt: e.tensor_scalar(out=tt[:, 0, :], in0=at[:], scalar1=math.pi / 2, scalar2=None, op0=ALU.add), reads=[ab], writes=[tb_])
            P.emit("dve", lambda e, at=at, tt=tt: e.tensor_single_scalar(out=tt[:, 1, :], in_=tt[:, 0, :], scalar=math.pi, op=ALU.is_gt), reads=[tb_], writes=[tb_])
            P.emit("dve", lambda e, at=at, tt=tt: e.scalar_tensor_tensor(out=tt[:, 0, :], in0=tt[:, 1, :], scalar=-TWO_PI, in1=tt[:, 0, :], op0=ALU.mult, op1=ALU.add), reads=[tb_], writes=[tb_])
            P.emit("dve", lambda e, at=at, tt=tt: e.tensor_copy(out=tt[:, 1, :], in_=at[:]), reads=[ab, tb_], writes=[tb_])
            P.emit("dve", lambda e, tt=tt: e.tensor_scalar(out=tt[:], in0=tt[:], scalar1=math.pi, scalar2=-math.pi, op0=ALU.min, op1=ALU.max),
                   reads=[tb_], writes=[tb_])
            P.emit("act", lambda e, tt=tt: e.activation(out=tt[:], in_=tt[:], func=AF.Sin), reads=[tb_], writes=[tb_])
            P.emit("dve", lambda e, tt=tt: e.tensor_scalar(out=tt[:, 1, :], in0=tt[:, 1, :], scalar1=sgn[:, 0:1], scalar2=None, op0=ALU.mult),
                   reads=[tb_, b_sgn], writes=[tb_])
            P.dma("pool", ROPE[:, :, sl].rearrange("c p t -> p c t"), tt[:], reads=[tb_], writes=[bROPE])

        cT, b_cT = one(es, nc, "cT", [128, 8, NSEG], F32)
        cs, b_cs = one(es, nc, "cs", [128, 8, NSEG], BF16)
        adab, b_adab = one(es, nc, "adab", [128, DEPTH, 24], F32)
        ng, b_ng = one(es, nc, "ng", [128, DEPTH, 8], F32)
        for s_ in range(NSEG):
            P.dma("sp", cT[:, :, s_], c_in[s_:s_ + 1, :].rearrange("o (k p) -> p (o k)", p=128), writes=[b_cT], reads=[b_cT], slow=True)
        for l_ in range(DEPTH):
            P.dma("sp", adab[:, l_, :], ada_b[l_:l_ + 1, :].rearrange("o (f p) -> p (o f)", p=128), writes=[b_adab], reads=[b_adab], slow=True)
            P.dma("sp", ng[:, l_, :], norm_g[l_:l_ + 1, :].rearrange("o (f p) -> p (o f)", p=128), writes=[b_ng], reads=[b_ng], slow=True)
        P.dma("sp", fing[:], fin_g.rearrange("o (f p) -> p (o f)", p=128), writes=[b_fing], slow=True)
        for l_ in range(2):
            P.dma("sp", dvec[:, l_, :], ssm_d[l_:l_ + 1, :].rearrange("o (f p) -> p (o f)", p=128), writes=[b_dvec], reads=[b_dvec], slow=True)
        P.emit("act", lambda e: e.activation(out=cs[:], in_=cT[:], func=AF.Silu), reads=[b_cT], writes=[b_cs])
        aw32 = Rot(es, nc, "aw32", 2, [128, 8, 128], F32)
        awb = Rot(es, nc, "awb", 2, [128, 8, 128], BF16)
        pada = Rot(es, nc, "pada", 2, [128, 512], F32, psum=True)
        adat, b_adat = one(es, nc, "adat", [128, DEPTH, 24, NSEG], F32)
        for l in range(DEPTH):
            for f in range(24):
                w32, wb32 = aw32.next()
                P.dma("sp", w32[:], ada_w[l:l + 1, :, f * 128:(f + 1) * 128].rearrange("o (k p) j -> p (o k) j", p=128), writes=[wb32])
                wb, wbb = awb.next()
                P.emit("pool", lambda e, wb=wb, w32=w32: e.tensor_copy(out=wb[:], in_=w32[:]), reads=[wb32], writes=[wbb])
                ps, psb = pada.next()
                for k in range(8):
                    P.emit("pe", lambda e, ps=ps, wb=wb, k=k: e.matmul(ps[:, 0:NSEG], lhsT=wb[:, k, :], rhs=cs[:, k, :], start=(k == 0), stop=(k == 7)),
                           reads=[wbb, b_cs], writes=[psb])
                P.emit("dve", lambda e, ps=ps, l=l, f=f: e.tensor_scalar(out=adat[:, l, f, :], in0=ps[:, 0:NSEG], scalar1=adab[:, l, f:f + 1], scalar2=None, op0=ALU.add),
                       reads=[psb, b_adab], writes=[b_adat])
        for l in range(DEPTH):
            for s in range(NSEG):
                P.emit("dve", lambda e, l=l, s=s: e.tensor_scalar(out=modA[:, l, 0, :, s], in0=adat[:, l, 8:16, s], scalar1=1.0, scalar2=None, op0=ALU.add),
                       reads=[b_adat], writes=[b_mod])
                P.emit("dve", lambda e, l=l, s=s: e.tensor_tensor(out=modA[:, l, 0, :, s], in0=modA[:, l, 0, :, s], in1=ng[:, l, :], op=ALU.mult),
                       reads=[b_mod, b_ng], writes=[b_mod])
                P.emit("dve", lambda e, l=l, s=s: e.tensor_copy(out=modA[:, l, 1, :, s], in_=adat[:, l, 0:8, s]), reads=[b_adat, b_mod], writes=[b_mod])
                P.emit("dve", lambda e, l=l, s=s: e.tensor_copy(out=modA[:, l, 2, :, s], in_=adat[:, l, 16:24, s]), reads=[b_adat, b_mod], writes=[b_mod])

        xin = Rot(es, nc, "xin", 2, [128, D], F32)
        pxt = Rot(es, nc, "pxt", 2, [128, 512], F32, psum=True)
        xst = Rot(es, nc, "xst", 2, [128, 8, 128], F32)
        for tt_ in range(NT):
            xt, xb = xin.next()
            P.dma("sp", xt[:], x_in[tt_ * 128:(tt_ + 1) * 128, :], writes=[xb])
            st, sb_ = xst.next()
            for h in range(2):
                ps, psb = pxt.next()
                for j in range(4):
                    f = h * 4 + j
                    P.emit("pe", lambda e, ps=ps, xt=xt, f=f, j=j: e.matmul(ps[:, j * 128:(j + 1) * 128], lhsT=xt[:, f * 128:(f + 1) * 128], rhs=ident_f[:], start=True, stop=True),
                           reads=[xb, b_idf], writes=[psb])
                P.emit("act", lambda e, ps=ps, st=st, h=h: e.activation(out=st[:, h * 4:(h + 1) * 4, :], in_=ps[:].rearrange("p (j t) -> p j t", j=4), func=AF.Copy),
                       reads=[psb], writes=[sb_])
            P.dma("pool", XT[:, :, tt_ * 128:(tt_ + 1) * 128].rearrange("f p t -> p f t"), st[:], reads=[sb_], writes=[bX])
        P.flush()

    def norm_phase(es_outer, l, s, hm, b_hm):
        with ExitStack() as es:
            _norm_phase(es, l, s, hm, b_hm)
            P.flush()

    def _norm_phase(es, l, s, hm, b_hm):
        xb_r = Rot(es, nc, "nx", 2, [128, 8, 512], F32)
        sq_r = Rot(es, nc, "nsq", 2, [128, 512], F32)
        ps_r = Rot(es, nc, "nps", 2, [128, 512], F32, psum=True)
        rs_r = Rot(es, nc, "nrs", 2, [128, 512], F32)
        tm_r = Rot(es, nc, "ntm", 3, [128, 512], F32)
        for tb in range(SEG // 512):
            t0 = s * SEG + tb * 512
            xt, xb = xb_r.next()
            P.dma("sp", xt[:], XT[:, :, t0:t0 + 512].rearrange("f p t -> p f t"), reads=[bX], writes=[xb])
            ps, psb = ps_r.next()
            for f in range(8):
                sq, sqb = sq_r.next()
                P.emit("act", lambda e, sq=sq, xt=xt, f=f: e.activation(out=sq[:], in_=xt[:, f, :], func=AF.Square), reads=[xb], writes=[sqb])
                P.emit("pe", lambda e, ps=ps, sq=sq, f=f: e.matmul(ps[:], lhsT=ones_f[:], rhs=sq[:], start=(f == 0), stop=(f == 7)),
                       reads=[sqb, b_onf], writes=[psb])
            rs, rsb = rs_r.next()
            P.emit("act", lambda e, rs=rs, ps=ps: e.activation(out=rs[:], in_=ps[:], func=AF.Sqrt, bias=1e-6, scale=1.0 / D), reads=[psb], writes=[rsb])
            P.emit("dve", lambda e, rs=rs: e.reciprocal(out=rs[:], in_=rs[:]), reads=[rsb], writes=[rsb])
            for f in range(8):
                tm, tmb = tm_r.next()
                P.emit("dve", lambda e, tm=tm, xt=xt, rs=rs, f=f: e.tensor_tensor(out=tm[:], in0=xt[:, f, :], in1=rs[:], op=ALU.mult), reads=[xb, rsb], writes=[tmb])
                if l < DEPTH:
                    P.emit("act", lambda e, tm=tm, f=f, tb=tb: e.activation(out=hm[:, f, tb * 512:(tb + 1) * 512], in_=tm[:], func=AF.Identity,
                                                                       bias=modA[:, l, 1, f, s:s + 1], scale=modA[:, l, 0, f, s:s + 1]),
                           reads=[tmb, b_mod], writes=[b_hm])

    def load_w_bf16(es, name, w_ap, ncol, wb, b_wb, rot32):
        for k in range(8):
            w32, b32 = rot32.next()
            P.dma("sp", w32[:, 0:ncol], w_ap[0:1, k * 128:(k + 1) * 128, :].rearrange("o p j -> p (o j)"), writes=[b32])
            P.emit("pool", lambda e, w32=w32, k=k: e.tensor_copy(out=wb[:, k, :], in_=w32[:, 0:ncol]), reads=[b32], writes=[b_wb])

    def outproj_phase(l, w_ap):
        with ExitStack() as es:
            wb, b_wb = one(es, nc, "ow", [128, 8, D], BF16)
            r32 = Rot(es, nc, "ow32", 2, [128, D], F32)
            load_w_bf16(es, "ow", w_ap, D, wb, b_wb, r32)
            yb_r = Rot(es, nc, "oy", 2, [128, 8, 512], BF16)
            xb_r = Rot(es, nc, "ox", 2, [128, 8, 512], F32)
            ps_r = Rot(es, nc, "ops", 3, [128, 512], F32, psum=True)
            for tb in range(N // 512):
                s = (tb * 512) // SEG
                t0 = tb * 512
                yt, yb = yb_r.next()
                P.dma("sp", yt[:], YT[:, :, t0:t0 + 512].rearrange("f p t -> p f t"), reads=[bYT], writes=[yb])
                xt, xb = xb_r.next()
                P.dma("sp", xt[:], XT[:, :, t0:t0 + 512].rearrange("f p t -> p f t"), reads=[bX], writes=[xb])
                for f in range(8):
                    ps, psb = ps_r.next()
                    for k in range(8):
                        P.emit("pe", lambda e, ps=ps, k=k, f=f, yt=yt: e.matmul(ps[:], lhsT=wb[:, k, f * 128:(f + 1) * 128], rhs=yt[:, k, :], start=(k == 0), stop=(k == 7)),
                               reads=[b_wb, yb], writes=[psb])
                    P.emit("dve", lambda e, ps=ps, xt=xt, f=f, s=s: e.scalar_tensor_tensor(out=xt[:, f, :], in0=ps[:], scalar=modA[:, l, 2, f, s:s + 1], in1=xt[:, f, :],
                                                                                  op0=ALU.mult, op1=ALU.add),
                           reads=[psb, xb, b_mod], writes=[xb])
                P.dma("pool", XT[:, :, t0:t0 + 512].rearrange("f p t -> p f t"), xt[:], reads=[xb], writes=[bX])
            P.flush()

    def attn_layer(l, la):
        for s in range(NSEG):
            with ExitStack() as es:
                t1_r = Rot(es, nc, "at1", 2, [128, 512], F32)
                t2_r = Rot(es, nc, "at2", 2, [128, 512], F32)
                hm, b_hm = one(es, nc, "hm", [128, 8, SEG], BF16)
                norm_phase(es, l, s, hm, b_hm)
                w32 = Rot(es, nc, "aw32", 2, [128, 8, 128], F32)
                wbr = Rot(es, nc, "awb", 3, [128, 8, 128], BF16)
                ps_r = Rot(es, nc, "aps", 4, [128, 512], F32, psum=True)
                ps2_r = Rot(es, nc, "aps2", 2, [128, 512], F32, psum=True)
                qb_r = Rot(es, nc, "aqb", 3, [128, 512], BF16)
                ropeS, b_ropeS = one(es, nc, "ropeS", [128, 2, SEG], F32)
                P.dma("sp", ropeS[:], ROPE[:, :, s * SEG:(s + 1) * SEG].rearrange("c p t -> p c t"), reads=[bROPE], writes=[b_ropeS])
                ob_r = Rot(es, nc, "aob", 4, [128, 512], BF16)
                ov_r = Rot(es, nc, "aov", 3, [128, 4, 130], BF16)
                for i in range(3):
                    P.emit("pool", lambda e, i=i: e.memset(ov_r.t[i][:], 1.0), writes=[ov_r.b[i]])
                zo_r = Rot(es, nc, "azo", 3, [128, 512], F32)
                slabs = [sl_ for sl_ in range(80) if DBG.get("kinds") is None or (9 if sl_ * 128 // D == 9 else (sl_ * 128 // D) % 3) in DBG["kinds"]]

                def load_w(slab):
                    c0_ = slab * 128
                    wt, wtb = w32.next()
                    P.dma("sp", wt[:], attn_w_in[la:la + 1, :, c0_:c0_ + 128].rearrange("o (k p) j -> p (o k) j", p=128), writes=[wtb])
                    wb, wbb = wbr.next()
                    P.emit("pool", lambda e, wb=wb, wt=wt: e.tensor_copy(out=wb[:], in_=wt[:]), reads=[wtb], writes=[wbb])
                    return wb, wbb
                nxt = load_w(slabs[0]) if slabs else None
                for si, slab in enumerate(slabs):
                    col0 = slab * 128
                    kind = col0 // D
                    sl8 = slab % 8
                    wb, wbb = nxt
                    if si + 1 < len(slabs):
                        nxt = load_w(slabs[si + 1])
                    if kind < 9 and kind % 3 == 2:
                        g = kind // 3
                        for t4 in range(SEG // 512):
                            ps, psb = ps_r.next()
                            for j in range(4):
                                tk = t4 * 4 + j
                                for k in range(8):
                                    P.emit("pe", lambda e, ps=ps, j=j, k=k, tk=tk, wb=wb: e.matmul(ps[:, j * 128:(j + 1) * 128], lhsT=hm[:, k, tk * 128:(tk + 1) * 128], rhs=wb[:, k, :],
                                                                                          start=(k == 0), stop=(k == 7)),
                                           reads=[b_hm, wbb], writes=[psb])
                            ov, ovb = ov_r.next()
                            P.emit("act", lambda e, ov=ov, ps=ps: e.activation(out=ov[:, :, 0:130].rearrange("p j (h c) -> p j h c", h=2)[:, :, :, 0:64],
                                                                          in_=ps[:].rearrange("p (j h c) -> p j h c", j=4, h=2), func=AF.Copy), reads=[psb], writes=[ovb])
                            r0 = s * SEG + t4 * 512
                            P.dma("sp", VV[g:g + 1, r0:r0 + 512, sl8, :].rearrange("o (j p) c -> p (o j) c", p=128), ov[:], reads=[ovb], writes=[bVV])
                        continue
                    for tb in range(SEG // 512):
                        t0 = s * SEG + tb * 512
                        ps, psb = ps_r.next()
                        for k in range(8):
                            P.emit("pe", lambda e, ps=ps, k=k, tb=tb, wb=wb: e.matmul(ps[:], lhsT=wb[:, k, :], rhs=hm[:, k, tb * 512:(tb + 1) * 512], start=(k == 0), stop=(k == 7)),
                                   reads=[b_hm, wbb], writes=[psb])
                        if kind == 9:
                            zo, zob = zo_r.next()
                            P.emit("act", lambda e, zo=zo, ps=ps: e.activation(out=zo[:], in_=ps[:], func=AF.Silu), reads=[psb], writes=[zob])
                            P.dma("sp", ZS[sl8, :, t0:t0 + 512], zo[:], reads=[zob], writes=[bZS])
                            continue
                        g = kind // 3
                        kq = DBG.get("kq", 9)
                        qb, qbb = qb_r.next()
                        P.emit("act", lambda e, qb=qb, ps=ps: e.activation(out=qb[:], in_=ps[:], func=AF.Copy), reads=[psb], writes=[qbb])
                        ob, obb = ob_r.next()
                        if kq >= 3:
                            ps2, ps2b = ps2_r.next()
                            P.emit("pe", lambda e, ps2=ps2, qb=qb: e.matmul(ps2[:], lhsT=perm_b[:], rhs=qb[:], start=True, stop=True), reads=[qbb, b_perm], writes=[ps2b])
                        rp, rpb = ropeS[:, :, tb * 512:(tb + 1) * 512], b_ropeS
                        if kq >= 4:
                            t1, t1b = t1_r.next()
                            t2, t2b = t2_r.next()
                            if kq in (4, 5, 9):
                                P.emit("dve", lambda e, t1=t1, ps=ps, rp=rp: e.tensor_tensor(out=t1[:], in0=ps[:], in1=rp[:, 0, :], op=ALU.mult), reads=[psb, rpb, qbb], writes=[t1b])
                            else:
                                P.emit("dve", lambda e, t1=t1, qb=qb, rp=rp: e.tensor_tensor(out=t1[:], in0=qb[:], in1=rp[:, 0, :], op=ALU.mult), reads=[qbb, rpb], writes=[t1b])
                            if kq in (4, 6, 9):
                                P.emit("dve", lambda e, t2=t2, ps2=ps2, rp=rp: e.tensor_tensor(out=t2[:], in0=ps2[:], in1=rp[:, 1, :], op=ALU.mult), reads=[ps2b, rpb], writes=[t2b])
                            else:
                                P.emit("dve", lambda e, t2=t2, qb=qb, rp=rp: e.tensor_tensor(out=t2[:], in0=qb[:], in1=rp[:, 1, :], op=ALU.mult), reads=[qbb, rpb], writes=[t2b])
                            P.emit("dve", lambda e, ob=ob, t1=t1, t2=t2: e.tensor_tensor(out=ob[:], in0=t1[:], in1=t2[:], op=ALU.add), reads=[t1b, t2b], writes=[obb])
                        else:
                            P.emit("dve", lambda e, ob=ob, qb=qb: e.tensor_copy(out=ob[:], in_=qb[:]), reads=[qbb], writes=[obb])
                        dst = (QT if kind % 3 == 0 else KT)
                        dbuf = (bQT if kind % 3 == 0 else bKT)
                        P.dma("sp", dst[g, sl8, :, t0:t0 + 512], ob[:], reads=[obb], writes=[dbuf])
                P.flush()

        if DBG["stop"] == "proj":
            return
        with ExitStack() as es:
            NK = 25
            q_r = Rot(es, nc, "cq", 2, [128, 3, 2, 128], BF16)
            for i in range(2):
                P.emit("pool", lambda e, i=i: e.memset(q_r.t[i][:], 0.0), writes=[q_r.b[i]])
            k_r = Rot(es, nc, "ck", 2, [128, NK * 128], BF16)
            v_r = Rot(es, nc, "cv", 2, [128, NK, 130], BF16)
            tp_r = Rot(es, nc, "ctp", 2, [128, 512], F32, psum=True)
            s_r = Rot(es, nc, "cs", 3, [128, 512], F32, psum=True)
            o_r = Rot(es, nc, "co", 2, [128, 512], F32, psum=True)
            p_r = Rot(es, nc, "cp", 4, [128, 512], BF16)
            rc_r = Rot(es, nc, "crc", 2, [128, 128], F32)
            on_r = Rot(es, nc, "con", 2, [128, 128], F32)
            z_r = Rot(es, nc, "cz", 2, [128, 8, 128], F32)
            y_r = Rot(es, nc, "cy", 2, [128, 8, 128], BF16)
            reach = (1, 2, 8)
            for qt in range(NT):
                seg = qt // (SEG // 128)
                q0 = qt * 128
                zt, zb = z_r.next()
                P.dma("sp", zt[:], ZS[:, :, q0:q0 + 128].rearrange("f p t -> p f t"), reads=[bZS], writes=[zb])
                yt, yb = y_r.next()
                for sl in range(8):
                    qtile, qb_ = q_r.next()
                    for h2 in range(2):
                        P.dma("sp", qtile[h2 * 64:(h2 + 1) * 64, :, h2, :], QT[:, sl, h2 * 64:(h2 + 1) * 64, q0:q0 + 128].rearrange("g p t -> p g t"), reads=[bQT], writes=[qb_])
                    ktile, kb_ = k_r.next()
                    vtile, vb_ = v_r.next()
                    tiles = []
                    off = 0
                    for g in range(3):
                        lo = max(0, qt - reach[g])
                        hi = min(NT - 1, qt + reach[g])
                        n = hi - lo + 1
                        if not DBG.get("nok"):
                            P.dma("sp", ktile[:, off * 128:(off + n) * 128], KT[g, sl, :, lo * 128:(hi + 1) * 128], reads=[bKT], writes=[kb_])
                        P.dma("sp", vtile[:, off:off + n, :], VV[g:g + 1, lo * 128:(hi + 1) * 128, sl, :].rearrange("o (k p) c -> p (o k) c", p=128), reads=[bVV], writes=[vb_])
                        for kt_abs in range(lo, hi + 1):
                            j = kt_abs - qt
                            mi = 3 * g + (0 if j == -reach[g] else (2 if j == reach[g] else 1))
                            cross = (kt_abs // (SEG // 128)) != seg
                            tiles.append((g, off + (kt_abs - lo), mi, cross))
                        off += n
                    ot, ob_ = o_r.next()
                    batches = []
                    i_ = 0
                    while i_ < len(tiles):
                        if i_ + 1 < len(tiles) and tiles[i_][3] == tiles[i_ + 1][3]:
                            batches.append([tiles[i_], tiles[i_ + 1]])
                            i_ += 2
                        else:
                            batches.append([tiles[i_]])
                            i_ += 1

                    def emit_qk(bi, ktile=ktile, qtile=qtile, kb_=kb_, qb_=qb_, batches=batches):
                        st, sb_ = s_r.next()
                        for u, (g, ix, mi, cross) in enumerate(batches[bi]):
                            c0 = u * 256
                            P.emit("pe", lambda e, st=st, ix=ix, g=g, c0=c0, u=u: e.matmul(st[:, c0:c0 + 256], lhsT=ktile[:, ix * 128:(ix + 1) * 128], rhs=qtile[:, g, :, :], start=(u == 0), stop=False),
                                   reads=[kb_, qb_], writes=[sb_])
                        nb = len(batches[bi])
                        for u, (g, ix, mi, cross) in enumerate(batches[bi]):
                            c0 = u * 256
                            P.emit("pe", lambda e, st=st, mi=mi, c0=c0, u=u, nb=nb: e.matmul(st[:, c0:c0 + 256], lhsT=ident_b[:], rhs=masks[:, mi:mi + 1, :].to_broadcast([128, 2, 128]), start=False, stop=(u == nb - 1)),
                                   reads=[b_idb, b_masks], writes=[sb_])
                        return st, sb_
                    pending = emit_qk(0)
                    nmm = 0
                    tot_mm = 2 * len(tiles)
                    for bi, batch in enumerate(batches):
                        st, sb_ = pending
                        if bi + 1 < len(batches):
                            pending = emit_qk(bi + 1)
                        w = 256 * len(batch)
                        pt, pb_ = p_r.next()
                        if batch[0][3]:
                            P.emit("act", lambda e, pt=pt, st=st, w=w: e.activation(out=pt[:, 0:w], in_=st[:, 0:w], func=AF.Exp, bias=flag[:, 1:2], scale=0.125),
                                   reads=[sb_, b_flag], writes=[pb_])
                        else:
                            P.emit("act", lambda e, pt=pt, st=st, w=w: e.activation(out=pt[:, 0:w], in_=st[:, 0:w], func=AF.Exp, scale=0.125), reads=[sb_], writes=[pb_])
                        for u, (g, ix, mi, cross) in enumerate(batch):
                            for hh in range(2):
                                c0 = u * 256 + hh * 128
                                first = nmm == 0
                                nmm += 1
                                last = nmm == tot_mm
                                P.emit("pe", lambda e, ot=ot, vtile=vtile, pt=pt, ix=ix, hh=hh, c0=c0, first=first, last=last: e.matmul(
                                    ot[:, hh * 65:(hh + 1) * 65], lhsT=pt[:, c0:c0 + 128], rhs=vtile[:, ix, hh * 65:(hh + 1) * 65], start=first, stop=last),
                                    reads=[vb_, pb_], writes=[ob_])
                    rc, rcb = rc_r.next()
                    P.emit("dve", lambda e, rc=rc, ot=ot: e.reciprocal(out=rc[:, 0:2], in_=ot[:, 64:130:65]), reads=[ob_], writes=[rcb])
                    on, onb = on_r.next()
                    for hh in range(2):
                        P.emit("dve", lambda e, on=on, ot=ot, rc=rc, hh=hh: e.tensor_scalar(out=on[:, hh * 64:(hh + 1) * 64], in0=ot[:, hh * 65:hh * 65 + 64], scalar1=rc[:, hh:hh + 1], scalar2=None, op0=ALU.mult),
                               reads=[ob_, rcb, onb], writes=[onb])
                    tp, tpb = tp_r.next()
                    P.emit("pe", lambda e, tp=tp, on=on: e.matmul(tp[:, 0:128], lhsT=on[:], rhs=ident_f[:], start=True, stop=True), reads=[onb, b_idf], writes=[tpb])
                    P.emit("dve", lambda e, tp=tp, yt=yt, zt=zt, sl=sl: e.tensor_tensor(out=yt[:, sl, :], in0=tp[:, 0:128], in1=zt[:, sl, :], op=ALU.mult), reads=[tpb, zb], writes=[yb])
                P.dma("pool", YT[:, :, q0:q0 + 128].rearrange("f p t -> p f t"), yt[:], reads=[yb], writes=[bYT])
            P.flush()
        if DBG["stop"] == "core":
            return
        outproj_phase(l, attn_w_out[la:la + 1, :, :])

    def ssm_prep(lb):
        with ExitStack() as es:
            mg2, b_mg2 = one(es, nc, "mg2", [128, 2], F32)
            mq, b_mq = one(es, nc, "mq", [128, 8], F32)
            pid, b_pid = one(es, nc, "spid", [128, 1], I32)
            pq, b_pq = one(es, nc, "spq", [128, 1], I32)
            pqf, b_pqf = one(es, nc, "spqf", [128, 1], F32)
            P.emit("dve", lambda e: e.memset(mg2[:], 0.0), writes=[b_mg2])
            P.emit("dve", lambda e: e.memset(mg2[0:64, 0:1], 1.0), reads=[b_mg2], writes=[b_mg2])
            P.emit("dve", lambda e: e.memset(mg2[64:128, 1:2], 1.0), reads=[b_mg2], writes=[b_mg2])
            P.emit("pool", lambda e: e.iota(pid[:], pattern=[[0, 1]], base=0, channel_multiplier=1), writes=[b_pid])
            P.emit("dve", lambda e: e.tensor_single_scalar(out=pq[:], in_=pid[:], scalar=4, op=ALU.arith_shift_right), reads=[b_pid], writes=[b_pq])
            P.emit("dve", lambda e: e.tensor_copy(out=pqf[:], in_=pq[:]), reads=[b_pq], writes=[b_pqf])
            for qq in range(8):
                P.emit("dve", lambda e, qq=qq: e.tensor_single_scalar(out=mq[:, qq:qq + 1], in_=pqf[:], scalar=float(qq), op=ALU.is_equal), reads=[b_pqf], writes=[b_mq])
            psm = Rot(es, nc, "sps", 3, [128, 512], F32, psum=True)
            for d in range(2):
                ld = lb * 2 + d
                nat, b_nat = one(es, nc, f"nat{d}", [32, 3, 128], F32)
                ldt, b_ldt = one(es, nc, f"ldt{d}", [32, 2], F32)
                P.dma("sp", nat[:, 0, :], lam_re_i[lb, d].rearrange("(q a) p -> q (a p)", a=2), writes=[b_nat])
                P.dma("sp", nat[:, 1, :], lam_im_i[lb, d].rearrange("(q a) p -> q (a p)", a=2), writes=[b_nat], reads=[b_nat])
                P.dma("sp", ldt[:], log_dt_i[lb, d:d + 1, :].rearrange("o (q a) -> q (o a)", a=2), writes=[b_ldt])
                for a in range(2):
                    P.emit("dve", lambda e, a=a, nat=nat, ldt=ldt: e.tensor_scalar(out=nat[:, 2, a * 64:(a + 1) * 64], in0=nat[:, 0, a * 64:(a + 1) * 64], scalar1=0.0, scalar2=ldt[:, a:a + 1],
                                                                         op0=ALU.mult, op1=ALU.add), reads=[b_nat, b_ldt], writes=[b_nat])
                L, b_L = one(es, nc, f"L{d}", [128, 16, 32], F32)
                ps, psb = psm.next()
                for i in range(3):
                    P.emit("pe", lambda e, ps=ps, i=i, nat=nat: e.matmul(ps[:, i * 32:(i + 1) * 32], lhsT=nat[:, i, :], rhs=ident_f[0:32, 0:32], start=True, stop=True),
                           reads=[b_nat, b_idf], writes=[psb])
                P.emit("dve", lambda e, ps=ps, L=L: e.tensor_copy(out=L[:, 0:3, :], in_=ps[:, 0:96].rearrange("p (i q) -> p i q", i=3)), reads=[psb], writes=[b_L])
                Ki, b_Ki = one(es, nc, f"Ki{d}", [128, 32], I32)

                def dv(fn):
                    P.emit("dve", fn, reads=[b_L], writes=[b_L])

                def ac(fn):
                    P.emit("act", fn, reads=[b_L], writes=[b_L])
                dv(lambda e, L=L: e.tensor_scalar(out=L[:, 0, :], in0=L[:, 0, :], scalar1=-1e-4, scalar2=None, op0=ALU.min))
                ac(lambda e, L=L: e.activation(out=L[:, 3, :], in_=L[:, 2, :], func=AF.Exp))
                dv(lambda e, L=L: e.tensor_tensor(out=L[:, 4, :], in0=L[:, 0, :], in1=L[:, 3, :], op=ALU.mult))
                dv(lambda e, L=L: e.tensor_tensor(out=L[:, 5, :], in0=L[:, 1, :], in1=L[:, 3, :], op=ALU.mult))
                ac(lambda e, L=L: e.activation(out=L[:, 6, :], in_=L[:, 4, :], func=AF.Exp))
                dv(lambda e, L=L: e.tensor_scalar(out=L[:, 13, :], in0=L[:, 5, :], scalar1=1.0 / TWO_PI, scalar2=None, op0=ALU.mult))
                P.emit("dve", lambda e, L=L, Ki=Ki: e.tensor_copy(out=Ki[:], in_=L[:, 13, :]), reads=[b_L], writes=[b_Ki])
                P.emit("dve", lambda e, L=L, Ki=Ki: e.tensor_copy(out=L[:, 13, :], in_=Ki[:]), reads=[b_Ki, b_L], writes=[b_L])
                dv(lambda e, L=L: e.scalar_tensor_tensor(out=L[:, 14, :], in0=L[:, 13, :], scalar=-C1, in1=L[:, 5, :], op0=ALU.mult, op1=ALU.add))
                dv(lambda e, L=L: e.scalar_tensor_tensor(out=L[:, 14, :], in0=L[:, 13, :], scalar=-C2, in1=L[:, 14, :], op0=ALU.mult, op1=ALU.add))
                dv(lambda e, L=L: e.tensor_scalar(out=L[:, 7, :], in0=L[:, 14, :], scalar1=math.pi / 2, scalar2=None, op0=ALU.add))
                dv(lambda e, L=L: e.tensor_single_scalar(out=L[:, 8, :], in_=L[:, 7, :], scalar=math.pi, op=ALU.is_gt))
                dv(lambda e, L=L: e.scalar_tensor_tensor(out=L[:, 7, :], in0=L[:, 8, :], scalar=-TWO_PI, in1=L[:, 7, :], op0=ALU.mult, op1=ALU.add))
                dv(lambda e, L=L: e.tensor_copy(out=L[:, 8, :], in_=L[:, 14, :]))
                dv(lambda e, L=L: e.tensor_scalar(out=L[:, 7:9, :], in0=L[:, 7:9, :], scalar1=math.pi, scalar2=-math.pi, op0=ALU.min, op1=ALU.max))
                ac(lambda e, L=L: e.activation(out=L[:, 7:9, :], in_=L[:, 7:9, :], func=AF.Sin))
                dv(lambda e, L=L: e.tensor_tensor(out=L[:, 9, :], in0=L[:, 6, :], in1=L[:, 7, :], op=ALU.mult))
                dv(lambda e, L=L: e.tensor_tensor(out=L[:, 10, :], in0=L[:, 6, :], in1=L[:, 8, :], op=ALU.mult))
                dv(lambda e, L=L: e.tensor_scalar(out=L[:, 13, :], in0=L[:, 9, :], scalar1=-1.0, scalar2=None, op0=ALU.add))
                dv(lambda e, L=L: e.tensor_tensor(out=L[:, 14, :], in0=L[:, 0, :], in1=L[:, 0, :], op=ALU.mult))
                dv(lambda e, L=L: e.tensor_tensor(out=L[:, 15, :], in0=L[:, 1, :], in1=L[:, 1, :], op=ALU.mult))
                dv(lambda e, L=L: e.tensor_tensor(out=L[:, 14, :], in0=L[:, 14, :], in1=L[:, 15, :], op=ALU.add))
                dv(lambda e, L=L: e.reciprocal(out=L[:, 14, :], in_=L[:, 14, :]))
                dv(lambda e, L=L: e.tensor_tensor(out=L[:, 11, :], in0=L[:, 13, :], in1=L[:, 0, :], op=ALU.mult))
                dv(lambda e, L=L: e.tensor_tensor(out=L[:, 15, :], in0=L[:, 10, :], in1=L[:, 1, :], op=ALU.mult))
                dv(lambda e, L=L: e.tensor_tensor(out=L[:, 11, :], in0=L[:, 11, :], in1=L[:, 15, :], op=ALU.add))
                dv(lambda e, L=L: e.tensor_tensor(out=L[:, 11, :], in0=L[:, 11, :], in1=L[:, 14, :], op=ALU.mult))
                dv(lambda e, L=L: e.tensor_tensor(out=L[:, 12, :], in0=L[:, 10, :], in1=L[:, 0, :], op=ALU.mult))
                dv(lambda e, L=L: e.tensor_tensor(out=L[:, 15, :], in0=L[:, 13, :], in1=L[:, 1, :], op=ALU.mult))
                dv(lambda e, L=L: e.tensor_tensor(out=L[:, 12, :], in0=L[:, 12, :], in1=L[:, 15, :], op=ALU.subtract))
                dv(lambda e, L=L: e.tensor_tensor(out=L[:, 12, :], in0=L[:, 12, :], in1=L[:, 14, :], op=ALU.mult))
                P.emit("dve", lambda e, L=L, ld=ld: e.tensor_copy(out=PW[:, ld, :, 0, 0], in_=L[:, 9, :]), reads=[b_L, b_PW], writes=[b_PW])
                P.emit("dve", lambda e, L=L, ld=ld: e.tensor_copy(out=PW[:, ld, :, 0, 1], in_=L[:, 10, :]), reads=[b_L, b_PW], writes=[b_PW])
                for j in range(1, 12):
                    def pwop(fn):
                        P.emit("dve", fn, reads=[b_PW, b_L], writes=[b_PW, b_L])
                    pwop(lambda e, L=L, ld=ld, j=j: e.tensor_tensor(out=L[:, 13, :], in0=PW[:, ld, :, j - 1, 0], in1=PW[:, ld, :, j - 1, 0], op=ALU.mult))
                    pwop(lambda e, L=L, ld=ld, j=j: e.tensor_tensor(out=L[:, 15, :], in0=PW[:, ld, :, j - 1, 1], in1=PW[:, ld, :, j - 1, 1], op=ALU.mult))
                    pwop(lambda e, L=L, ld=ld, j=j: e.tensor_tensor(out=PW[:, ld, :, j, 0], in0=L[:, 13, :], in1=L[:, 15, :], op=ALU.subtract))
                    pwop(lambda e, L=L, ld=ld, j=j: e.tensor_tensor(out=L[:, 13, :], in0=PW[:, ld, :, j - 1, 0], in1=PW[:, ld, :, j - 1, 1], op=ALU.mult))
                    pwop(lambda e, L=L, ld=ld, j=j: e.tensor_scalar(out=PW[:, ld, :, j, 1], in0=L[:, 13, :], scalar1=2.0, scalar2=None, op0=ALU.mult))
                P.emit("dve", lambda e, ld=ld: e.tensor_scalar(out=PW[:, ld, :, :, 2], in0=PW[:, ld, :, :, 1], scalar1=-1.0, scalar2=None, op0=ALU.mult),
                       reads=[b_PW], writes=[b_PW])
                Bn, b_Bn = one(es, nc, f"Bn{d}", [128, 2, 32, 16], F32)
                Bb, b_Bb = one(es, nc, f"Bb{d}", [128, 2, 32, 16], F32)
                tmpB, b_tB = one(es, nc, f"tB{d}", [128, 32, 16], F32)
                P.dma("sp", Bn[:, 0, :, :], b_re_i[lb, d].rearrange("(q a) p c -> (a p) q c", a=2), writes=[b_Bn])
                P.dma("sp", Bn[:, 1, :, :], b_im_i[lb, d].rearrange("(q a) p c -> (a p) q c", a=2), writes=[b_Bn], reads=[b_Bn])
                crb = L[:, 11, :].unsqueeze(2).to_broadcast([128, 32, 16])
                cib = L[:, 12, :].unsqueeze(2).to_broadcast([128, 32, 16])

                def bop(fn):
                    P.emit("dve", fn, reads=[b_Bn, b_L, b_Bb, b_tB], writes=[b_Bb, b_tB])
                bop(lambda e, Bn=Bn, Bb=Bb, crb=crb: e.tensor_tensor(out=Bb[:, 0], in0=Bn[:, 0], in1=crb, op=ALU.mult))
                bop(lambda e, Bn=Bn, tmpB=tmpB, cib=cib: e.tensor_tensor(out=tmpB[:], in0=Bn[:, 1], in1=cib, op=ALU.mult))
                bop(lambda e, Bb=Bb, tmpB=tmpB: e.tensor_tensor(out=Bb[:, 0], in0=Bb[:, 0], in1=tmpB[:], op=ALU.subtract))
                bop(lambda e, Bn=Bn, Bb=Bb, crb=crb: e.tensor_tensor(out=Bb[:, 1], in0=Bn[:, 1], in1=crb, op=ALU.mult))
                bop(lambda e, Bn=Bn, tmpB=tmpB, cib=cib: e.tensor_tensor(out=tmpB[:], in0=Bn[:, 0], in1=cib, op=ALU.mult))
                bop(lambda e, Bb=Bb, tmpB=tmpB: e.tensor_tensor(out=Bb[:, 1], in0=Bb[:, 1], in1=tmpB[:], op=ALU.add))
                Cn, b_Cn = one(es, nc, f"Cn{d}", [128, 2, 8, 64], F32)
                P.dma("sp", Cn[:, 0], c_re_i[lb, d].rearrange("(k q) c p -> (q c) k p", k=8), writes=[b_Cn])
                P.dma("sp", Cn[:, 1], c_im_i[lb, d].rearrange("(k q) c p -> (q c) k p", k=8), writes=[b_Cn], reads=[b_Cn])
                inX, b_inX = one(es, nc, f"inX{d}", [128, 32, 128], F32)
                stg = Rot(es, nc, f"stg{d}", 2, [128, 4, 128], BF16)
                for which in range(4):
                    ri = which % 2
                    P.emit("pool", lambda e, inX=inX: e.memset(inX[:], 0.0), writes=[b_inX])
                    if which < 2:
                        for j4 in range(4):
                            for a in range(2):
                                c0 = j4 * 32 + a * 16
                                P.emit("dve", lambda e, inX=inX, Bb=Bb, j4=j4, a=a, c0=c0, ri=ri: e.tensor_scalar(
                                    out=inX[:, j4::4, c0:c0 + 16], in0=Bb[:, ri, j4::4, :], scalar1=mg2[:, a:a + 1], scalar2=None, op0=ALU.mult),
                                    reads=[b_Bb, b_mg2, b_inX], writes=[b_inX])
                    else:
                        for j4 in range(4):
                            for a in range(2):
                                P.emit("dve", lambda e, inX=inX, Cn=Cn, j4=j4, a=a, ri=ri: e.tensor_scalar(
                                    out=inX[:, j4::4, a * 64:(a + 1) * 64], in0=Cn[:, ri, :, :], scalar1=mq[:, j4 * 2 + a:j4 * 2 + a + 1], scalar2=None, op0=ALU.mult),
                                    reads=[b_Cn, b_mq, b_inX], writes=[b_inX])
                    for p4 in range(8):
                        ps, psb = psm.next()
                        for j in range(4):
                            pp = p4 * 4 + j
                            P.emit("pe", lambda e, ps=ps, j=j, pp=pp, inX=inX: e.matmul(ps[:, j * 128:(j + 1) * 128], lhsT=inX[:, pp, :], rhs=ident_f[:], start=True, stop=True),
                                   reads=[b_inX, b_idf], writes=[psb])
                        st, stb = stg.next()
                        sc = -1.0 if which == 3 else 1.0
                        P.emit("act", lambda e, st=st, ps=ps, sc=sc: e.activation(out=st[:], in_=ps[:].rearrange("p (j c) -> p j c", j=4), func=AF.Copy, scale=sc),
                               reads=[psb], writes=[stb])
                        P.dma("pool", TAB[lb, d, p4 * 4:(p4 + 1) * 4, :, which, :].rearrange("j p c -> p j c"), st[:], reads=[stb], writes=[bTAB])
            P.flush()

    def ssm_layer(l, lb):
        ssm_prep(lb)
        if DBG["stop"] == "prep":
            return
        for s in range(NSEG):
            with ExitStack() as es:
                hm, b_hm = one(es, nc, "hm", [128, 8, SEG], BF16)
                norm_phase(es, l, s, hm, b_hm)
                w32 = Rot(es, nc, "sw32", 2, [128, 8, 128], F32)
                wbr = Rot(es, nc, "swb", 2, [128, 8, 128], BF16)
                ps_r = Rot(es, nc, "sps", 3, [128, 512], F32, psum=True)
                ub_r = Rot(es, nc, "sub", 2, [128, 512], BF16)
                uf_r = Rot(es, nc, "suf", 3, [128, 512], F32)
                for slab in range(16):
                    col0 = slab * 128
                    f = slab % 8
                    wt, wtb = w32.next()
                    P.dma("sp", wt[:], ssm_w_in[lb:lb + 1, :, col0:col0 + 128].rearrange("o (k p) j -> p (o k) j", p=128), writes=[wtb])
                    wb, wbb = wbr.next()
                    P.emit("pool", lambda e, wb=wb, wt=wt: e.tensor_copy(out=wb[:], in_=wt[:]), reads=[wtb], writes=[wbb])
                    for tb in range(SEG // 512):
                        t0 = s * SEG + tb * 512
                        ps, psb = ps_r.next()
                        for k in range(8):
                            P.emit("pe", lambda e, ps=ps, k=k, tb=tb, wb=wb: e.matmul(ps[:], lhsT=wb[:, k, :], rhs=hm[:, k, tb * 512:(tb + 1) * 512], start=(k == 0), stop=(k == 7)),
                                   reads=[b_hm, wbb], writes=[psb])
                        uf, ufb = uf_r.next()
                        if slab < 8:
                            ub, ubb = ub_r.next()
                            P.emit("act", lambda e, ub=ub, ps=ps: e.activation(out=ub[:], in_=ps[:], func=AF.Copy), reads=[psb], writes=[ubb])
                            P.dma("sp", UT[f, :, t0:t0 + 512], ub[:], reads=[ubb], writes=[bUT])
                            P.emit("dve", lambda e, uf=uf, ps=ps, f=f: e.tensor_scalar(out=uf[:], in0=ps[:], scalar1=dvec[:, lb, f:f + 1], scalar2=None, op0=ALU.mult),
                                   reads=[psb, b_dvec, ubb], writes=[ufb])
                            P.dma("sp", Y0[f, :, t0:t0 + 512], uf[:], reads=[ufb], writes=[bY0])
                        else:
                            P.emit("act", lambda e, uf=uf, ps=ps: e.activation(out=uf[:], in_=ps[:], func=AF.Silu), reads=[psb], writes=[ufb])
                            P.dma("sp", ZS[f, :, t0:t0 + 512], uf[:], reads=[ufb], writes=[bZS])
                P.flush()

        if DBG["stop"] == "inproj":
            return
        with ExitStack() as es:
            ut_r = Rot(es, nc, "qu", 1, [128, N], BF16)
            ya_r = Rot(es, nc, "qy", 1, [128, N], F32)
            tb_r = Rot(es, nc, "qt", 4, [128, 4, 128], BF16)
            x_r = Rot(es, nc, "qx", 3, [128, 2, SEG], F32)
            xb_r = Rot(es, nc, "qxb", 1, [128, 2, SEG], BF16)
            pb_r = Rot(es, nc, "qpb", 4, [128, 512], F32, psum=True)
            pc_r = Rot(es, nc, "qpc", 3, [128, 512], F32, psum=True)
            fins = [one(es, nc, "qfin", [128, 2], F32) for _ in range(2)]
            injs = [one(es, nc, "qinj", [128, 4], F32) for _ in range(2)]
            g1_r = Rot(es, nc, "qg1", 2, [128, 512], F32)
            g2_r = Rot(es, nc, "qg2", 2, [128, 512], F32)
            gb_r = Rot(es, nc, "qgb", 2, [128, 512], BF16)
            LV = 12
            dirs = DBG.get("dirs", (0, 1))

            def bu_fill(tbt, tbb, ut, ub, s):
                X, Xb_ = x_r.next()
                for blk in range(SEG // 512):
                    c0 = s * SEG + blk * 512
                    for ri in range(2):
                        ps, psb = pb_r.next()
                        P.emit("pe", lambda e, ps=ps, tbt=tbt, ut=ut, ri=ri, c0=c0: e.matmul(ps[:], lhsT=tbt[:, ri, :], rhs=ut[:, c0:c0 + 512], start=True, stop=True),
                               reads=[tbb, ub], writes=[psb])
                        P.emit("act", lambda e, ps=ps, X=X, ri=ri, blk=blk: e.activation(out=X[:, ri, blk * 512:(blk + 1) * 512], in_=ps[:], func=AF.Copy),
                               reads=[psb], writes=[Xb_])
                return X, Xb_

            def inject(X, Xb_, d, pw, fin, b_fin, inj, b_inj):
                tcol = 0 if d == 0 else SEG - 1

                def io(fn):
                    P.emit("dve", fn, reads=[b_fin, b_inj, b_PW, b_flag, Xb_], writes=[b_inj, Xb_])
                io(lambda e: e.tensor_scalar(out=inj[:, 0:1], in0=fin[:, 0:1], scalar1=pw(0, 0), scalar2=None, op0=ALU.mult))
                io(lambda e: e.scalar_tensor_tensor(out=inj[:, 0:1], in0=fin[:, 1:2], scalar=pw(0, 2), in1=inj[:, 0:1], op0=ALU.mult, op1=ALU.add))
                io(lambda e: e.tensor_scalar(out=inj[:, 1:2], in0=fin[:, 1:2], scalar1=pw(0, 0), scalar2=None, op0=ALU.mult))
                io(lambda e: e.scalar_tensor_tensor(out=inj[:, 1:2], in0=fin[:, 0:1], scalar=pw(0, 1), in1=inj[:, 1:2], op0=ALU.mult, op1=ALU.add))
                for ri in range(2):
                    io(lambda e, ri=ri: e.scalar_tensor_tensor(out=X[:, ri, tcol:tcol + 1], in0=inj[:, ri:ri + 1], scalar=flag[:, 0:1],
                                                         in1=X[:, ri, tcol:tcol + 1], op0=ALU.mult, op1=ALU.add))

            def scan_ops(X, Xb_, d, pw):
                ops = []

                def cstep(dsl, ssl, j):
                    def so(fn):
                        ops.append(lambda fn=fn: P.emit("dve", fn, reads=[Xb_, b_PW], writes=[Xb_]))
                    so(lambda e: e.scalar_tensor_tensor(out=X[:, :, dsl], in0=X[:, :, ssl], scalar=pw(j, 0), in1=X[:, :, dsl], op0=ALU.mult, op1=ALU.add))
                    so(lambda e: e.scalar_tensor_tensor(out=X[:, 0, dsl], in0=X[:, 1, ssl], scalar=pw(j, 2), in1=X[:, 0, dsl], op0=ALU.mult, op1=ALU.add))
                    so(lambda e: e.scalar_tensor_tensor(out=X[:, 1, dsl], in0=X[:, 0, ssl], scalar=pw(j, 1), in1=X[:, 1, dsl], op0=ALU.mult, op1=ALU.add))
                if DBG.get("noscan"):
                    return ops
                for j in range(LV):
                    S_, h = 2 ** (j + 1), 2 ** j
                    if d == 0:
                        cstep(slice(S_ - 1, SEG, S_), slice(h - 1, SEG, S_), j)
                    else:
                        cstep(slice(0, SEG, S_), slice(h, SEG, S_), j)
                for j in range(LV - 2, -1, -1):
                    S_, h = 2 ** (j + 1), 2 ** j
                    cnt = SEG // S_ - 1
                    if d == 0:
                        cstep(slice(S_ + h - 1, S_ + h - 1 + (cnt - 1) * S_ + 1, S_), slice(S_ - 1, S_ - 1 + (cnt - 1) * S_ + 1, S_), j)
                    else:
                        cstep(slice(h, h + (cnt - 1) * S_ + 1, S_), slice(S_, S_ + (cnt - 1) * S_ + 1, S_), j)
                return ops

            def interleave(lists):
                n = max(len(l_) for l_ in lists)
                for i in range(n):
                    for l_ in lists:
                        if i < len(l_):
                            l_[i]()

            def cmat(X, Xb_, tbt, tbb, ya, yab, s):
                Xh, Xhb = xb_r.next()
                P.emit("act", lambda e, Xh=Xh, X=X: e.activation(out=Xh[:], in_=X[:], func=AF.Copy), reads=[Xb_], writes=[Xhb])
                for blk in range(SEG // 512):
                    c0 = s * SEG + blk * 512
                    ps, psb = pc_r.next()
                    for ri in range(2):
                        P.emit("pe", lambda e, ps=ps, tbt=tbt, Xh=Xh, ri=ri, blk=blk: e.matmul(ps[:], lhsT=tbt[:, 2 + ri, :], rhs=Xh[:, ri, blk * 512:(blk + 1) * 512], start=(ri == 0), stop=(ri == 1)),
                               reads=[tbb, Xhb], writes=[psb])
                    P.emit("dve", lambda e, ps=ps, ya=ya, c0=c0: e.tensor_tensor(out=ya[:, c0:c0 + 512], in0=ps[:], in1=ya[:, c0:c0 + 512], op=ALU.add),
                           reads=[psb, yab], writes=[yab])

            for kt in range(8):
                ut, ub = ut_r.next()
                P.dma("sp", ut[:], UT[kt, :, :], reads=[bUT], writes=[ub])
                ya, yab = ya_r.next()
                P.dma("sp", ya[:], Y0[kt, :, :], reads=[bY0], writes=[yab])
                for j4 in range(4):
                    pp = kt * 4 + j4
                    tbs, pws = {}, {}
                    for d in dirs:
                        ld = lb * 2 + d
                        tbt, tbb = tb_r.next()
                        P.dma("sp", tbt[:], TAB[lb, d, pp, :, :, :], reads=[bTAB], writes=[tbb])
                        tbs[d] = (tbt, tbb)
                        pws[d] = (lambda j, c, ld=ld, pp=pp: PW[:, ld, pp, j, c:c + 1])
                    first = {}
                    for d in dirs:
                        s = 0 if d == 0 else 1
                        X, Xb_ = bu_fill(tbs[d][0], tbs[d][1], ut, ub, s)
                        first[d] = (X, Xb_, s)
                    interleave([scan_ops(first[d][0], first[d][1], d, pws[d]) for d in dirs])
                    for d in dirs:
                        X, Xb_, s = first[d]
                        fcol = SEG - 1 if d == 0 else 0
                        fin, b_fin = fins[d]
                        P.emit("dve", lambda e, X=X, fcol=fcol, fin=fin: e.tensor_copy(out=fin[:], in_=X[:, :, fcol]), reads=[Xb_, b_fin], writes=[b_fin])
                    second = {}
                    for d in dirs:
                        X, Xb_, s = first[d]
                        cmat(X, Xb_, tbs[d][0], tbs[d][1], ya, yab, s)
                        s2 = 1 - s
                        X2, X2b = bu_fill(tbs[d][0], tbs[d][1], ut, ub, s2)
                        inject(X2, X2b, d, pws[d], fins[d][0], fins[d][1], injs[d][0], injs[d][1])
                        second[d] = (X2, X2b, s2)
                    interleave([scan_ops(second[d][0], second[d][1], d, pws[d]) for d in dirs])
                    for d in dirs:
                        X2, X2b, s2 = second[d]
                        cmat(X2, X2b, tbs[d][0], tbs[d][1], ya, yab, s2)
                for cb in range(N // 512):
                    sl = slice(cb * 512, (cb + 1) * 512)
                    if DBG.get("rawY"):
                        P.dma("pool", GT[kt, :, sl], ya[:, sl], reads=[yab], writes=[bGT])
                        continue
                    g1, g1b = g1_r.next()
                    g2, g2b = g2_r.next()
                    P.emit("act", lambda e, g1=g1, ya=ya, sl=sl: e.activation(out=g1[:], in_=ya[:, sl], func=AF.Square), reads=[yab], writes=[g1b])
                    P.emit("dve", lambda e, g1=g1: e.tensor_scalar(out=g1[:], in0=g1[:], scalar1=0.044715, scalar2=1.0, op0=ALU.mult, op1=ALU.add), reads=[g1b], writes=[g1b])
                    P.emit("dve", lambda e, g1=g1, ya=ya, sl=sl: e.tensor_tensor(out=g1[:], in0=g1[:], in1=ya[:, sl], op=ALU.mult), reads=[g1b, yab], writes=[g1b])
                    P.emit("act", lambda e, g1=g1: e.activation(out=g1[:], in_=g1[:], func=AF.Sigmoid, scale=1.5957691216057308), reads=[g1b], writes=[g1b])
                    P.emit("dve", lambda e, g1=g1, g2=g2, ya=ya, sl=sl: e.tensor_tensor(out=g2[:], in0=g1[:], in1=ya[:, sl], op=ALU.mult), reads=[g1b, yab], writes=[g2b])
                    gb, gbb = gb_r.next()
                    P.emit("act", lambda e, gb=gb, g2=g2: e.activation(out=gb[:], in_=g2[:], func=AF.Copy), reads=[g2b], writes=[gbb])
                    P.dma("pool", GT[kt, :, sl], g2[:], reads=[g2b], writes=[bGT])
                    P.dma("pool", GB[kt, :, sl], gb[:], reads=[gbb], writes=[bGB])
            P.flush()

        if DBG["stop"] == "scan":
            return
        with ExitStack() as es:
            wb, b_wb = one(es, nc, "gw", [128, 8, D], BF16)
            r32 = Rot(es, nc, "gw32", 2, [128, D], F32)
            load_w_bf16(es, "gw", ssm_w_glu[lb:lb + 1, :, :], D, wb, b_wb, r32)
            gb_r = Rot(es, nc, "gg", 2, [128, 8, 512], BF16)
            g32_r = Rot(es, nc, "gg32", 2, [128, 8, 512], F32)
            z_r = Rot(es, nc, "gz", 2, [128, 8, 512], F32)
            ps_r = Rot(es, nc, "gps", 3, [128, 512], F32, psum=True)
            sg_r = Rot(es, nc, "gsg", 3, [128, 512], F32)
            yo_r = Rot(es, nc, "gyo", 2, [128, 8, 512], BF16)
            for tb in range(N // 512):
                t0 = tb * 512
                gt, gtb = gb_r.next()
                P.dma("sp", gt[:], GB[:, :, t0:t0 + 512].rearrange("f p t -> p f t"), reads=[bGB], writes=[gtb])
                g32, g32b = g32_r.next()
                P.dma("sp", g32[:], GT[:, :, t0:t0 + 512].rearrange("f p t -> p f t"), reads=[bGT], writes=[g32b])
                zt, ztb = z_r.next()
                P.dma("sp", zt[:], ZS[:, :, t0:t0 + 512].rearrange("f p t -> p f t"), reads=[bZS], writes=[ztb])
                yo, yob = yo_r.next()
                for f in range(8):
                    ps, psb = ps_r.next()
                    for k in range(8):
                        P.emit("pe", lambda e, ps=ps, k=k, f=f, gt=gt: e.matmul(ps[:], lhsT=wb[:, k, f * 128:(f + 1) * 128], rhs=gt[:, k, :], start=(k == 0), stop=(k == 7)),
                               reads=[b_wb, gtb], writes=[psb])
                    sg, sgb = sg_r.next()
                    P.emit("act", lambda e, sg=sg, ps=ps: e.activation(out=sg[:], in_=ps[:], func=AF.Sigmoid), reads=[psb], writes=[sgb])
                    P.emit("dve", lambda e, sg=sg, g32=g32, f=f: e.tensor_tensor(out=sg[:], in0=sg[:], in1=g32[:, f, :], op=ALU.mult), reads=[sgb, g32b], writes=[sgb])
                    P.emit("dve", lambda e, sg=sg, zt=zt, yo=yo, f=f: e.tensor_tensor(out=yo[:, f, :], in0=sg[:], in1=zt[:, f, :], op=ALU.mult), reads=[sgb, ztb], writes=[yob])
                P.dma("pool", YT[:, :, t0:t0 + 512].rearrange("f p t -> p f t"), yo[:], reads=[yob], writes=[bYT])
            P.flush()
        outproj_phase(l, ssm_w_out[lb:lb + 1, :, :])

    for l in layers:
        if l % 2 == 0:
            attn_layer(l, l // 2)
        else:
            ssm_layer(l, l // 2)

    with ExitStack() as es:
        xb_r = Rot(es, nc, "fx", 2, [128, 8, 512], F32)
        sq_r = Rot(es, nc, "fsq", 2, [128, 512], F32)
        ps_r = Rot(es, nc, "fps", 2, [128, 512], F32, psum=True)
        rs_r = Rot(es, nc, "frs", 2, [128, 512], F32)
        pt_r = Rot(es, nc, "fpt", 4, [128, 512], F32, psum=True)
        os_r = Rot(es, nc, "fos", 2, [128, D], F32)
        for tb in range(N // 512):
            t0 = tb * 512
            xt, xb = xb_r.next()
            P.dma("sp", xt[:], XT[:, :, t0:t0 + 512].rearrange("f p t -> p f t"), reads=[bX], writes=[xb])
            ps, psb = ps_r.next()
            for f in range(8):
                sq, sqb = sq_r.next()
                P.emit("act", lambda e, sq=sq, xt=xt, f=f: e.activation(out=sq[:], in_=xt[:, f, :], func=AF.Square), reads=[xb], writes=[sqb])
                P.emit("pe", lambda e, ps=ps, sq=sq, f=f: e.matmul(ps[:], lhsT=ones_f[:], rhs=sq[:], start=(f == 0), stop=(f == 7)), reads=[sqb, b_onf], writes=[psb])
            rs, rsb = rs_r.next()
            P.emit("act", lambda e, rs=rs, ps=ps: e.activation(out=rs[:], in_=ps[:], func=AF.Sqrt, bias=1e-6, scale=1.0 / D), reads=[psb], writes=[rsb])
            P.emit("dve", lambda e, rs=rs: e.reciprocal(out=rs[:], in_=rs[:]), reads=[rsb], writes=[rsb])
            for f in range(8):
                P.emit("dve", lambda e, xt=xt, rs=rs, f=f: e.tensor_tensor(out=xt[:, f, :], in0=xt[:, f, :], in1=rs[:], op=ALU.mult), reads=[xb, rsb], writes=[xb])
                P.emit("dve", lambda e, xt=xt, f=f: e.tensor_scalar(out=xt[:, f, :], in0=xt[:, f, :], scalar1=fing[:, f:f + 1], scalar2=None, op0=ALU.mult), reads=[xb, b_fing], writes=[xb])
            for sub in range(4):
                ot, otb = os_r.next()
                for h in range(2):
                    pt, ptb = pt_r.next()
                    for j in range(4):
                        f = h * 4 + j
                        P.emit("pe", lambda e, pt=pt, xt=xt, f=f, j=j, sub=sub: e.matmul(pt[:, j * 128:(j + 1) * 128], lhsT=xt[:, f, sub * 128:(sub + 1) * 128], rhs=ident_f[:], start=True, stop=True),
                               reads=[xb, b_idf], writes=[ptb])
                    P.emit("act", lambda e, pt=pt, ot=ot, h=h: e.activation(out=ot[:, h * 512:(h + 1) * 512], in_=pt[:], func=AF.Copy), reads=[ptb], writes=[otb])
                r0 = t0 + sub * 128
                P.dma("pool", y_out[r0:r0 + 128, :], ot[:], reads=[otb], writes=[bOUT])
        P.flush()
    top.close()
    return nc, P


_CACHE = {}


def kernel(**inputs):
    x_prompt = np.asarray(inputs["x_prompt"], np.float32)
    x_sample = np.asarray(inputs["x_sample"], np.float32)
    c_prompt = np.asarray(inputs["c_prompt"], np.float32)
    c_sample = np.asarray(inputs["c_sample"], np.float32)
    if "nc" not in _CACHE:
        _CACHE["nc"] = build_program()[0]
    nc = _CACHE["nc"]
    shared = {k: np.ascontiguousarray(np.asarray(inputs[k], np.float32)) for k in (
        "norm_g", "ada_w", "ada_b", "attn_w_in", "attn_w_out", "ssm_w_in", "ssm_lam_re", "ssm_lam_im",
        "ssm_log_dt", "ssm_b_re", "ssm_b_im", "ssm_c_re", "ssm_c_im", "ssm_d", "ssm_w_glu", "ssm_w_out")}
    shared["final_norm_g"] = np.ascontiguousarray(np.asarray(inputs["final_norm_g"], np.float32).reshape(1, D))
    in_maps = []
    for core in range(8):
        c = core % 4
        m = dict(shared)
        if c < 2:
            m["x_in"] = np.ascontiguousarray(x_prompt[c])
            m["c_in"] = np.ascontiguousarray(np.stack([c_prompt[c], c_prompt[c]]))
            m["pos_in"] = np.arange(N, dtype=np.float32).reshape(1, N)
            fl = np.zeros((128, 2), np.float32)
            fl[:, 0] = 1.0
        else:
            a = 2 * (c - 2)
            m["x_in"] = np.ascontiguousarray(np.concatenate([x_sample[a], x_sample[a + 1]], axis=0))
            m["c_in"] = np.ascontiguousarray(np.stack([c_sample[a], c_sample[a + 1]]))
            m["pos_in"] = np.concatenate([np.arange(SEG), np.arange(SEG)]).astype(np.float32).reshape(1, N)
            fl = np.zeros((128, 2), np.float32)
            fl[:, 1] = NEGM
        m["flag_in"] = fl
        in_maps.append(m)
    res = run_bass_kernel_spmd(nc, in_maps, core_ids=list(range(8)))
    outs = [np.asarray(r["y_out"], np.float32) for r in res.results]
    y_prompt = np.stack([outs[0], outs[1]]).reshape(2, N, D)
    y_sample = np.stack([outs[2][:SEG], outs[2][SEG:], outs[3][:SEG], outs[3][SEG:]])
    return (y_prompt, y_sample)
```

```python
import math
import numpy as np
from contextlib import ExitStack
import concourse.bass as bass
import concourse.mybir as mybir
from concourse.bass_utils import run_bass_kernel_spmd

F32 = mybir.dt.float32
BF16 = mybir.dt.bfloat16
I32 = mybir.dt.int32
AF = mybir.ActivationFunctionType
ALU = mybir.AluOpType

D = 1024
NSEG = 2
SEG = 4096
N = NSEG * SEG
NT = N // 128
DEPTH = 4
NEGM = -30000.0
TWO_PI = 2.0 * math.pi
C1 = 6.28125
C2 = TWO_PI - C1


class Buf:
    __slots__ = ("w", "r")

    def __init__(self):
        self.w = None
        self.r = {}


class Prog:
    ENG = ("pe", "act", "dve", "pool", "sp")
    NDSEM = 10

    def __init__(self):
        self.nc = bass.Bass("TRN2", target_bir_lowering=False)
        nc = self.nc
        self.sem = {e: nc.alloc_semaphore("sem_" + e) for e in self.ENG}
        self.cnt = {e: 0 for e in self.ENG}
        self.seen = {e: {} for e in self.ENG}
        self.q = {e: [] for e in self.ENG}
        self.dsem, self.dval, self.drr = {}, {}, {}
        for qn in ("sp", "pool", "act"):
            self.dsem[qn] = [nc.alloc_semaphore(f"dsem_{qn}{i}") for i in range(self.NDSEM)]
            self.dval[qn] = [0] * self.NDSEM
            self.drr[qn] = 0
        self.ninst = 0

    def _deps(self, e, reads, writes, extra=()):
        deps = {}

        def need(key, sem, val):
            if key == "pe" and e == "pe":
                return
            cur = deps.get(key)
            if cur is None or cur[1] < val:
                deps[key] = (sem, val)

        for b in reads:
            if b.w is not None:
                need(*b.w)
        for b in writes:
            if b.w is not None:
                need(*b.w)
            for k, (s, v) in b.r.items():
                need(k, s, v)
        for ev in extra:
            need(*ev)
        seen = self.seen[e]
        for key, (sem, val) in deps.items():
            if seen.get(key, 0) < val:
                self.q[e].append(("wait", sem, val))
                seen[key] = val

    def _mark(self, ev, reads, writes):
        key, sem, val = ev
        for b in reads:
            b.r[key] = (sem, val)
        for b in writes:
            b.w = ev
            b.r = {}

    def emit(self, e, fn, reads=(), writes=()):
        self._deps(e, reads, writes)
        self.cnt[e] += 1
        self.q[e].append(("inst", fn, self.sem[e], 1))
        self._mark((e, self.sem[e], self.cnt[e]), reads, writes)
        self.ninst += 1

    def dma(self, qn, out, in_, reads=(), writes=(), slow=False):
        k = self.drr[qn]
        self.drr[qn] = (k + 1) % self.NDSEM
        sem = self.dsem[qn][k]
        key = f"d{qn}{k}"
        prev = self.dval[qn][k]
        extra = ((key, sem, prev),) if prev else ()
        self._deps(qn, reads, writes, extra)
        val = prev + 16
        self.dval[qn][k] = val
        kw = {"allow_slow_non_contiguous": True} if slow else {}
        self.q[qn].append(("inst", lambda eng, o=out, i=in_, kw=kw: eng.dma_start(out=o, in_=i, **kw), sem, 16))
        self._mark((key, sem, val), reads, writes)
        self.ninst += 1

    def flush(self):
        for qn in self.dsem:
            for k in range(self.NDSEM):
                if self.dval[qn][k]:
                    self.q["sp"].append(("wait", self.dsem[qn][k], self.dval[qn][k]))
        for e in self.ENG:
            if e != "sp" and self.cnt[e]:
                self.q["sp"].append(("wait", self.sem[e], self.cnt[e]))
        nc = self.nc
        with nc.Block() as block:
            def replay(e):
                items = self.q[e]

                def body(eng):
                    for it in items:
                        if it[0] == "wait":
                            eng.wait_ge(it[1], it[2])
                        else:
                            it[1](eng).then_inc(it[2], it[3])
                return body
            block.tensor(replay("pe"))
            block.scalar(replay("act"))
            block.vector(replay("dve"))
            block.gpsimd(replay("pool"))
            block.sync(replay("sp"))
        self.q = {e: [] for e in self.ENG}


DBG = {"stop": None}
_UID = [0]


def _uid(name):
    _UID[0] += 1
    return f"{name}_{_UID[0]}"


class Rot:
    def __init__(self, es, nc, name, n, shape, dt, psum=False):
        mk = nc.psum_tensor if psum else nc.sbuf_tensor
        self.t = [es.enter_context(mk(_uid(name), list(shape), dt)) for i in range(n)]
        self.b = [Buf() for _ in range(n)]
        self.i = 0

    def next(self):
        k = self.i
        self.i = (k + 1) % len(self.t)
        return self.t[k], self.b[k]


def one(es, nc, name, shape, dt, psum=False):
    mk = nc.psum_tensor if psum else nc.sbuf_tensor
    return es.enter_context(mk(_uid(name), list(shape), dt)), Buf()


def build_program(layers=None):
    if layers is None:
        layers = list(range(DEPTH))
    P = Prog()
    nc = P.nc

    def din(name, shape, dt=F32):
        return nc.dram_tensor(name, list(shape), dt, kind="ExternalInput").ap()

    def dscr(name, shape, dt):
        return nc.dram_tensor(name, list(shape), dt, kind="Internal").ap()

    x_in = din("x_in", [N, D])
    c_in = din("c_in", [NSEG, D])
    pos_in = din("pos_in", [1, N])
    flag_in = din("flag_in", [128, 2])
    norm_g = din("norm_g", [DEPTH, D])
    ada_w = din("ada_w", [DEPTH, D, 3 * D])
    ada_b = din("ada_b", [DEPTH, 3 * D])
    attn_w_in = din("attn_w_in", [2, D, 10 * D])
    attn_w_out = din("attn_w_out", [2, D, D])
    ssm_w_in = din("ssm_w_in", [2, D, 2 * D])
    lam_re_i = din("ssm_lam_re", [2, 2, 64, 64])
    lam_im_i = din("ssm_lam_im", [2, 2, 64, 64])
    log_dt_i = din("ssm_log_dt", [2, 2, 64])
    b_re_i = din("ssm_b_re", [2, 2, 64, 64, 16])
    b_im_i = din("ssm_b_im", [2, 2, 64, 64, 16])
    c_re_i = din("ssm_c_re", [2, 2, 64, 16, 64])
    c_im_i = din("ssm_c_im", [2, 2, 64, 16, 64])
    ssm_d = din("ssm_d", [2, D])
    ssm_w_glu = din("ssm_w_glu", [2, D, D])
    ssm_w_out = din("ssm_w_out", [2, D, D])
    fin_g = din("final_norm_g", [1, D])
    y_out = nc.dram_tensor("y_out", [N, D], F32, kind="ExternalOutput").ap()

    XT = dscr("XT", [8, 128, N], F32)
    ROPE = dscr("ROPE", [2, 128, N], F32)
    QT = dscr("QT", [3, 8, 128, N], BF16)
    KT = dscr("KT", [3, 8, 128, N], BF16)
    VV = dscr("VV", [3, N, 8, 130], BF16)
    ZS = dscr("ZS", [8, 128, N], F32)
    YT = dscr("YT", [8, 128, N], BF16)
    UT = dscr("UT", [8, 128, N], BF16)
    Y0 = dscr("Y0", [8, 128, N], F32)
    GT = nc.dram_tensor("GT", [8, 128, N], F32, kind="ExternalOutput").ap() if DBG.get("dumpGT") else dscr("GT", [8, 128, N], F32)
    GB = dscr("GB", [8, 128, N], BF16)
    TAB = dscr("TAB", [2, 2, 32, 128, 4, 128], BF16)
    bX, bROPE, bQT, bKT, bVV, bZS, bYT, bUT, bY0, bGT, bGB, bTAB, bOUT = [Buf() for _ in range(13)]

    top = ExitStack()
    ident_f, b_idf = one(top, nc, "ident_f", [128, 128], F32)
    ones_f, b_onf = one(top, nc, "ones_f", [128, 128], F32)
    ident_b, b_idb = one(top, nc, "ident_b", [128, 128], BF16)
    perm_b, b_perm = one(top, nc, "perm_b", [128, 128], BF16)
    masks, b_masks = one(top, nc, "masks", [128, 9, 128], BF16)
    Eo, b_Eo = one(top, nc, "Eo", [128, 2, 128], BF16)
    flag, b_flag = one(top, nc, "flag", [128, 2], F32)
    modA, b_mod = one(top, nc, "modA", [128, DEPTH, 3, 8, NSEG], F32)
    fing, b_fing = one(top, nc, "fing", [128, 8], F32)
    dvec, b_dvec = one(top, nc, "dvec", [128, 2, 8], F32)
    PW, b_PW = one(top, nc, "PW", [128, 4, 32, 12, 3], F32)

    with ExitStack() as es:
        P.emit("pool", lambda e: e.memset(ident_f[:], 0.0), writes=[b_idf])
        P.emit("pool", lambda e: e.affine_select(out=ident_f[:], in_=ident_f[:], pattern=[[-1, 128]],
                                                 compare_op=ALU.not_equal, fill=1.0, base=0, channel_multiplier=1),
               reads=[b_idf], writes=[b_idf])
        P.emit("dve", lambda e: e.tensor_copy(out=ident_b[:], in_=ident_f[:]), reads=[b_idf], writes=[b_idb])
        P.emit("dve", lambda e: e.memset(ones_f[:], 1.0), writes=[b_onf])
        pf, b_pf = one(es, nc, "pf", [128, 128], F32)
        P.emit("pool", lambda e: e.memset(pf[:], 0.0), writes=[b_pf])
        for blk in range(2):
            for half in range(2):
                c0 = blk * 64 + half * 32
                base = -(c0 + (32 if half == 0 else -32))
                P.emit("pool", lambda e, c0=c0, base=base: e.affine_select(
                    out=pf[:, c0:c0 + 32], in_=pf[:, c0:c0 + 32], pattern=[[-1, 32]],
                    compare_op=ALU.not_equal, fill=1.0, base=base, channel_multiplier=1),
                    reads=[b_pf], writes=[b_pf])
        P.emit("dve", lambda e: e.tensor_copy(out=perm_b[:], in_=pf[:]), reads=[b_pf], writes=[b_perm])
        P.emit("dve", lambda e: e.memset(Eo[:], 0.0), writes=[b_Eo])
        P.emit("dve", lambda e: e.memset(Eo[:, 0, 0:64], 1.0), writes=[b_Eo], reads=[b_Eo])
        P.emit("dve", lambda e: e.memset(Eo[:, 1, 64:128], 1.0), writes=[b_Eo], reads=[b_Eo])
        P.dma("sp", flag[:], flag_in, writes=[b_flag])

        Dm, b_Dm = one(es, nc, "Dm", [128, 128], I32)
        Df, b_Df = one(es, nc, "Df", [128, 128], F32)
        t_i, b_ti = one(es, nc, "t_i", [128, 128], I32)
        mm, b_mm = one(es, nc, "mm", [128, 6, 128], F32)
        mk, b_mk = one(es, nc, "mk", [128, 9, 128], F32)
        P.emit("pool", lambda e: e.iota(Dm[:], pattern=[[-1, 128]], base=128, channel_multiplier=1), writes=[b_Dm])
        P.emit("dve", lambda e: e.tensor_copy(out=Df[:], in_=Dm[:]), reads=[b_Dm], writes=[b_Df])
        for idx, msk in ((0, 15), (1, 3)):
            P.emit("dve", lambda e, msk=msk: e.tensor_single_scalar(out=t_i[:], in_=Dm[:], scalar=msk, op=ALU.bitwise_and),
                   reads=[b_Dm], writes=[b_ti])
            P.emit("dve", lambda e, idx=idx: e.tensor_single_scalar(out=mm[:, idx, :], in_=t_i[:], scalar=0, op=ALU.is_equal),
                   reads=[b_ti], writes=[b_mm])
        P.emit("dve", lambda e: e.tensor_single_scalar(out=mm[:, 2, :], in_=Df[:], scalar=128.0, op=ALU.is_ge), reads=[b_Df], writes=[b_mm])
        P.emit("dve", lambda e: e.tensor_single_scalar(out=mm[:, 3, :], in_=Df[:], scalar=128.0, op=ALU.is_le), reads=[b_Df], writes=[b_mm])
        P.emit("dve", lambda e: e.tensor_single_scalar(out=mk[:, 0, :], in_=Df[:], scalar=192.0, op=ALU.is_ge), reads=[b_Df], writes=[b_mk])
        P.emit("dve", lambda e: e.tensor_scalar(out=mk[:, 1, :], in0=Df[:], scalar1=64.0, scalar2=None, op0=ALU.is_ge), reads=[b_Df], writes=[b_mk])
        P.emit("dve", lambda e: e.tensor_single_scalar(out=mm[:, 4, :], in_=Df[:], scalar=192.0, op=ALU.is_le), reads=[b_Df], writes=[b_mm])
        P.emit("dve", lambda e: e.tensor_tensor(out=mk[:, 1, :], in0=mk[:, 1, :], in1=mm[:, 4, :], op=ALU.mult), reads=[b_mk, b_mm], writes=[b_mk])
        P.emit("dve", lambda e: e.tensor_single_scalar(out=mk[:, 2, :], in_=Df[:], scalar=64.0, op=ALU.is_le), reads=[b_Df], writes=[b_mk])
        for gi, mi in ((1, 1), (2, 0)):
            o = 3 * gi
            P.emit("dve", lambda e, o=o, mi=mi: e.tensor_tensor(out=mk[:, o, :], in0=mm[:, mi, :], in1=mm[:, 2, :], op=ALU.mult), reads=[b_mm], writes=[b_mk])
            P.emit("dve", lambda e, o=o, mi=mi: e.tensor_copy(out=mk[:, o + 1, :], in_=mm[:, mi, :]), reads=[b_mm], writes=[b_mk])
            P.emit("dve", lambda e, o=o, mi=mi: e.tensor_tensor(out=mk[:, o + 2, :], in0=mm[:, mi, :], in1=mm[:, 3, :], op=ALU.mult), reads=[b_mm], writes=[b_mk])
        P.emit("dve", lambda e: e.tensor_scalar(out=masks[:], in0=mk[:], scalar1=-1.0, scalar2=-NEGM, op0=ALU.add, op1=ALU.mult),
               reads=[b_mk], writes=[b_masks])

        invf, b_invf = one(es, nc, "invf", [128, 1], F32)
        pid, b_pid = one(es, nc, "pid", [128, 1], I32)
        pidf, b_pidf = one(es, nc, "pidf", [128, 1], F32)
        sgn, b_sgn = one(es, nc, "sgn", [128, 1], F32)
        P.emit("pool", lambda e: e.iota(pid[:], pattern=[[0, 1]], base=0, channel_multiplier=1), writes=[b_pid])
        P.emit("dve", lambda e: e.tensor_single_scalar(out=pid[:], in_=pid[:], scalar=31, op=ALU.bitwise_and), reads=[b_pid], writes=[b_pid])
        P.emit("dve", lambda e: e.tensor_copy(out=pidf[:], in_=pid[:]), reads=[b_pid], writes=[b_pidf])
        P.emit("act", lambda e: e.activation(out=invf[:], in_=pidf[:], func=AF.Exp, scale=-math.log(10000.0) / 32.0),
               reads=[b_pidf], writes=[b_invf])
        P.emit("dve", lambda e: e.memset(sgn[:], 1.0), writes=[b_sgn])
        for hb in range(2):
            P.emit("dve", lambda e, hb=hb: e.memset(sgn[hb * 64:hb * 64 + 32, :], -1.0), reads=[b_sgn], writes=[b_sgn])
        CH = 2048
        posb = Rot(es, nc, "posb", 2, [128, CH], F32)
        ang = Rot(es, nc, "ang", 2, [128, CH], F32)
        kf = Rot(es, nc, "kf", 2, [128, CH], F32)
        ki = Rot(es, nc, "ki", 2, [128, CH], I32)
        tro = Rot(es, nc, "tro", 2, [128, 2, CH], F32)
        for cb in range(N // CH):
            sl = slice(cb * CH, (cb + 1) * CH)
            pt, pb_ = posb.next()
            P.dma("sp", pt[:], pos_in[0:1, sl].to_broadcast([128, CH]), writes=[pb_])
            at, ab = ang.next()
            P.emit("dve", lambda e, at=at, pt=pt: e.tensor_scalar(out=at[:], in0=pt[:], scalar1=invf[:, 0:1], scalar2=None, op0=ALU.mult),
                   reads=[pb_, b_invf], writes=[ab])
            kt_, kb_ = kf.next()
            kit, kib = ki.next()
            P.emit("dve", lambda e, at=at, kt_=kt_: e.tensor_scalar(out=kt_[:], in0=at[:], scalar1=1.0 / TWO_PI, scalar2=None, op0=ALU.mult),
                   reads=[ab], writes=[kb_])
            P.emit("dve", lambda e, kit=kit, kt_=kt_: e.tensor_copy(out=kit[:], in_=kt_[:]), reads=[kb_], writes=[kib])
            P.emit("dve", lambda e, kit=kit, kt_=kt_: e.tensor_copy(out=kt_[:], in_=kit[:]), reads=[kib], writes=[kb_])
            P.emit("dve", lambda e, at=at, kt_=kt_: e.scalar_tensor_tensor(out=at[:], in0=kt_[:], scalar=-C1, in1=at[:], op0=ALU.mult, op1=ALU.add),
                   reads=[kb_, ab], writes=[ab])
            P.emit("dve", lambda e, at=at, kt_=kt_: e.scalar_tensor_tensor(out=at[:], in0=kt_[:], scalar=-C2, in1=at[:], op0=ALU.mult, op1=ALU.add),
                   reads=[kb_, ab], writes=[ab])
            tt, tb_ = tro.next()
            P.emit("dve", lambda e, at=at, tt=tt: e.tensor_scalar(out=tt[:, 0, :], in0=at[:], scalar1=math.pi / 2, scalar2=None, op0=ALU.add), reads=[ab], writes=[tb_])
            P.emit("dve", lambda e, at=at, tt=tt: e.tensor_single_scalar(out=tt[:, 1, :], in_=tt[:, 0, :], scalar=math.pi, op=ALU.is_gt), reads=[tb_], writes=[tb_])
            P.emit("dve", lambda e, at=at, tt=tt: e.scalar_tensor_tensor(out=tt[:, 0, :], in0=tt[:, 1, :], scalar=-TWO_PI, in1=tt[:, 0, :], op0=ALU.mult, op1=ALU.add), reads=[tb_], writes=[tb_])
            P.emit("dve", lambda e, at=at, tt=tt: e.tensor_copy(out=tt[:, 1, :], in_=at[:]), reads=[ab, tb_], writes=[tb_])
            P.emit("dve", lambda e, tt=tt: e.tensor_scalar(out=tt[:], in0=tt[:], scalar1=math.pi, scalar2=-math.pi, op0=ALU.min, op1=ALU.max),
                   reads=[tb_], writes=[tb_])
            P.emit("act", lambda e, tt=tt: e.activation(out=tt[:], in_=tt[:], func=AF.Sin), reads=[tb_], writes=[tb_])
            P.emit("dve", lambda e, tt=tt: e.tensor_scalar(out=tt[:, 1, :], in0=tt[:, 1, :], scalar1=sgn[:, 0:1], scalar2=None, op0=ALU.mult),
                   reads=[tb_, b_sgn], writes=[tb_])
            P.dma("pool", ROPE[:, :, sl].rearrange("c p t -> p c t"), tt[:], reads=[tb_], writes=[bROPE])

        cT, b_cT = one(es, nc, "cT", [128, 8, NSEG], F32)
        cs, b_cs = one(es, nc, "cs", [128, 8, NSEG], BF16)
        adab, b_adab = one(es, nc, "adab", [128, DEPTH, 24], F32)
        ng, b_ng = one(es, nc, "ng", [128, DEPTH, 8], F32)
        for s_ in range(NSEG):
            P.dma("sp", cT[:, :, s_], c_in[s_:s_ + 1, :].rearrange("o (k p) -> p (o k)", p=128), writes=[b_cT], reads=[b_cT], slow=True)
        for l_ in range(DEPTH):
            P.dma("sp", adab[:, l_, :], ada_b[l_:l_ + 1, :].rearrange("o (f p) -> p (o f)", p=128), writes=[b_adab], reads=[b_adab], slow=True)
            P.dma("sp", ng[:, l_, :], norm_g[l_:l_ + 1, :].rearrange("o (f p) -> p (o f)", p=128), writes=[b_ng], reads=[b_ng], slow=True)
        P.dma("sp", fing[:], fin_g.rearrange("o (f p) -> p (o f)", p=128), writes=[b_fing], slow=True)
        for l_ in range(2):
            P.dma("sp", dvec[:, l_, :], ssm_d[l_:l_ + 1, :].rearrange("o (f p) -> p (o f)", p=128), writes=[b_dvec], reads=[b_dvec], slow=True)
        P.emit("act", lambda e: e.activation(out=cs[:], in_=cT[:], func=AF.Silu), reads=[b_cT], writes=[b_cs])
        aw32 = Rot(es, nc, "aw32", 2, [128, 8, 128], F32)
        awb = Rot(es, nc, "awb", 2, [128, 8, 128], BF16)
        pada = Rot(es, nc, "pada", 2, [128, 512], F32, psum=True)
        adat, b_adat = one(es, nc, "adat", [128, DEPTH, 24, NSEG], F32)
        for l in range(DEPTH):
            for f in range(24):
                w32, wb32 = aw32.next()
                P.dma("sp", w32[:], ada_w[l:l + 1, :, f * 128:(f + 1) * 128].rearrange("o (k p) j -> p (o k) j", p=128), writes=[wb32])
                wb, wbb = awb.next()
                P.emit("pool", lambda e, wb=wb, w32=w32: e.tensor_copy(out=wb[:], in_=w32[:]), reads=[wb32], writes=[wbb])
                ps, psb = pada.next()
                for k in range(8):
                    P.emit("pe", lambda e, ps=ps, wb=wb, k=k: e.matmul(ps[:, 0:NSEG], lhsT=wb[:, k, :], rhs=cs[:, k, :], start=(k == 0), stop=(k == 7)),
                           reads=[wbb, b_cs], writes=[psb])
                P.emit("dve", lambda e, ps=ps, l=l, f=f: e.tensor_scalar(out=adat[:, l, f, :], in0=ps[:, 0:NSEG], scalar1=adab[:, l, f:f + 1], scalar2=None, op0=ALU.add),
                       reads=[psb, b_adab], writes=[b_adat])
        for l in range(DEPTH):
            for s in range(NSEG):
                P.emit("dve", lambda e, l=l, s=s: e.tensor_scalar(out=modA[:, l, 0, :, s], in0=adat[:, l, 8:16, s], scalar1=1.0, scalar2=None, op0=ALU.add),
                       reads=[b_adat], writes=[b_mod])
                P.emit("dve", lambda e, l=l, s=s: e.tensor_tensor(out=modA[:, l, 0, :, s], in0=modA[:, l, 0, :, s], in1=ng[:, l, :], op=ALU.mult),
                       reads=[b_mod, b_ng], writes=[b_mod])
                P.emit("dve", lambda e, l=l, s=s: e.tensor_copy(out=modA[:, l, 1, :, s], in_=adat[:, l, 0:8, s]), reads=[b_adat, b_mod], writes=[b_mod])
                P.emit("dve", lambda e, l=l, s=s: e.tensor_copy(out=modA[:, l, 2, :, s], in_=adat[:, l, 16:24, s]), reads=[b_adat, b_mod], writes=[b_mod])

        xin = Rot(es, nc, "xin", 2, [128, D], F32)
        pxt = Rot(es, nc, "pxt", 2, [128, 512], F32, psum=True)
        xst = Rot(es, nc, "xst", 2, [128, 8, 128], F32)
        for tt_ in range(NT):
            xt, xb = xin.next()
            P.dma("sp", xt[:], x_in[tt_ * 128:(tt_ + 1) * 128, :], writes=[xb])
            st, sb_ = xst.next()
            for h in range(2):
                ps, psb = pxt.next()
                for j in range(4):
                    f = h * 4 + j
                    P.emit("pe", lambda e, ps=ps, xt=xt, f=f, j=j: e.matmul(ps[:, j * 128:(j + 1) * 128], lhsT=xt[:, f * 128:(f + 1) * 128], rhs=ident_f[:], start=True, stop=True),
                           reads=[xb, b_idf], writes=[psb])
                P.emit("act", lambda e, ps=ps, st=st, h=h: e.activation(out=st[:, h * 4:(h + 1) * 4, :], in_=ps[:].rearrange("p (j t) -> p j t", j=4), func=AF.Copy),
                       reads=[psb], writes=[sb_])
            P.dma("pool", XT[:, :, tt_ * 128:(tt_ + 1) * 128].rearrange("f p t -> p f t"), st[:], reads=[sb_], writes=[bX])
        P.flush()

    def norm_phase(es_outer, l, s, hm, b_hm):
        with ExitStack() as es:
            _norm_phase(es, l, s, hm, b_hm)
            P.flush()

    def _norm_phase(es, l, s, hm, b_hm):
        xb_r = Rot(es, nc, "nx", 2, [128, 8, 512], F32)
        sq_r = Rot(es, nc, "nsq", 2, [128, 512], F32)
        ps_r = Rot(es, nc, "nps", 2, [128, 512], F32, psum=True)
        rs_r = Rot(es, nc, "nrs", 2, [128, 512], F32)
        tm_r = Rot(es, nc, "ntm", 3, [128, 512], F32)
        for tb in range(SEG // 512):
            t0 = s * SEG + tb * 512
            xt, xb = xb_r.next()
            P.dma("sp", xt[:], XT[:, :, t0:t0 + 512].rearrange("f p t -> p f t"), reads=[bX], writes=[xb])
            ps, psb = ps_r.next()
            for f in range(8):
                sq, sqb = sq_r.next()
                P.emit("act", lambda e, sq=sq, xt=xt, f=f: e.activation(out=sq[:], in_=xt[:, f, :], func=AF.Square), reads=[xb], writes=[sqb])
                P.emit("pe", lambda e, ps=ps, sq=sq, f=f: e.matmul(ps[:], lhsT=ones_f[:], rhs=sq[:], start=(f == 0), stop=(f == 7)),
                       reads=[sqb, b_onf], writes=[psb])
            rs, rsb = rs_r.next()
            P.emit("act", lambda e, rs=rs, ps=ps: e.activation(out=rs[:], in_=ps[:], func=AF.Sqrt, bias=1e-6, scale=1.0 / D), reads=[psb], writes=[rsb])
            P.emit("dve", lambda e, rs=rs: e.reciprocal(out=rs[:], in_=rs[:]), reads=[rsb], writes=[rsb])
            for f in range(8):
                tm, tmb = tm_r.next()
                P.emit("dve", lambda e, tm=tm, xt=xt, rs=rs, f=f: e.tensor_tensor(out=tm[:], in0=xt[:, f, :], in1=rs[:], op=ALU.mult), reads=[xb, rsb], writes=[tmb])
                if l < DEPTH:
                    P.emit("act", lambda e, tm=tm, f=f, tb=tb: e.activation(out=hm[:, f, tb * 512:(tb + 1) * 512], in_=tm[:], func=AF.Identity,
                                                                       bias=modA[:, l, 1, f, s:s + 1], scale=modA[:, l, 0, f, s:s + 1]),
                           reads=[tmb, b_mod], writes=[b_hm])

    def load_w_bf16(es, name, w_ap, ncol, wb, b_wb, rot32):
        for k in range(8):
            w32, b32 = rot32.next()
            P.dma("sp", w32[:, 0:ncol], w_ap[0:1, k * 128:(k + 1) * 128, :].rearrange("o p j -> p (o j)"), writes=[b32])
            P.emit("pool", lambda e, w32=w32, k=k: e.tensor_copy(out=wb[:, k, :], in_=w32[:, 0:ncol]), reads=[b32], writes=[b_wb])

    def outproj_phase(l, w_ap):
        with ExitStack() as es:
            wb, b_wb = one(es, nc, "ow", [128, 8, D], BF16)
            r32 = Rot(es, nc, "ow32", 2, [128, D], F32)
            load_w_bf16(es, "ow", w_ap, D, wb, b_wb, r32)
            yb_r = Rot(es, nc, "oy", 2, [128, 8, 512], BF16)
            xb_r = Rot(es, nc, "ox", 2, [128, 8, 512], F32)
            ps_r = Rot(es, nc, "ops", 3, [128, 512], F32, psum=True)
            for tb in range(N // 512):
                s = (tb * 512) // SEG
                t0 = tb * 512
                yt, yb = yb_r.next()
                P.dma("sp", yt[:], YT[:, :, t0:t0 + 512].rearrange("f p t -> p f t"), reads=[bYT], writes=[yb])
                xt, xb = xb_r.next()
                P.dma("sp", xt[:], XT[:, :, t0:t0 + 512].rearrange("f p t -> p f t"), reads=[bX], writes=[xb])
                for f in range(8):
                    ps, psb = ps_r.next()
                    for k in range(8):
                        P.emit("pe", lambda e, ps=ps, k=k, f=f, yt=yt: e.matmul(ps[:], lhsT=wb[:, k, f * 128:(f + 1) * 128], rhs=yt[:, k, :], start=(k == 0), stop=(k == 7)),
                               reads=[b_wb, yb], writes=[psb])
                    P.emit("dve", lambda e, ps=ps, xt=xt, f=f, s=s: e.scalar_tensor_tensor(out=xt[:, f, :], in0=ps[:], scalar=modA[:, l, 2, f, s:s + 1], in1=xt[:, f, :],
                                                                                  op0=ALU.mult, op1=ALU.add),
                           reads=[psb, xb, b_mod], writes=[xb])
                P.dma("pool", XT[:, :, t0:t0 + 512].rearrange("f p t -> p f t"), xt[:], reads=[xb], writes=[bX])
            P.flush()

    def attn_layer(l, la):
        for s in range(NSEG):
            with ExitStack() as es:
                t1_r = Rot(es, nc, "at1", 2, [128, 512], F32)
                t2_r = Rot(es, nc, "at2", 2, [128, 512], F32)
                hm, b_hm = one(es, nc, "hm", [128, 8, SEG], BF16)
                norm_phase(es, l, s, hm, b_hm)
                w32 = Rot(es, nc, "aw32", 2, [128, 8, 128], F32)
                wbr = Rot(es, nc, "awb", 3, [128, 8, 128], BF16)
                ps_r = Rot(es, nc, "aps", 4, [128, 512], F32, psum=True)
                ps2_r = Rot(es, nc, "aps2", 2, [128, 512], F32, psum=True)
                qb_r = Rot(es, nc, "aqb", 3, [128, 512], BF16)
                ropeS, b_ropeS = one(es, nc, "ropeS", [128, 2, SEG], F32)
                P.dma("sp", ropeS[:], ROPE[:, :, s * SEG:(s + 1) * SEG].rearrange("c p t -> p c t"), reads=[bROPE], writes=[b_ropeS])
                ob_r = Rot(es, nc, "aob", 4, [128, 512], BF16)
                ov_r = Rot(es, nc, "aov", 3, [128, 4, 130], BF16)
                for i in range(3):
                    P.emit("pool", lambda e, i=i: e.memset(ov_r.t[i][:], 1.0), writes=[ov_r.b[i]])
                zo_r = Rot(es, nc, "azo", 3, [128, 512], F32)
                slabs = [sl_ for sl_ in range(80) if DBG.get("kinds") is None or (9 if sl_ * 128 // D == 9 else (sl_ * 128 // D) % 3) in DBG["kinds"]]

                def load_w(slab):
                    c0_ = slab * 128
                    wt, wtb = w32.next()
                    P.dma("sp", wt[:], attn_w_in[la:la + 1, :, c0_:c0_ + 128].rearrange("o (k p) j -> p (o k) j", p=128), writes=[wtb])
                    wb, wbb = wbr.next()
                    P.emit("pool", lambda e, wb=wb, wt=wt: e.tensor_copy(out=wb[:], in_=wt[:]), reads=[wtb], writes=[wbb])
                    return wb, wbb
                nxt = load_w(slabs[0]) if slabs else None
                for si, slab in enumerate(slabs):
                    col0 = slab * 128
                    kind = col0 // D
                    sl8 = slab % 8
                    wb, wbb = nxt
                    if si + 1 < len(slabs):
                        nxt = load_w(slabs[si + 1])
                    if kind < 9 and kind % 3 == 2:
                        g = kind // 3
                        for t4 in range(SEG // 512):
                            ps, psb = ps_r.next()
                            for j in range(4):
                                tk = t4 * 4 + j
                                for k in range(8):
                                    P.emit("pe", lambda e, ps=ps, j=j, k=k, tk=tk, wb=wb: e.matmul(ps[:, j * 128:(j + 1) * 128], lhsT=hm[:, k, tk * 128:(tk + 1) * 128], rhs=wb[:, k, :],
                                                                                          start=(k == 0), stop=(k == 7)),
                                           reads=[b_hm, wbb], writes=[psb])
                            ov, ovb = ov_r.next()
                            P.emit("act", lambda e, ov=ov, ps=ps: e.activation(out=ov[:, :, 0:130].rearrange("p j (h c) -> p j h c", h=2)[:, :, :, 0:64],
                                                                          in_=ps[:].rearrange("p (j h c) -> p j h c", j=4, h=2), func=AF.Copy), reads=[psb], writes=[ovb])
                            r0 = s * SEG + t4 * 512
                            P.dma("sp", VV[g:g + 1, r0:r0 + 512, sl8, :].rearrange("o (j p) c -> p (o j) c", p=128), ov[:], reads=[ovb], writes=[bVV])
                        continue
                    for tb in range(SEG // 512):
                        t0 = s * SEG + tb * 512
                        ps, psb = ps_r.next()
                        for k in range(8):
                            P.emit("pe", lambda e, ps=ps, k=k, tb=tb, wb=wb: e.matmul(ps[:], lhsT=wb[:, k, :], rhs=hm[:, k, tb * 512:(tb + 1) * 512], start=(k == 0), stop=(k == 7)),
                                   reads=[b_hm, wbb], writes=[psb])
                        if kind == 9:
                            zo, zob = zo_r.next()
                            P.emit("act", lambda e, zo=zo, ps=ps: e.activation(out=zo[:], in_=ps[:], func=AF.Silu), reads=[psb], writes=[zob])
                            P.dma("sp", ZS[sl8, :, t0:t0 + 512], zo[:], reads=[zob], writes=[bZS])
                            continue
                        g = kind // 3
                        kq = DBG.get("kq", 9)
                        qb, qbb = qb_r.next()
                        P.emit("act", lambda e, qb=qb, ps=ps: e.activation(out=qb[:], in_=ps[:], func=AF.Copy), reads=[psb], writes=[qbb])
                        ob, obb = ob_r.next()
                        if kq >= 3:
                            ps2, ps2b = ps2_r.next()
                            P.emit("pe", lambda e, ps2=ps2, qb=qb: e.matmul(ps2[:], lhsT=perm_b[:], rhs=qb[:], start=True, stop=True), reads=[qbb, b_perm], writes=[ps2b])
                        rp, rpb = ropeS[:, :, tb * 512:(tb + 1) * 512], b_ropeS
                        if kq >= 4:
                            t1, t1b = t1_r.next()
                            t2, t2b = t2_r.next()
                            if kq in (4, 5, 9):
                                P.emit("dve", lambda e, t1=t1, ps=ps, rp=rp: e.tensor_tensor(out=t1[:], in0=ps[:], in1=rp[:, 0, :], op=ALU.mult), reads=[psb, rpb, qbb], writes=[t1b])
                            else:
                                P.emit("dve", lambda e, t1=t1, qb=qb, rp=rp: e.tensor_tensor(out=t1[:], in0=qb[:], in1=rp[:, 0, :], op=ALU.mult), reads=[qbb, rpb], writes=[t1b])
                            if kq in (4, 6, 9):
                                P.emit("dve", lambda e, t2=t2, ps2=ps2, rp=rp: e.tensor_tensor(out=t2[:], in0=ps2[:], in1=rp[:, 1, :], op=ALU.mult), reads=[ps2b, rpb], writes=[t2b])
                            else:
                                P.emit("dve", lambda e, t2=t2, qb=qb, rp=rp: e.tensor_tensor(out=t2[:], in0=qb[:], in1=rp[:, 1, :], op=ALU.mult), reads=[qbb, rpb], writes=[t2b])
                            P.emit("dve", lambda e, ob=ob, t1=t1, t2=t2: e.tensor_tensor(out=ob[:], in0=t1[:], in1=t2[:], op=ALU.add), reads=[t1b, t2b], writes=[obb])
                        else:
                            P.emit("dve", lambda e, ob=ob, qb=qb: e.tensor_copy(out=ob[:], in_=qb[:]), reads=[qbb], writes=[obb])
                        dst = (QT if kind % 3 == 0 else KT)
                        dbuf = (bQT if kind % 3 == 0 else bKT)
                        P.dma("sp", dst[g, sl8, :, t0:t0 + 512], ob[:], reads=[obb], writes=[dbuf])
                P.flush()

        if DBG["stop"] == "proj":
            return
        with ExitStack() as es:
            NK = 25
            q_r = Rot(es, nc, "cq", 2, [128, 3, 2, 128], BF16)
            for i in range(2):
                P.emit("pool", lambda e, i=i: e.memset(q_r.t[i][:], 0.0), writes=[q_r.b[i]])
            tp_r = Rot(es, nc, "ctp", 2, [128, 512], F32, psum=True)
            s_r = Rot(es, nc, "cs", 4, [128, 512], F32, psum=True)
            o_r = Rot(es, nc, "co", 2, [128, 512], F32, psum=True)
            p_r = Rot(es, nc, "cp", 6, [128, 512], BF16)
            rc_r = Rot(es, nc, "crc", 2, [128, 128], F32)
            on_r = Rot(es, nc, "con", 2, [128, 128], F32)
            z_r = Rot(es, nc, "cz", 3, [128, 128], F32)
            y_r = Rot(es, nc, "cy", 3, [128, 128], BF16)
            reach = (1, 2, 8)
            RING = [2 * r + 3 for r in reach]
            kring = [one(es, nc, f"kring{g}", [128, RING[g] * 128], BF16)[0] for g in range(3)]
            vring = [one(es, nc, f"vring{g}", [128, RING[g], 130], BF16)[0] for g in range(3)]
            kbuf = [[Buf() for _ in range(RING[g])] for g in range(3)]
            vbuf = [[Buf() for _ in range(RING[g])] for g in range(3)]
            for sl in range(8):
              ring_has = [dict() for _ in range(3)]

              def ensure(g, kt_abs, sl=sl, ring_has=ring_has):
                  slot = kt_abs % RING[g]
                  if ring_has[g].get(slot) == kt_abs:
                      return
                  ring_has[g][slot] = kt_abs
                  P.dma("sp", kring[g][:, slot * 128:(slot + 1) * 128], KT[g, sl, :, kt_abs * 128:(kt_abs + 1) * 128], reads=[bKT], writes=[kbuf[g][slot]])
                  P.dma("sp", vring[g][:, slot, :], VV[g, kt_abs * 128:(kt_abs + 1) * 128, sl, :], reads=[bVV], writes=[vbuf[g][slot]])
              for qt in range(NT):
                seg = qt // (SEG // 128)
                q0 = qt * 128
                if True:
                    zt, zb = z_r.next()
                    P.dma("sp", zt[:], ZS[sl, :, q0:q0 + 128], reads=[bZS], writes=[zb])
                    yt, yb = y_r.next()
                    qtile, qb_ = q_r.next()
                    for h2 in range(2):
                        P.dma("sp", qtile[h2 * 64:(h2 + 1) * 64, :, h2, :], QT[:, sl, h2 * 64:(h2 + 1) * 64, q0:q0 + 128].rearrange("g p t -> p g t"), reads=[bQT], writes=[qb_])
                    tiles = []
                    for g in range(3):
                        lo = max(0, qt - reach[g])
                        hi = min(NT - 1, qt + reach[g])
                        for kt_abs in range(lo, min(NT - 1, qt + 1 + reach[g]) + 1):
                            ensure(g, kt_abs)
                        for kt_abs in range(lo, hi + 1):
                            j = kt_abs - qt
                            mi = 3 * g + (0 if j == -reach[g] else (2 if j == reach[g] else 1))
                            cross = (kt_abs // (SEG // 128)) != seg
                            tiles.append((g, kt_abs % RING[g], mi, cross))
                    ot, ob_ = o_r.next()
                    batches = []
                    i_ = 0
                    while i_ < len(tiles):
                        if i_ + 1 < len(tiles) and tiles[i_][3] == tiles[i_ + 1][3]:
                            batches.append([tiles[i_], tiles[i_ + 1]])
                            i_ += 2
                        else:
                            batches.append([tiles[i_]])
                            i_ += 1

                    def emit_qk(bi, qtile=qtile, qb_=qb_, batches=batches):
                        st, sb_ = s_r.next()
                        for u, (g, ix, mi, cross) in enumerate(batches[bi]):
                            c0 = u * 256
                            P.emit("pe", lambda e, st=st, ix=ix, g=g, c0=c0, u=u: e.matmul(st[:, c0:c0 + 256], lhsT=kring[g][:, ix * 128:(ix + 1) * 128], rhs=qtile[:, g, :, :], start=(u == 0), stop=False),
                                   reads=[kbuf[g][ix], qb_], writes=[sb_])
                        nb = len(batches[bi])
                        mis = [t_[2] for t_ in batches[bi]]
                        if nb == 2 and mis[0] == mis[1]:
                            mrhs = masks[:, mis[0]:mis[0] + 1, :].to_broadcast([128, 4, 128])
                            P.emit("pe", lambda e, st=st, mrhs=mrhs: e.matmul(st[:, 0:512], lhsT=ident_b[:], rhs=mrhs, start=False, stop=True),
                                   reads=[b_idb, b_masks], writes=[sb_])
                        else:
                            for u, (g, ix, mi, cross) in enumerate(batches[bi]):
                                c0 = u * 256
                                P.emit("pe", lambda e, st=st, mi=mi, c0=c0, u=u, nb=nb: e.matmul(st[:, c0:c0 + 256], lhsT=ident_b[:], rhs=masks[:, mi:mi + 1, :].to_broadcast([128, 2, 128]), start=False, stop=(u == nb - 1)),
                                       reads=[b_idb, b_masks], writes=[sb_])
                        return st, sb_
                    DEPTH_QK = 2
                    pend = [emit_qk(i) for i in range(min(DEPTH_QK, len(batches)))]
                    nmm = 0
                    tot_mm = 2 * len(tiles)
                    for bi, batch in enumerate(batches):
                        st, sb_ = pend.pop(0)
                        if bi + DEPTH_QK < len(batches):
                            pend.append(emit_qk(bi + DEPTH_QK))
                        w = 256 * len(batch)
                        pt, pb_ = p_r.next()
                        if batch[0][3]:
                            P.emit("act", lambda e, pt=pt, st=st, w=w: e.activation(out=pt[:, 0:w], in_=st[:, 0:w], func=AF.Exp, bias=flag[:, 1:2], scale=0.125),
                                   reads=[sb_, b_flag], writes=[pb_])
                        else:
                            P.emit("act", lambda e, pt=pt, st=st, w=w: e.activation(out=pt[:, 0:w], in_=st[:, 0:w], func=AF.Exp, scale=0.125), reads=[sb_], writes=[pb_])
                        for u, (g, ix, mi, cross) in enumerate(batch):
                            for hh in range(2):
                                c0 = u * 256 + hh * 128
                                first = nmm == 0
                                nmm += 1
                                last = nmm == tot_mm
                                P.emit("pe", lambda e, ot=ot, g=g, pt=pt, ix=ix, hh=hh, c0=c0, first=first, last=last: e.matmul(
                                    ot[:, hh * 65:(hh + 1) * 65], lhsT=pt[:, c0:c0 + 128], rhs=vring[g][:, ix, hh * 65:(hh + 1) * 65], start=first, stop=last),
                                    reads=[vbuf[g][ix], pb_], writes=[ob_])
                    rc, rcb = rc_r.next()
                    P.emit("dve", lambda e, rc=rc, ot=ot: e.reciprocal(out=rc[:, 0:2], in_=ot[:, 64:130:65]), reads=[ob_], writes=[rcb])
                    on, onb = on_r.next()
                    for hh in range(2):
                        P.emit("dve", lambda e, on=on, ot=ot, rc=rc, hh=hh: e.tensor_scalar(out=on[:, hh * 64:(hh + 1) * 64], in0=ot[:, hh * 65:hh * 65 + 64], scalar1=rc[:, hh:hh + 1], scalar2=None, op0=ALU.mult),
                               reads=[ob_, rcb, onb], writes=[onb])
                    tp, tpb = tp_r.next()
                    P.emit("pe", lambda e, tp=tp, on=on: e.matmul(tp[:, 0:128], lhsT=on[:], rhs=ident_f[:], start=True, stop=True), reads=[onb, b_idf], writes=[tpb])
                    P.emit("dve", lambda e, tp=tp, yt=yt, zt=zt: e.tensor_tensor(out=yt[:], in0=tp[:, 0:128], in1=zt[:], op=ALU.mult), reads=[tpb, zb], writes=[yb])
                    P.dma("pool", YT[sl, :, q0:q0 + 128], yt[:], reads=[yb], writes=[bYT])
            P.flush()
        if DBG["stop"] == "core":
            return
        outproj_phase(l, attn_w_out[la:la + 1, :, :])

    def ssm_prep(lb):
        with ExitStack() as es:
            mg2, b_mg2 = one(es, nc, "mg2", [128, 2], F32)
            mq, b_mq = one(es, nc, "mq", [128, 8], F32)
            pid, b_pid = one(es, nc, "spid", [128, 1], I32)
            pq, b_pq = one(es, nc, "spq", [128, 1], I32)
            pqf, b_pqf = one(es, nc, "spqf", [128, 1], F32)
            P.emit("dve", lambda e: e.memset(mg2[:], 0.0), writes=[b_mg2])
            P.emit("dve", lambda e: e.memset(mg2[0:64, 0:1], 1.0), reads=[b_mg2], writes=[b_mg2])
            P.emit("dve", lambda e: e.memset(mg2[64:128, 1:2], 1.0), reads=[b_mg2], writes=[b_mg2])
            P.emit("pool", lambda e: e.iota(pid[:], pattern=[[0, 1]], base=0, channel_multiplier=1), writes=[b_pid])
            P.emit("dve", lambda e: e.tensor_single_scalar(out=pq[:], in_=pid[:], scalar=4, op=ALU.arith_shift_right), reads=[b_pid], writes=[b_pq])
            P.emit("dve", lambda e: e.tensor_copy(out=pqf[:], in_=pq[:]), reads=[b_pq], writes=[b_pqf])
            for qq in range(8):
                P.emit("dve", lambda e, qq=qq: e.tensor_single_scalar(out=mq[:, qq:qq + 1], in_=pqf[:], scalar=float(qq), op=ALU.is_equal), reads=[b_pqf], writes=[b_mq])
            psm = Rot(es, nc, "sps", 3, [128, 512], F32, psum=True)
            for d in range(2):
                ld = lb * 2 + d
                nat, b_nat = one(es, nc, f"nat{d}", [32, 3, 128], F32)
                ldt, b_ldt = one(es, nc, f"ldt{d}", [32, 2], F32)
                P.dma("sp", nat[:, 0, :], lam_re_i[lb, d].rearrange("(q a) p -> q (a p)", a=2), writes=[b_nat])
                P.dma("sp", nat[:, 1, :], lam_im_i[lb, d].rearrange("(q a) p -> q (a p)", a=2), writes=[b_nat], reads=[b_nat])
                P.dma("sp", ldt[:], log_dt_i[lb, d:d + 1, :].rearrange("o (q a) -> q (o a)", a=2), writes=[b_ldt])
                for a in range(2):
                    P.emit("dve", lambda e, a=a, nat=nat, ldt=ldt: e.tensor_scalar(out=nat[:, 2, a * 64:(a + 1) * 64], in0=nat[:, 0, a * 64:(a + 1) * 64], scalar1=0.0, scalar2=ldt[:, a:a + 1],
                                                                         op0=ALU.mult, op1=ALU.add), reads=[b_nat, b_ldt], writes=[b_nat])
                L, b_L = one(es, nc, f"L{d}", [128, 16, 32], F32)
                ps, psb = psm.next()
                for i in range(3):
                    P.emit("pe", lambda e, ps=ps, i=i, nat=nat: e.matmul(ps[:, i * 32:(i + 1) * 32], lhsT=nat[:, i, :], rhs=ident_f[0:32, 0:32], start=True, stop=True),
                           reads=[b_nat, b_idf], writes=[psb])
                P.emit("dve", lambda e, ps=ps, L=L: e.tensor_copy(out=L[:, 0:3, :], in_=ps[:, 0:96].rearrange("p (i q) -> p i q", i=3)), reads=[psb], writes=[b_L])
                Ki, b_Ki = one(es, nc, f"Ki{d}", [128, 32], I32)

                def dv(fn):
                    P.emit("dve", fn, reads=[b_L], writes=[b_L])

                def ac(fn):
                    P.emit("act", fn, reads=[b_L], writes=[b_L])
                dv(lambda e, L=L: e.tensor_scalar(out=L[:, 0, :], in0=L[:, 0, :], scalar1=-1e-4, scalar2=None, op0=ALU.min))
                ac(lambda e, L=L: e.activation(out=L[:, 3, :], in_=L[:, 2, :], func=AF.Exp))
                dv(lambda e, L=L: e.tensor_tensor(out=L[:, 4, :], in0=L[:, 0, :], in1=L[:, 3, :], op=ALU.mult))
                dv(lambda e, L=L: e.tensor_tensor(out=L[:, 5, :], in0=L[:, 1, :], in1=L[:, 3, :], op=ALU.mult))
                ac(lambda e, L=L: e.activation(out=L[:, 6, :], in_=L[:, 4, :], func=AF.Exp))
                dv(lambda e, L=L: e.tensor_scalar(out=L[:, 13, :], in0=L[:, 5, :], scalar1=1.0 / TWO_PI, scalar2=None, op0=ALU.mult))
                P.emit("dve", lambda e, L=L, Ki=Ki: e.tensor_copy(out=Ki[:], in_=L[:, 13, :]), reads=[b_L], writes=[b_Ki])
                P.emit("dve", lambda e, L=L, Ki=Ki: e.tensor_copy(out=L[:, 13, :], in_=Ki[:]), reads=[b_Ki, b_L], writes=[b_L])
                dv(lambda e, L=L: e.scalar_tensor_tensor(out=L[:, 14, :], in0=L[:, 13, :], scalar=-C1, in1=L[:, 5, :], op0=ALU.mult, op1=ALU.add))
                dv(lambda e, L=L: e.scalar_tensor_tensor(out=L[:, 14, :], in0=L[:, 13, :], scalar=-C2, in1=L[:, 14, :], op0=ALU.mult, op1=ALU.add))
                dv(lambda e, L=L: e.tensor_scalar(out=L[:, 7, :], in0=L[:, 14, :], scalar1=math.pi / 2, scalar2=None, op0=ALU.add))
                dv(lambda e, L=L: e.tensor_single_scalar(out=L[:, 8, :], in_=L[:, 7, :], scalar=math.pi, op=ALU.is_gt))
                dv(lambda e, L=L: e.scalar_tensor_tensor(out=L[:, 7, :], in0=L[:, 8, :], scalar=-TWO_PI, in1=L[:, 7, :], op0=ALU.mult, op1=ALU.add))
                dv(lambda e, L=L: e.tensor_copy(out=L[:, 8, :], in_=L[:, 14, :]))
                dv(lambda e, L=L: e.tensor_scalar(out=L[:, 7:9, :], in0=L[:, 7:9, :], scalar1=math.pi, scalar2=-math.pi, op0=ALU.min, op1=ALU.max))
                ac(lambda e, L=L: e.activation(out=L[:, 7:9, :], in_=L[:, 7:9, :], func=AF.Sin))
                dv(lambda e, L=L: e.tensor_tensor(out=L[:, 9, :], in0=L[:, 6, :], in1=L[:, 7, :], op=ALU.mult))
                dv(lambda e, L=L: e.tensor_tensor(out=L[:, 10, :], in0=L[:, 6, :], in1=L[:, 8, :], op=ALU.mult))
                dv(lambda e, L=L: e.tensor_scalar(out=L[:, 13, :], in0=L[:, 9, :], scalar1=-1.0, scalar2=None, op0=ALU.add))
                dv(lambda e, L=L: e.tensor_tensor(out=L[:, 14, :], in0=L[:, 0, :], in1=L[:, 0, :], op=ALU.mult))
                dv(lambda e, L=L: e.tensor_tensor(out=L[:, 15, :], in0=L[:, 1, :], in1=L[:, 1, :], op=ALU.mult))
                dv(lambda e, L=L: e.tensor_tensor(out=L[:, 14, :], in0=L[:, 14, :], in1=L[:, 15, :], op=ALU.add))
                dv(lambda e, L=L: e.reciprocal(out=L[:, 14, :], in_=L[:, 14, :]))
                dv(lambda e, L=L: e.tensor_tensor(out=L[:, 11, :], in0=L[:, 13, :], in1=L[:, 0, :], op=ALU.mult))
                dv(lambda e, L=L: e.tensor_tensor(out=L[:, 15, :], in0=L[:, 10, :], in1=L[:, 1, :], op=ALU.mult))
                dv(lambda e, L=L: e.tensor_tensor(out=L[:, 11, :], in0=L[:, 11, :], in1=L[:, 15, :], op=ALU.add))
                dv(lambda e, L=L: e.tensor_tensor(out=L[:, 11, :], in0=L[:, 11, :], in1=L[:, 14, :], op=ALU.mult))
                dv(lambda e, L=L: e.tensor_tensor(out=L[:, 12, :], in0=L[:, 10, :], in1=L[:, 0, :], op=ALU.mult))
                dv(lambda e, L=L: e.tensor_tensor(out=L[:, 15, :], in0=L[:, 13, :], in1=L[:, 1, :], op=ALU.mult))
                dv(lambda e, L=L: e.tensor_tensor(out=L[:, 12, :], in0=L[:, 12, :], in1=L[:, 15, :], op=ALU.subtract))
                dv(lambda e, L=L: e.tensor_tensor(out=L[:, 12, :], in0=L[:, 12, :], in1=L[:, 14, :], op=ALU.mult))
                P.emit("dve", lambda e, L=L, ld=ld: e.tensor_copy(out=PW[:, ld, :, 0, 0], in_=L[:, 9, :]), reads=[b_L, b_PW], writes=[b_PW])
                P.emit("dve", lambda e, L=L, ld=ld: e.tensor_copy(out=PW[:, ld, :, 0, 1], in_=L[:, 10, :]), reads=[b_L, b_PW], writes=[b_PW])
                for j in range(1, 12):
                    def pwop(fn):
                        P.emit("dve", fn, reads=[b_PW, b_L], writes=[b_PW, b_L])
                    pwop(lambda e, L=L, ld=ld, j=j: e.tensor_tensor(out=L[:, 13, :], in0=PW[:, ld, :, j - 1, 0], in1=PW[:, ld, :, j - 1, 0], op=ALU.mult))
                    pwop(lambda e, L=L, ld=ld, j=j: e.tensor_tensor(out=L[:, 15, :], in0=PW[:, ld, :, j - 1, 1], in1=PW[:, ld, :, j - 1, 1], op=ALU.mult))
                    pwop(lambda e, L=L, ld=ld, j=j: e.tensor_tensor(out=PW[:, ld, :, j, 0], in0=L[:, 13, :], in1=L[:, 15, :], op=ALU.subtract))
                    pwop(lambda e, L=L, ld=ld, j=j: e.tensor_tensor(out=L[:, 13, :], in0=PW[:, ld, :, j - 1, 0], in1=PW[:, ld, :, j - 1, 1], op=ALU.mult))
                    pwop(lambda e, L=L, ld=ld, j=j: e.tensor_scalar(out=PW[:, ld, :, j, 1], in0=L[:, 13, :], scalar1=2.0, scalar2=None, op0=ALU.mult))
                P.emit("dve", lambda e, ld=ld: e.tensor_scalar(out=PW[:, ld, :, :, 2], in0=PW[:, ld, :, :, 1], scalar1=-1.0, scalar2=None, op0=ALU.mult),
                       reads=[b_PW], writes=[b_PW])
                Bn, b_Bn = one(es, nc, f"Bn{d}", [128, 2, 32, 16], F32)
                Bb, b_Bb = one(es, nc, f"Bb{d}", [128, 2, 32, 16], F32)
                tmpB, b_tB = one(es, nc, f"tB{d}", [128, 32, 16], F32)
                P.dma("sp", Bn[:, 0, :, :], b_re_i[lb, d].rearrange("(q a) p c -> (a p) q c", a=2), writes=[b_Bn])
                P.dma("sp", Bn[:, 1, :, :], b_im_i[lb, d].rearrange("(q a) p c -> (a p) q c", a=2), writes=[b_Bn], reads=[b_Bn])
                crb = L[:, 11, :].unsqueeze(2).to_broadcast([128, 32, 16])
                cib = L[:, 12, :].unsqueeze(2).to_broadcast([128, 32, 16])

                def bop(fn):
                    P.emit("dve", fn, reads=[b_Bn, b_L, b_Bb, b_tB], writes=[b_Bb, b_tB])
                bop(lambda e, Bn=Bn, Bb=Bb, crb=crb: e.tensor_tensor(out=Bb[:, 0], in0=Bn[:, 0], in1=crb, op=ALU.mult))
                bop(lambda e, Bn=Bn, tmpB=tmpB, cib=cib: e.tensor_tensor(out=tmpB[:], in0=Bn[:, 1], in1=cib, op=ALU.mult))
                bop(lambda e, Bb=Bb, tmpB=tmpB: e.tensor_tensor(out=Bb[:, 0], in0=Bb[:, 0], in1=tmpB[:], op=ALU.subtract))
                bop(lambda e, Bn=Bn, Bb=Bb, crb=crb: e.tensor_tensor(out=Bb[:, 1], in0=Bn[:, 1], in1=crb, op=ALU.mult))
                bop(lambda e, Bn=Bn, tmpB=tmpB, cib=cib: e.tensor_tensor(out=tmpB[:], in0=Bn[:, 0], in1=cib, op=ALU.mult))
                bop(lambda e, Bb=Bb, tmpB=tmpB: e.tensor_tensor(out=Bb[:, 1], in0=Bb[:, 1], in1=tmpB[:], op=ALU.add))
                Cn, b_Cn = one(es, nc, f"Cn{d}", [128, 2, 8, 64], F32)
                P.dma("sp", Cn[:, 0], c_re_i[lb, d].rearrange("(k q) c p -> (q c) k p", k=8), writes=[b_Cn])
                P.dma("sp", Cn[:, 1], c_im_i[lb, d].rearrange("(k q) c p -> (q c) k p", k=8), writes=[b_Cn], reads=[b_Cn])
                inX, b_inX = one(es, nc, f"inX{d}", [128, 32, 128], F32)
                stg = Rot(es, nc, f"stg{d}", 2, [128, 4, 128], BF16)
                for which in range(4):
                    ri = which % 2
                    P.emit("pool", lambda e, inX=inX: e.memset(inX[:], 0.0), writes=[b_inX])
                    if which < 2:
                        for j4 in range(4):
                            for a in range(2):
                                c0 = j4 * 32 + a * 16
                                P.emit("dve", lambda e, inX=inX, Bb=Bb, j4=j4, a=a, c0=c0, ri=ri: e.tensor_scalar(
                                    out=inX[:, j4::4, c0:c0 + 16], in0=Bb[:, ri, j4::4, :], scalar1=mg2[:, a:a + 1], scalar2=None, op0=ALU.mult),
                                    reads=[b_Bb, b_mg2, b_inX], writes=[b_inX])
                    else:
                        for j4 in range(4):
                            for a in range(2):
                                P.emit("dve", lambda e, inX=inX, Cn=Cn, j4=j4, a=a, ri=ri: e.tensor_scalar(
                                    out=inX[:, j4::4, a * 64:(a + 1) * 64], in0=Cn[:, ri, :, :], scalar1=mq[:, j4 * 2 + a:j4 * 2 + a + 1], scalar2=None, op0=ALU.mult),
                                    reads=[b_Cn, b_mq, b_inX], writes=[b_inX])
                    for p4 in range(8):
                        ps, psb = psm.next()
                        for j in range(4):
                            pp = p4 * 4 + j
                            P.emit("pe", lambda e, ps=ps, j=j, pp=pp, inX=inX: e.matmul(ps[:, j * 128:(j + 1) * 128], lhsT=inX[:, pp, :], rhs=ident_f[:], start=True, stop=True),
                                   reads=[b_inX, b_idf], writes=[psb])
                        st, stb = stg.next()
                        sc = -1.0 if which == 3 else 1.0
                        P.emit("act", lambda e, st=st, ps=ps, sc=sc: e.activation(out=st[:], in_=ps[:].rearrange("p (j c) -> p j c", j=4), func=AF.Copy, scale=sc),
                               reads=[psb], writes=[stb])
                        P.dma("pool", TAB[lb, d, p4 * 4:(p4 + 1) * 4, :, which, :].rearrange("j p c -> p j c"), st[:], reads=[stb], writes=[bTAB])
            P.flush()

    def ssm_layer(l, lb):
        ssm_prep(lb)
        if DBG["stop"] == "prep":
            return
        for s in range(NSEG):
            with ExitStack() as es:
                hm, b_hm = one(es, nc, "hm", [128, 8, SEG], BF16)
                norm_phase(es, l, s, hm, b_hm)
                w32 = Rot(es, nc, "sw32", 2, [128, 8, 128], F32)
                wbr = Rot(es, nc, "swb", 2, [128, 8, 128], BF16)
                ps_r = Rot(es, nc, "sps", 3, [128, 512], F32, psum=True)
                ub_r = Rot(es, nc, "sub", 2, [128, 512], BF16)
                uf_r = Rot(es, nc, "suf", 3, [128, 512], F32)
                for slab in range(16):
                    col0 = slab * 128
                    f = slab % 8
                    wt, wtb = w32.next()
                    P.dma("sp", wt[:], ssm_w_in[lb:lb + 1, :, col0:col0 + 128].rearrange("o (k p) j -> p (o k) j", p=128), writes=[wtb])
                    wb, wbb = wbr.next()
                    P.emit("pool", lambda e, wb=wb, wt=wt: e.tensor_copy(out=wb[:], in_=wt[:]), reads=[wtb], writes=[wbb])
                    for tb in range(SEG // 512):
                        t0 = s * SEG + tb * 512
                        ps, psb = ps_r.next()
                        for k in range(8):
                            P.emit("pe", lambda e, ps=ps, k=k, tb=tb, wb=wb: e.matmul(ps[:], lhsT=wb[:, k, :], rhs=hm[:, k, tb * 512:(tb + 1) * 512], start=(k == 0), stop=(k == 7)),
                                   reads=[b_hm, wbb], writes=[psb])
                        uf, ufb = uf_r.next()
                        if slab < 8:
                            ub, ubb = ub_r.next()
                            P.emit("act", lambda e, ub=ub, ps=ps: e.activation(out=ub[:], in_=ps[:], func=AF.Copy), reads=[psb], writes=[ubb])
                            P.dma("sp", UT[f, :, t0:t0 + 512], ub[:], reads=[ubb], writes=[bUT])
                            P.emit("dve", lambda e, uf=uf, ps=ps, f=f: e.tensor_scalar(out=uf[:], in0=ps[:], scalar1=dvec[:, lb, f:f + 1], scalar2=None, op0=ALU.mult),
                                   reads=[psb, b_dvec, ubb], writes=[ufb])
                            P.dma("sp", Y0[f, :, t0:t0 + 512], uf[:], reads=[ufb], writes=[bY0])
                        else:
                            P.emit("act", lambda e, uf=uf, ps=ps: e.activation(out=uf[:], in_=ps[:], func=AF.Silu), reads=[psb], writes=[ufb])
                            P.dma("sp", ZS[f, :, t0:t0 + 512], uf[:], reads=[ufb], writes=[bZS])
                P.flush()

        if DBG["stop"] == "inproj":
            return
        with ExitStack() as es:
            ut_r = Rot(es, nc, "qu", 1, [128, N], BF16)
            ya_r = Rot(es, nc, "qy", 1, [128, N], F32)
            tb_r = Rot(es, nc, "qt", 4, [128, 4, 128], BF16)
            x_r = Rot(es, nc, "qx", 3, [128, 2, SEG], F32)
            xb_r = Rot(es, nc, "qxb", 1, [128, 2, SEG], BF16)
            pb_r = Rot(es, nc, "qpb", 4, [128, 512], F32, psum=True)
            pc_r = Rot(es, nc, "qpc", 3, [128, 512], F32, psum=True)
            fins = [one(es, nc, "qfin", [128, 2], F32) for _ in range(2)]
            injs = [one(es, nc, "qinj", [128, 4], F32) for _ in range(2)]
            g1_r = Rot(es, nc, "qg1", 2, [128, 512], F32)
            g2_r = Rot(es, nc, "qg2", 2, [128, 512], F32)
            gb_r = Rot(es, nc, "qgb", 2, [128, 512], BF16)
            LV = 12
            dirs = DBG.get("dirs", (0, 1))

            def bu_fill(tbt, tbb, ut, ub, s):
                X, Xb_ = x_r.next()
                for blk in range(SEG // 512):
                    c0 = s * SEG + blk * 512
                    for ri in range(2):
                        ps, psb = pb_r.next()
                        P.emit("pe", lambda e, ps=ps, tbt=tbt, ut=ut, ri=ri, c0=c0: e.matmul(ps[:], lhsT=tbt[:, ri, :], rhs=ut[:, c0:c0 + 512], start=True, stop=True),
                               reads=[tbb, ub], writes=[psb])
                        P.emit("act", lambda e, ps=ps, X=X, ri=ri, blk=blk: e.activation(out=X[:, ri, blk * 512:(blk + 1) * 512], in_=ps[:], func=AF.Copy),
                               reads=[psb], writes=[Xb_])
                return X, Xb_

            def inject(X, Xb_, d, pw, fin, b_fin, inj, b_inj):
                tcol = 0 if d == 0 else SEG - 1

                def io(fn):
                    P.emit("dve", fn, reads=[b_fin, b_inj, b_PW, b_flag, Xb_], writes=[b_inj, Xb_])
                io(lambda e: e.tensor_scalar(out=inj[:, 0:1], in0=fin[:, 0:1], scalar1=pw(0, 0), scalar2=None, op0=ALU.mult))
                io(lambda e: e.scalar_tensor_tensor(out=inj[:, 0:1], in0=fin[:, 1:2], scalar=pw(0, 2), in1=inj[:, 0:1], op0=ALU.mult, op1=ALU.add))
                io(lambda e: e.tensor_scalar(out=inj[:, 1:2], in0=fin[:, 1:2], scalar1=pw(0, 0), scalar2=None, op0=ALU.mult))
                io(lambda e: e.scalar_tensor_tensor(out=inj[:, 1:2], in0=fin[:, 0:1], scalar=pw(0, 1), in1=inj[:, 1:2], op0=ALU.mult, op1=ALU.add))
                for ri in range(2):
                    io(lambda e, ri=ri: e.scalar_tensor_tensor(out=X[:, ri, tcol:tcol + 1], in0=inj[:, ri:ri + 1], scalar=flag[:, 0:1],
                                                         in1=X[:, ri, tcol:tcol + 1], op0=ALU.mult, op1=ALU.add))

            def scan_ops(X, Xb_, d, pw):
                ops = []

                def cstep(dsl, ssl, j):
                    def so(fn):
                        ops.append(lambda fn=fn: P.emit("dve", fn, reads=[Xb_, b_PW], writes=[Xb_]))
                    so(lambda e: e.scalar_tensor_tensor(out=X[:, :, dsl], in0=X[:, :, ssl], scalar=pw(j, 0), in1=X[:, :, dsl], op0=ALU.mult, op1=ALU.add))
                    so(lambda e: e.scalar_tensor_tensor(out=X[:, 0, dsl], in0=X[:, 1, ssl], scalar=pw(j, 2), in1=X[:, 0, dsl], op0=ALU.mult, op1=ALU.add))
                    so(lambda e: e.scalar_tensor_tensor(out=X[:, 1, dsl], in0=X[:, 0, ssl], scalar=pw(j, 1), in1=X[:, 1, dsl], op0=ALU.mult, op1=ALU.add))
                if DBG.get("noscan"):
                    return ops
                for j in range(LV):
                    S_, h = 2 ** (j + 1), 2 ** j
                    if d == 0:
                        cstep(slice(S_ - 1, SEG, S_), slice(h - 1, SEG, S_), j)
                    else:
                        cstep(slice(0, SEG, S_), slice(h, SEG, S_), j)
                for j in range(LV - 2, -1, -1):
                    S_, h = 2 ** (j + 1), 2 ** j
                    cnt = SEG // S_ - 1
                    if d == 0:
                        cstep(slice(S_ + h - 1, S_ + h - 1 + (cnt - 1) * S_ + 1, S_), slice(S_ - 1, S_ - 1 + (cnt - 1) * S_ + 1, S_), j)
                    else:
                        cstep(slice(h, h + (cnt - 1) * S_ + 1, S_), slice(S_, S_ + (cnt - 1) * S_ + 1, S_), j)
                return ops

            def interleave(lists):
                n = max(len(l_) for l_ in lists)
                for i in range(n):
                    for l_ in lists:
                        if i < len(l_):
                            l_[i]()

            def cmat(X, Xb_, tbt, tbb, ya, yab, s):
                Xh, Xhb = xb_r.next()
                P.emit("act", lambda e, Xh=Xh, X=X: e.activation(out=Xh[:], in_=X[:], func=AF.Copy), reads=[Xb_], writes=[Xhb])
                for blk in range(SEG // 512):
                    c0 = s * SEG + blk * 512
                    ps, psb = pc_r.next()
                    for ri in range(2):
                        P.emit("pe", lambda e, ps=ps, tbt=tbt, Xh=Xh, ri=ri, blk=blk: e.matmul(ps[:], lhsT=tbt[:, 2 + ri, :], rhs=Xh[:, ri, blk * 512:(blk + 1) * 512], start=(ri == 0), stop=(ri == 1)),
                               reads=[tbb, Xhb], writes=[psb])
                    P.emit("dve", lambda e, ps=ps, ya=ya, c0=c0: e.tensor_tensor(out=ya[:, c0:c0 + 512], in0=ps[:], in1=ya[:, c0:c0 + 512], op=ALU.add),
                           reads=[psb, yab], writes=[yab])

            for kt in range(8):
                ut, ub = ut_r.next()
                P.dma("sp", ut[:], UT[kt, :, :], reads=[bUT], writes=[ub])
                ya, yab = ya_r.next()
                P.dma("sp", ya[:], Y0[kt, :, :], reads=[bY0], writes=[yab])
                for j4 in range(4):
                    pp = kt * 4 + j4
                    tbs, pws = {}, {}
                    for d in dirs:
                        ld = lb * 2 + d
                        tbt, tbb = tb_r.next()
                        P.dma("sp", tbt[:], TAB[lb, d, pp, :, :, :], reads=[bTAB], writes=[tbb])
                        tbs[d] = (tbt, tbb)
                        pws[d] = (lambda j, c, ld=ld, pp=pp: PW[:, ld, pp, j, c:c + 1])
                    first = {}
                    for d in dirs:
                        s = 0 if d == 0 else 1
                        X, Xb_ = bu_fill(tbs[d][0], tbs[d][1], ut, ub, s)
                        first[d] = (X, Xb_, s)
                    interleave([scan_ops(first[d][0], first[d][1], d, pws[d]) for d in dirs])
                    for d in dirs:
                        X, Xb_, s = first[d]
                        fcol = SEG - 1 if d == 0 else 0
                        fin, b_fin = fins[d]
                        P.emit("dve", lambda e, X=X, fcol=fcol, fin=fin: e.tensor_copy(out=fin[:], in_=X[:, :, fcol]), reads=[Xb_, b_fin], writes=[b_fin])
                    second = {}
                    for d in dirs:
                        X, Xb_, s = first[d]
                        cmat(X, Xb_, tbs[d][0], tbs[d][1], ya, yab, s)
                        s2 = 1 - s
                        X2, X2b = bu_fill(tbs[d][0], tbs[d][1], ut, ub, s2)
                        inject(X2, X2b, d, pws[d], fins[d][0], fins[d][1], injs[d][0], injs[d][1])
                        second[d] = (X2, X2b, s2)
                    interleave([scan_ops(second[d][0], second[d][1], d, pws[d]) for d in dirs])
                    for d in dirs:
                        X2, X2b, s2 = second[d]
                        cmat(X2, X2b, tbs[d][0], tbs[d][1], ya, yab, s2)
                for cb in range(N // 512):
                    sl = slice(cb * 512, (cb + 1) * 512)
                    if DBG.get("rawY"):
                        P.dma("pool", GT[kt, :, sl], ya[:, sl], reads=[yab], writes=[bGT])
                        continue
                    g1, g1b = g1_r.next()
                    g2, g2b = g2_r.next()
                    P.emit("act", lambda e, g1=g1, ya=ya, sl=sl: e.activation(out=g1[:], in_=ya[:, sl], func=AF.Square), reads=[yab], writes=[g1b])
                    P.emit("dve", lambda e, g1=g1: e.tensor_scalar(out=g1[:], in0=g1[:], scalar1=0.044715, scalar2=1.0, op0=ALU.mult, op1=ALU.add), reads=[g1b], writes=[g1b])
                    P.emit("dve", lambda e, g1=g1, ya=ya, sl=sl: e.tensor_tensor(out=g1[:], in0=g1[:], in1=ya[:, sl], op=ALU.mult), reads=[g1b, yab], writes=[g1b])
                    P.emit("act", lambda e, g1=g1: e.activation(out=g1[:], in_=g1[:], func=AF.Sigmoid, scale=1.5957691216057308), reads=[g1b], writes=[g1b])
                    P.emit("dve", lambda e, g1=g1, g2=g2, ya=ya, sl=sl: e.tensor_tensor(out=g2[:], in0=g1[:], in1=ya[:, sl], op=ALU.mult), reads=[g1b, yab], writes=[g2b])
                    gb, gbb = gb_r.next()
                    P.emit("act", lambda e, gb=gb, g2=g2: e.activation(out=gb[:], in_=g2[:], func=AF.Copy), reads=[g2b], writes=[gbb])
                    P.dma("pool", GT[kt, :, sl], g2[:], reads=[g2b], writes=[bGT])
                    P.dma("pool", GB[kt, :, sl], gb[:], reads=[gbb], writes=[bGB])
            P.flush()

        if DBG["stop"] == "scan":
            return
        with ExitStack() as es:
            wb, b_wb = one(es, nc, "gw", [128, 8, D], BF16)
            r32 = Rot(es, nc, "gw32", 2, [128, D], F32)
            load_w_bf16(es, "gw", ssm_w_glu[lb:lb + 1, :, :], D, wb, b_wb, r32)
            gb_r = Rot(es, nc, "gg", 2, [128, 8, 512], BF16)
            g32_r = Rot(es, nc, "gg32", 2, [128, 8, 512], F32)
            z_r = Rot(es, nc, "gz", 2, [128, 8, 512], F32)
            ps_r = Rot(es, nc, "gps", 3, [128, 512], F32, psum=True)
            sg_r = Rot(es, nc, "gsg", 3, [128, 512], F32)
            yo_r = Rot(es, nc, "gyo", 2, [128, 8, 512], BF16)
            for tb in range(N // 512):
                t0 = tb * 512
                gt, gtb = gb_r.next()
                P.dma("sp", gt[:], GB[:, :, t0:t0 + 512].rearrange("f p t -> p f t"), reads=[bGB], writes=[gtb])
                g32, g32b = g32_r.next()
                P.dma("sp", g32[:], GT[:, :, t0:t0 + 512].rearrange("f p t -> p f t"), reads=[bGT], writes=[g32b])
                zt, ztb = z_r.next()
                P.dma("sp", zt[:], ZS[:, :, t0:t0 + 512].rearrange("f p t -> p f t"), reads=[bZS], writes=[ztb])
                yo, yob = yo_r.next()
                for f in range(8):
                    ps, psb = ps_r.next()
                    for k in range(8):
                        P.emit("pe", lambda e, ps=ps, k=k, f=f, gt=gt: e.matmul(ps[:], lhsT=wb[:, k, f * 128:(f + 1) * 128], rhs=gt[:, k, :], start=(k == 0), stop=(k == 7)),
                               reads=[b_wb, gtb], writes=[psb])
                    sg, sgb = sg_r.next()
                    P.emit("act", lambda e, sg=sg, ps=ps: e.activation(out=sg[:], in_=ps[:], func=AF.Sigmoid), reads=[psb], writes=[sgb])
                    P.emit("dve", lambda e, sg=sg, g32=g32, f=f: e.tensor_tensor(out=sg[:], in0=sg[:], in1=g32[:, f, :], op=ALU.mult), reads=[sgb, g32b], writes=[sgb])
                    P.emit("dve", lambda e, sg=sg, zt=zt, yo=yo, f=f: e.tensor_tensor(out=yo[:, f, :], in0=sg[:], in1=zt[:, f, :], op=ALU.mult), reads=[sgb, ztb], writes=[yob])
                P.dma("pool", YT[:, :, t0:t0 + 512].rearrange("f p t -> p f t"), yo[:], reads=[yob], writes=[bYT])
            P.flush()
        outproj_phase(l, ssm_w_out[lb:lb + 1, :, :])

    for l in layers:
        if l % 2 == 0:
            attn_layer(l, l // 2)
        else:
            ssm_layer(l, l // 2)

    with ExitStack() as es:
        xb_r = Rot(es, nc, "fx", 2, [128, 8, 512], F32)
        sq_r = Rot(es, nc, "fsq", 2, [128, 512], F32)
        ps_r = Rot(es, nc, "fps", 2, [128, 512], F32, psum=True)
        rs_r = Rot(es, nc, "frs", 2, [128, 512], F32)
        pt_r = Rot(es, nc, "fpt", 4, [128, 512], F32, psum=True)
        os_r = Rot(es, nc, "fos", 2, [128, D], F32)
        for tb in range(N // 512):
            t0 = tb * 512
            xt, xb = xb_r.next()
            P.dma("sp", xt[:], XT[:, :, t0:t0 + 512].rearrange("f p t -> p f t"), reads=[bX], writes=[xb])
            ps, psb = ps_r.next()
            for f in range(8):
                sq, sqb = sq_r.next()
                P.emit("act", lambda e, sq=sq, xt=xt, f=f: e.activation(out=sq[:], in_=xt[:, f, :], func=AF.Square), reads=[xb], writes=[sqb])
                P.emit("pe", lambda e, ps=ps, sq=sq, f=f: e.matmul(ps[:], lhsT=ones_f[:], rhs=sq[:], start=(f == 0), stop=(f == 7)), reads=[sqb, b_onf], writes=[psb])
            rs, rsb = rs_r.next()
            P.emit("act", lambda e, rs=rs, ps=ps: e.activation(out=rs[:], in_=ps[:], func=AF.Sqrt, bias=1e-6, scale=1.0 / D), reads=[psb], writes=[rsb])
            P.emit("dve", lambda e, rs=rs: e.reciprocal(out=rs[:], in_=rs[:]), reads=[rsb], writes=[rsb])
            for f in range(8):
                P.emit("dve", lambda e, xt=xt, rs=rs, f=f: e.tensor_tensor(out=xt[:, f, :], in0=xt[:, f, :], in1=rs[:], op=ALU.mult), reads=[xb, rsb], writes=[xb])
                P.emit("dve", lambda e, xt=xt, f=f: e.tensor_scalar(out=xt[:, f, :], in0=xt[:, f, :], scalar1=fing[:, f:f + 1], scalar2=None, op0=ALU.mult), reads=[xb, b_fing], writes=[xb])
            for sub in range(4):
                ot, otb = os_r.next()
                for h in range(2):
                    pt, ptb = pt_r.next()
                    for j in range(4):
                        f = h * 4 + j
                        P.emit("pe", lambda e, pt=pt, xt=xt, f=f, j=j, sub=sub: e.matmul(pt[:, j * 128:(j + 1) * 128], lhsT=xt[:, f, sub * 128:(sub + 1) * 128], rhs=ident_f[:], start=True, stop=True),
                               reads=[xb, b_idf], writes=[ptb])
                    P.emit("act", lambda e, pt=pt, ot=ot, h=h: e.activation(out=ot[:, h * 512:(h + 1) * 512], in_=pt[:], func=AF.Copy), reads=[ptb], writes=[otb])
                r0 = t0 + sub * 128
                P.dma("pool", y_out[r0:r0 + 128, :], ot[:], reads=[otb], writes=[bOUT])
        P.flush()
    top.close()
    return nc, P


_CACHE = {}


def kernel(**inputs):
    x_prompt = np.asarray(inputs["x_prompt"], np.float32)
    x_sample = np.asarray(inputs["x_sample"], np.float32)
    c_prompt = np.asarray(inputs["c_prompt"], np.float32)
    c_sample = np.asarray(inputs["c_sample"], np.float32)
    if "nc" not in _CACHE:
        _CACHE["nc"] = build_program()[0]
    nc = _CACHE["nc"]
    shared = {k: np.ascontiguousarray(np.asarray(inputs[k], np.float32)) for k in (
        "norm_g", "ada_w", "ada_b", "attn_w_in", "attn_w_out", "ssm_w_in", "ssm_lam_re", "ssm_lam_im",
        "ssm_log_dt", "ssm_b_re", "ssm_b_im", "ssm_c_re", "ssm_c_im", "ssm_d", "ssm_w_glu", "ssm_w_out")}
    shared["final_norm_g"] = np.ascontiguousarray(np.asarray(inputs["final_norm_g"], np.float32).reshape(1, D))
    in_maps = []
    for core in range(8):
        c = core % 4
        m = dict(shared)
        if c < 2:
            m["x_in"] = np.ascontiguousarray(x_prompt[c])
            m["c_in"] = np.ascontiguousarray(np.stack([c_prompt[c], c_prompt[c]]))
            m["pos_in"] = np.arange(N, dtype=np.float32).reshape(1, N)
            fl = np.zeros((128, 2), np.float32)
            fl[:, 0] = 1.0
        else:
            a = 2 * (c - 2)
            m["x_in"] = np.ascontiguousarray(np.concatenate([x_sample[a], x_sample[a + 1]], axis=0))
            m["c_in"] = np.ascontiguousarray(np.stack([c_sample[a], c_sample[a + 1]]))
            m["pos_in"] = np.concatenate([np.arange(SEG), np.arange(SEG)]).astype(np.float32).reshape(1, N)
            fl = np.zeros((128, 2), np.float32)
            fl[:, 1] = NEGM
        m["flag_in"] = fl
        in_maps.append(m)
    res = run_bass_kernel_spmd(nc, in_maps, core_ids=list(range(8)))
    outs = [np.asarray(r["y_out"], np.float32) for r in res.results]
    y_prompt = np.stack([outs[0], outs[1]]).reshape(2, N, D)
    y_sample = np.stack([outs[2][:SEG], outs[2][SEG:], outs[3][:SEG], outs[3][SEG:]])
    return (y_prompt, y_sample)
```
